# Optimizing a Trainium2 kernel written in Bass

```python
import jax, jax.numpy as jnp
from jax import lax
import numpy as np

D_MODEL = 2048
BATCH = 4
SEQ = 8192
DEPTH = 1
DEC_BATCH = 1
DEC_SEQ = 16384
PAST_LEN = 128

GRID_W = 64
HEAD_DIM = 128
NA_HEADS = 8
NA_WIN_H = 8
NA_WIN_W = 16
GQA_Q_HEADS = 8
GQA_KV_HEADS = 2
ROPE_THETA = 10000.0
ROPE_AXIS_PAIRS = HEAD_DIM // 4
Q_BLOCK = 128
N_MEM = 256
CROSS_HEADS = 4
D_FF = 4 * D_MODEL
EPS = 1e-6
NEG_INF = -1e30
NA_W = NA_HEADS * HEAD_DIM
GQA_QW = GQA_Q_HEADS * HEAD_DIM
GQA_KVW = GQA_KV_HEADS * HEAD_DIM
CROSS_W = CROSS_HEADS * HEAD_DIM
IN_SPLITS = (NA_W, NA_W, NA_W, GQA_QW, GQA_KVW, GQA_KVW, D_MODEL, D_MODEL)
D_IN = sum(IN_SPLITS)

kernel_name = 'hybrid_natten_gqa_xattn_encoder'


def _rmsnorm(x, g):
    xf = x.astype(jnp.float32)
    y = xf * lax.rsqrt(jnp.mean(xf * xf, axis=-1, keepdims=True) + EPS)
    return (y * g.astype(jnp.float32)).astype(x.dtype)


def _neighbourhood_attention(q, k, v, rpb):
    b, s, h, dh = q.shape
    rows = s // GRID_W
    kh = min(NA_WIN_H, rows)
    qg = q.reshape(b, rows, GRID_W, h, dh).transpose(1, 0, 2, 3, 4)
    kg = k.reshape(b, rows, GRID_W, h, dh)
    vg = v.reshape(b, rows, GRID_W, h, dh)
    cols = np.arange(GRID_W)
    col_start = np.clip(cols - NA_WIN_W // 2, 0, GRID_W - NA_WIN_W)
    col_mask = (cols[None, :] >= col_start[:, None]) & (cols[None, :] < col_start[:, None] + NA_WIN_W)
    col_idx = np.clip(cols[None, :] - cols[:, None] + NA_WIN_W - 1, 0, 2 * NA_WIN_W - 2)
    band_mask = jnp.asarray(np.broadcast_to(col_mask[:, None, :], (GRID_W, kh, GRID_W)).reshape(GRID_W, kh * GRID_W))
    scale = dh ** -0.5

    def row_block(args):
        r, q_r = args
        start = jnp.clip(r - kh // 2, 0, rows - kh)
        k_r = lax.dynamic_slice_in_dim(kg, start, kh, axis=1).reshape(b, kh * GRID_W, h, dh)
        v_r = lax.dynamic_slice_in_dim(vg, start, kh, axis=1).reshape(b, kh * GRID_W, h, dh)
        ridx = start + jnp.arange(kh) - r + NA_WIN_H - 1
        bias = rpb[:, ridx][:, :, col_idx]
        bias = bias.transpose(0, 2, 1, 3).reshape(h, GRID_W, kh * GRID_W).astype(jnp.float32)
        sc = jnp.einsum('bqhd,bkhd->bhqk', q_r, k_r).astype(jnp.float32) * scale + bias
        sc = jnp.where(band_mask, sc, NEG_INF)
        p = jax.nn.softmax(sc, axis=-1).astype(v_r.dtype)
        return jnp.einsum('bhqk,bkhd->bqhd', p, v_r)

    out = lax.map(row_block, (jnp.arange(rows), qg))
    return out.transpose(1, 0, 2, 3, 4).reshape(b, s, h * dh)


def _axial_rope_tables(s):
    t = jnp.arange(s)
    row = (t // GRID_W).astype(jnp.float32)
    col = (t % GRID_W).astype(jnp.float32)
    inv_freq = ROPE_THETA ** (-jnp.arange(ROPE_AXIS_PAIRS, dtype=jnp.float32) / ROPE_AXIS_PAIRS)
    ang_r = (row[:, None] * inv_freq[None, :])[:, None, :]
    ang_c = (col[:, None] * inv_freq[None, :])[:, None, :]
    return jnp.cos(ang_r), jnp.sin(ang_r), jnp.cos(ang_c), jnp.sin(ang_c)


def _rope_axis(x, cos, sin):
    x1, x2 = jnp.split(x, 2, axis=-1)
    return jnp.concatenate([x1 * cos - x2 * sin, x2 * cos + x1 * sin], axis=-1)


def _axial_rope(x, tables):
    cos_r, sin_r, cos_c, sin_c = tables
    xf = x.astype(jnp.float32)
    xr, xc = jnp.split(xf, 2, axis=-1)
    y = jnp.concatenate([_rope_axis(xr, cos_r, sin_r), _rope_axis(xc, cos_c, sin_c)], axis=-1)
    return y.astype(x.dtype)


def _gqa_attention(q, k, v):
    b, s, hq, dh = q.shape
    hkv = k.shape[2]
    g = hq // hkv
    nb = s // Q_BLOCK
    scale = dh ** -0.5
    qb = q.reshape(b, nb, Q_BLOCK, hkv, g, dh).transpose(1, 0, 2, 3, 4, 5)

    def block(q_blk):
        sc = jnp.einsum('bqkgd,bskd->bkgqs', q_blk, k).astype(jnp.float32) * scale
        p = jax.nn.softmax(sc, axis=-1).astype(v.dtype)
        return jnp.einsum('bkgqs,bskd->bqkgd', p, v)

    out = lax.map(block, qb)
    return out.transpose(1, 0, 2, 3, 4, 5).reshape(b, s, hq * dh)


def _cross_attention(h, mem_h, w_cq, w_ckv, w_co):
    b, s, _ = h.shape
    m = mem_h.shape[1]
    q = (h @ w_cq).reshape(b, s, CROSS_HEADS, HEAD_DIM)
    k, v = jnp.split(mem_h @ w_ckv, 2, axis=-1)
    k = k.reshape(b, m, CROSS_HEADS, HEAD_DIM)
    v = v.reshape(b, m, CROSS_HEADS, HEAD_DIM)
    sc = jnp.einsum('bqhd,bmhd->bhqm', q, k).astype(jnp.float32) * (HEAD_DIM ** -0.5)
    p = jax.nn.softmax(sc, axis=-1).astype(v.dtype)
    o = jnp.einsum('bhqm,bmhd->bqhd', p, v).reshape(b, s, CROSS_W)
    return o @ w_co


def _trunk(x, mem, g_mix, w_in, rpb, g_q, g_k, w_pa, w_pb, w_o, g_cross, g_mem, w_cq, w_ckv, w_co,
           g_mlp, w_up, w_down, g_final):
    b, s, _ = x.shape
    tables = _axial_rope_tables(s)
    split_pts = [int(p) for p in np.cumsum(IN_SPLITS)[:-1]]
    for l in range(DEPTH):
        h = _rmsnorm(x, g_mix[l])
        z = h @ w_in[l]
        qa, ka, va, qb, kb, vb, ga, gb = jnp.split(z, split_pts, axis=-1)
        y_a = _neighbourhood_attention(qa.reshape(b, s, NA_HEADS, HEAD_DIM),
                                       ka.reshape(b, s, NA_HEADS, HEAD_DIM),
                                       va.reshape(b, s, NA_HEADS, HEAD_DIM), rpb[l]) @ w_pa[l]
        qb = _axial_rope(_rmsnorm(qb.reshape(b, s, GQA_Q_HEADS, HEAD_DIM), g_q[l]), tables)
        kb = _axial_rope(_rmsnorm(kb.reshape(b, s, GQA_KV_HEADS, HEAD_DIM), g_k[l]), tables)
        y_b = _gqa_attention(qb, kb, vb.reshape(b, s, GQA_KV_HEADS, HEAD_DIM)) @ w_pb[l]
        mixed = jax.nn.sigmoid(ga) * y_a + jax.nn.sigmoid(gb) * y_b
        x = x + mixed @ w_o[l]
        x = x + _cross_attention(_rmsnorm(x, g_cross[l]), _rmsnorm(mem, g_mem[l]), w_cq[l], w_ckv[l], w_co[l])
        hm = _rmsnorm(x, g_mlp[l]) @ w_up[l]
        x = x + jnp.square(jax.nn.relu(hm)) @ w_down[l]
    return _rmsnorm(x, g_final)


def setup_inputs(seed: int = 0) -> dict:
    key = jax.random.key(seed)
    ks = jax.random.split(key, 24)
    f32 = jnp.float32

    def w(k, shape, fan_in):
        return jax.random.normal(k, shape, f32) * (fan_in ** -0.5)

    def gain(k, shape):
        return 1.0 + 0.01 * jax.random.normal(k, shape, f32)

    return {
        'x_prompt': jax.random.normal(ks[0], (BATCH, SEQ, D_MODEL), f32),
        'x_sample': jax.random.normal(ks[1], (DEC_BATCH, DEC_SEQ, D_MODEL), f32),
        'mem_prompt': jax.random.normal(ks[2], (BATCH, N_MEM, D_MODEL), f32),
        'mem_sample': jax.random.normal(ks[3], (DEC_BATCH, N_MEM, D_MODEL), f32),
        'g_mix': gain(ks[4], (DEPTH, D_MODEL)),
        'w_in': w(ks[5], (DEPTH, D_MODEL, D_IN), D_MODEL),
        'rpb': 0.02 * jax.random.normal(ks[6], (DEPTH, NA_HEADS, 2 * NA_WIN_H - 1, 2 * NA_WIN_W - 1), f32),
        'g_q': gain(ks[7], (DEPTH, HEAD_DIM)),
        'g_k': gain(ks[8], (DEPTH, HEAD_DIM)),
        'w_pa': w(ks[9], (DEPTH, NA_W, D_MODEL), NA_W),
        'w_pb': w(ks[10], (DEPTH, GQA_QW, D_MODEL), GQA_QW),
        'w_o': w(ks[11], (DEPTH, D_MODEL, D_MODEL), D_MODEL),
        'g_cross': gain(ks[12], (DEPTH, D_MODEL)),
        'g_mem': gain(ks[13], (DEPTH, D_MODEL)),
        'w_cq': w(ks[14], (DEPTH, D_MODEL, CROSS_W), D_MODEL),
        'w_ckv': w(ks[15], (DEPTH, D_MODEL, 2 * CROSS_W), D_MODEL),
        'w_co': w(ks[16], (DEPTH, CROSS_W, D_MODEL), CROSS_W),
        'g_mlp': gain(ks[17], (DEPTH, D_MODEL)),
        'w_up': w(ks[18], (DEPTH, D_MODEL, D_FF), D_MODEL),
        'w_down': w(ks[19], (DEPTH, D_FF, D_MODEL), D_FF),
        'g_final': gain(ks[20], (D_MODEL,)),
    }


def reference(x_prompt, x_sample, mem_prompt, mem_sample, g_mix, w_in, rpb, g_q, g_k, w_pa, w_pb, w_o,
              g_cross, g_mem, w_cq, w_ckv, w_co, g_mlp, w_up, w_down, g_final):
    y_prompt = _trunk(x_prompt, mem_prompt, g_mix, w_in, rpb, g_q, g_k, w_pa, w_pb, w_o, g_cross, g_mem,
                      w_cq, w_ckv, w_co, g_mlp, w_up, w_down, g_final)
    y_sample = _trunk(x_sample, mem_sample, g_mix, w_in, rpb, g_q, g_k, w_pa, w_pb, w_o, g_cross, g_mem,
                      w_cq, w_ckv, w_co, g_mlp, w_up, w_down, g_final)
    return (y_prompt, y_sample)
```

```python
import numpy as np
from contextlib import ExitStack
import concourse.bass as bass
import concourse.mybir as mybir
from concourse.bass_utils import run_bass_kernel_spmd

F32 = mybir.dt.float32
BF16 = mybir.dt.bfloat16
U8 = mybir.dt.uint8
ALU = mybir.AluOpType
AF = mybir.ActivationFunctionType

D = 2048
DIN = 8704
DFF = 8192
NMEM = 256
EPS = 1e-6
SCALE = 128 ** -0.5


class Buf:
    __slots__ = ("name", "writers", "readers")

    def __init__(self, name):
        self.name = name
        self.writers = {}
        self.readers = {}


class Op:
    __slots__ = ("eng", "fn", "waits", "signal", "sigval", "idx", "dma")

    def __init__(self, eng, fn):
        self.eng = eng
        self.fn = fn
        self.waits = []
        self.signal = False
        self.sigval = None
        self.idx = None
        self.dma = None


ENGS = ("pe", "act", "dve", "pool", "sp")
STRICT = ("act", "dve", "pool")


class Prog:
    def __init__(self, nc):
        self.nc = nc
        self.ops = {e: [] for e in ENGS}
        self.waited = {e: {} for e in ENGS}
        self.dma_count = {}

    def buf(self, name="b"):
        return Buf(name)

    def bufs(self, n, name="b"):
        return [Buf(f"{name}{i}") for i in range(n)]

    def add(self, eng, fn, reads=(), writes=(), dma=None):
        mykey = dma if dma is not None else eng
        deps = []
        for b in reads:
            for k, t in b.writers.items():
                if k != mykey or (dma is None and eng in STRICT):
                    deps.append(t)
        for b in writes:
            for k, t in b.writers.items():
                if k != mykey:
                    deps.append(t)
            for k, t in b.readers.items():
                if k != mykey:
                    deps.append(t)
        op = Op(eng, fn)
        w = self.waited[eng]
        for t in deps:
            if t[0] == "c":
                p = t[1]
                if w.get(p.eng, -1) >= p.idx:
                    continue
                w[p.eng] = p.idx
                p.signal = True
                op.waits.append(t)
            else:
                sk = t[1]
                v = self.dma_count[sk]
                if w.get(sk, 0) >= v:
                    continue
                w[sk] = v
                op.waits.append(("d", sk, v))
        op.idx = len(self.ops[eng])
        self.ops[eng].append(op)
        if dma is not None:
            self.dma_count[dma] = self.dma_count.get(dma, 0) + 16
            op.dma = (dma, self.dma_count[dma])
            tok = ("d", dma, self.dma_count[dma])
        else:
            tok = ("c", op)
        for b in reads:
            b.readers[mykey] = tok
        for b in writes:
            b.writers = {mykey: tok}
            b.readers = {}
        return op

    def barrier(self):
        lasts = {}
        for e in ("pe", "act", "dve", "pool"):
            cl = [o for o in self.ops[e] if o.dma is None and o.fn is not None]
            if cl:
                lasts[e] = cl[-1]
                cl[-1].signal = True
        for f in ENGS:
            op = Op(f, None)
            w = self.waited[f]
            for e, l in lasts.items():
                if e != f and w.get(e, -1) < l.idx:
                    w[e] = l.idx
                    op.waits.append(("c", l))
            for sk, v in self.dma_count.items():
                if w.get(sk, 0) < v:
                    w[sk] = v
                    op.waits.append(("d", sk, v))
            op.idx = len(self.ops[f])
            self.ops[f].append(op)

    def emit(self):
        nc = self.nc
        self.barrier()
        for e in ENGS:
            c = 0
            for op in self.ops[e]:
                if op.signal and op.dma is None:
                    c += 1
                    op.sigval = c
        with ExitStack() as st:
            esem = {e: st.enter_context(nc.semaphore(f"s_{e}")) for e in ENGS}
            dsem = {}
            for i, sk in enumerate(self.dma_count):
                dsem[sk] = st.enter_context(nc.semaphore(f"d_{i}"))
            block = st.enter_context(nc.Block())

            def run(e, eng):
                for op in self.ops[e]:
                    for t in op.waits:
                        if t[0] == "c":
                            eng.wait_ge(esem[t[1].eng], t[1].sigval)
                        else:
                            eng.wait_ge(dsem[t[1]], t[2])
                    if op.fn is None:
                        continue
                    ins = op.fn(eng)
                    if op.dma is not None:
                        ins.then_inc(dsem[op.dma[0]], 16)
                    elif op.signal:
                        ins.then_inc(esem[e], 1)

            block.tensor(lambda eng: run("pe", eng))
            block.scalar(lambda eng: run("act", eng))
            block.vector(lambda eng: run("dve", eng))
            block.gpsimd(lambda eng: run("pool", eng))
            block.sync(lambda eng: run("sp", eng))
        self.stats = {e: len(self.ops[e]) for e in ENGS}


def na_segments(first, last):
    out = []
    for j in range(8):
        m_lo, m_hi = max(0, j - 4), min(3, j)
        if 2 <= j <= 5:
            if first:
                m_lo = 0
            if last:
                m_hi = 3
        r_lo, r_hi = 2 * m_lo, 2 * m_hi + 2
        segs = []
        r = r_lo
        while r < r_hi:
            if first and r < 4:
                e_ = min(4, r_hi)
                tb = "first1" if 2 <= j <= 5 else "first0"
            elif last and r >= 5:
                e_ = r_hi
                tb = "last1" if 2 <= j <= 5 else "last0"
            else:
                e_ = r_hi
                if last:
                    e_ = min(e_, 5)
                tb = "int"
            segs.append((r, e_, tb, 11 - 2 * j + r))
            r = e_
        for (a, b, tb, s0) in segs:
            assert 0 <= s0 and s0 + (b - a) <= 16, (j, segs)
        out.append((m_lo, m_hi, segs))
    return out


class Cfg:
    def __init__(self, S_p, S_s):
        self.jobs = [dict(n="p", S=S_p, T=S_p // 2), dict(n="s", S=S_s, T=S_s // 8)]


def build(cfg, debug=False):
    nc = bass.Bass("TRN2", target_bir_lowering=False)
    P = Prog(nc)
    st = ExitStack()
    dkind = "ExternalOutput" if debug else "Internal"

    def din(name, shape, dt=F32):
        return nc.dram_tensor(name, list(shape), dt, kind="ExternalInput").ap()

    def dscr(name, shape, dt=BF16):
        return nc.dram_tensor(name, list(shape), dt, kind=dkind).ap()

    W = {}
    wshape = dict(w_in=(D, DIN), w_pa=(1024, D), w_pb=(1024, D), w_o=(D, D), w_cq=(D, 512),
                  w_ckv=(D, 1024), w_co=(512, D), w_up=(D, DFF), w_down=(DFF, D))
    WB = {}
    WBUF = {}
    for k, s in wshape.items():
        W[k] = din(k, s)
        WB[k] = nc.dram_tensor("b_" + k, list(s), BF16, kind="Internal").ap()
        WBUF[k] = P.buf("wb_" + k)
    gcols = {k: din(k, (128, 16)) for k in ("g_mix_c", "g_cross_c", "g_mem_c", "g_mlp_c")}
    gq_bc_d = din("gq_bc", (128, 128))
    gk_bc_d = din("gk_bc", (128, 128))
    gfin_d = din("gfin_bc", (128, D))
    ident_d = din("ident", (128, 128))
    sel_d = din("sel", (128, 4))
    rpbx_d = din("rpb_exp", (128, 8 * 16 * 64))
    mfull_d = din("mask_full", (128, 16 * 64))
    mint_d = din("mask_int", (128, 16 * 64))
    tabs_d = dscr("tabs", (10, 128, 8192))
    J = cfg.jobs
    for jb in J:
        n, S, T = jb["n"], jb["S"], jb["T"]
        jb["x_seq"] = din(f"x{n}_seq", (S, D))
        jb["x_own"] = din(f"x{n}_own", (T, D))
        jb["x_halo"] = din(f"x{n}_halo", (512, D))
        jb["mem"] = din(f"mem{n}", (NMEM, D))
        jb["rk_c"] = din(f"rk{n}_c", (S, 128))
        jb["rk_s"] = din(f"rk{n}_s", (S, 128))
        jb["rq_c"] = din(f"rq{n}_c", (T, 128))
        jb["rq_s"] = din(f"rq{n}_s", (T, 128))
        jb["y"] = nc.dram_tensor(f"y{n}", [T, D], F32, kind="ExternalOutput").ap()
        jb["KbT"] = dscr(f"KbT{n}", (256, S))
        jb["Vb"] = dscr(f"Vb{n}", (S, 256))
        jb["QaT"] = dscr(f"QaT{n}", (1024, T))
        jb["KaT"] = dscr(f"KaT{n}", (1024, T + 512))
        jb["Va"] = dscr(f"Va{n}", (T + 512, 1024))
        jb["QbT"] = dscr(f"QbT{n}", (1024, T))
        jb["sgT"] = dscr(f"sgT{n}", (4096, T))
        jb["OaT"] = dscr(f"OaT{n}", (1024, T))
        jb["ObT"] = dscr(f"ObT{n}", (1024, T))

    def sb(name, shape, dt):
        return st.enter_context(nc.sbuf_tensor("s_" + name, list(shape), dt))

    ARENA = 186 * 1024
    arena = sb("arena", [128, ARENA], U8)

    def carve(off, shape, dt):
        esz = 4 if dt == F32 else 2
        n = int(np.prod(shape[1:]))
        assert off % 32 == 0 and off + n * esz <= ARENA, (off, shape)
        v = arena[:, off:off + n * esz].bitcast(dt)
        if len(shape) == 3:
            v = v.rearrange("p (a b) -> p a b", b=shape[2])
        elif len(shape) == 4:
            v = v.rearrange("p (a b c) -> p a b c", b=shape[2], c=shape[3])
        return v, off + n * esz

    ident = sb("ident", [128, 128], BF16)
    gcol = {k: sb(k, [128, 16], F32) for k in gcols}
    gq_bc = sb("gq_bc", [128, 128], F32)
    gk_bc = sb("gk_bc", [128, 128], F32)
    sel = sb("sel", [128, 4], F32)
    nsel = sb("nsel", [128, 4], F32)
    stat = sb("stat", [128, 64], F32)
    epsb = sb("epsb", [128, 1], F32)
    rden = sb("rden", [128, 8], F32)
    b_rden = P.buf("rden")
    KcT = [sb(f"KcT{i}", [128, 4, NMEM], BF16) for i in range(2)]
    Vc1 = [sb(f"Vc1{i}", [128, 2, 4, 136], BF16) for i in range(2)]
    b_const = P.buf("const")
    b_stat = P.bufs(8, "stat")
    b_kc = P.bufs(2, "kc")

    psbig = [st.enter_context(nc.psum_tensor(f"ps{i}", [128, 1024], F32)) for i in range(4)]
    psum = [psbig[i // 2][:, (i % 2) * 512:(i % 2 + 1) * 512] for i in range(8)]
    b_ps = P.bufs(8, "ps")

    def psb(i):
        return psum[i].bitcast(BF16)

    def mm(out, lhsT, rhs, start, stop, reads, writes, skip=False):
        if skip:
            P.add("pe", lambda e: e.matmul(out, lhsT, rhs, start=start, stop=stop, skip_group_check=True), reads, writes)
        else:
            P.add("pe", lambda e: e.matmul(out, lhsT, rhs, start=start, stop=stop), reads, writes)

    def tr(out, in_, reads, writes):
        P.add("pe", lambda e: e.transpose(out, in_, ident[:]), list(reads) + [b_const], writes)

    def act(out, in_, func, reads, writes, bias=0.0, scale=1.0, accum=None):
        if accum is None:
            P.add("act", lambda e: e.activation(out, in_, func, bias=bias, scale=scale), reads, writes)
        else:
            P.add("act", lambda e: e.activation(out, in_, func, bias=bias, scale=scale, accum_out=accum), reads, writes)

    def ts(eng, out, in0, s1, s2, op0, op1, reads, writes):
        if op1 is None:
            P.add(eng, lambda e: e.tensor_scalar(out, in0, s1, s2, op0), reads, writes)
        else:
            P.add(eng, lambda e: e.tensor_scalar(out, in0, s1, s2, op0, op1), reads, writes)

    def tt(eng, out, in0, in1, op, reads, writes):
        P.add(eng, lambda e: e.tensor_tensor(out, in0, in1, op), reads, writes)

    def stt(eng, out, in0, scalar, in1, op0, op1, reads, writes):
        P.add(eng, lambda e: e.scalar_tensor_tensor(out, in0, scalar, in1, op0, op1), reads, writes)

    def cp(eng, out, in_, reads, writes):
        if eng == "act":
            P.add("act", lambda e: e.copy(out, in_), reads, writes)
        else:
            P.add(eng, lambda e: e.tensor_copy(out, in_), reads, writes)

    def memset(eng, ap, val, writes):
        P.add(eng, lambda e: e.memset(ap, val), (), writes)

    def dma(q, out, in_, key, reads=(), writes=()):
        if q == "sp!":
            q = "sp"
        elif q == "sp" and len(writes) == 0:
            q = "pool"
        P.add(q, lambda e: e.dma_start(out=out, in_=in_), reads, writes, dma=key)

    def run_pipe(jobs, ahead=2):
        L = [i for i, j in enumerate(jobs) if j[0] is not None]
        slots = {}
        nl = 0
        seen = 0
        for i, (lf, cf) in enumerate(jobs):
            if lf is not None:
                seen += 1
            while nl < len(L) and nl < seen + ahead:
                slots[L[nl]] = jobs[L[nl]][0]()
                nl += 1
            cf(slots.get(i))

    dma("pool", ident[:], ident_d, "c_id", writes=[b_const])
    for k in gcols:
        dma("sp", gcol[k][:], gcols[k], "c_" + k, writes=[b_const])
    dma("sp", gq_bc[:], gq_bc_d, "c_gq", writes=[b_const])
    dma("sp", gk_bc[:], gk_bc_d, "c_gk", writes=[b_const])
    dma("sp", sel[:], sel_d, "c_sel", writes=[b_const])
    memset("dve", epsb[:], EPS, [b_const])
    ts("dve", nsel[:], sel[:], -1.0, 1.0, ALU.mult, ALU.add, [b_const], [b_stat[7]])
    def cast_weights(names):
        for k in names:
            rows = wshape[k][0]
            step = 256
            for r0 in range(0, rows, step):
                dma("pool", WB[k][r0:r0 + step, :], W[k][r0:r0 + step, :], "cast_" + k, writes=[WBUF[k]])
    WB["w_kv"] = nc.dram_tensor("b_w_kv", [D, 512], BF16, kind="Internal").ap()
    WBUF["w_kv"] = P.buf("wb_w_kv")
    dma("pool", WB["w_kv"], W["w_in"][:, 4096:4608], "cast_w_kv", writes=[WBUF["w_kv"]])
    cast_weights(["w_ckv"])

    off = 0
    slab = []
    for i in range(3):
        v, off = carve(off, [128, 16, 512], BF16)
        slab.append(v)
    hT = []
    for i in range(2):
        v, off = carve(off, [128, 16, 512], BF16)
        hT.append(v)
    xres, off = carve(off, [128, 4, D], F32)
    xnb = []
    for i in range(2):
        v, off = carve(off, [128, D], BF16)
        xnb.append(v)
    OFF_COMMON = off
    stg = []
    for i in range(3):
        v, off = carve(off, [128, 4, 512], BF16)
        stg.append(v)
    ropeC, off = carve(off, [128, 4, 128], F32)
    ropeS, off = carve(off, [128, 4, 128], F32)
    kn, off = carve(off, [128, 4, 128], F32)
    t1, off = carve(off, [128, 4, 128], F32)
    t2, off = carve(off, [128, 4, 128], F32)
    krb = []
    for i in range(2):
        v, off = carve(off, [128, 4, 128], BF16)
        krb.append(v)
    OFF_P13 = off
    b_slab = P.bufs(3, "slab")
    b_hT = P.bufs(2, "hT")
    b_x = P.bufs(4, "x")
    b_xnb = P.bufs(2, "xn")
    b_stg = P.bufs(3, "stg")
    b_rope = P.buf("rope")
    b_kn, b_t1, b_t2 = P.bufs(3, "ropetmp")
    b_krb = P.bufs(2, "krb")
    cnt = dict(slab=0, stg=0, hT=0, ps=0, st=0, xn=0)

    def next_slab():
        i = cnt["slab"] % 3
        cnt["slab"] += 1
        return i

    def load_slab(wname, k0, nk, c0, ncols=512):
        i = next_slab()
        src = WB[wname][k0 * 128:(k0 + nk) * 128, c0:c0 + ncols].rearrange("(c p) n -> p c n", p=128)
        dma("sp", slab[i][:, 0:nk, 0:ncols], src, f"slab{i}", reads=[WBUF[wname]], writes=[b_slab[i]])
        return i

    def next_ps(lo=2, n=4):
        i = lo + cnt["ps"] % n
        cnt["ps"] += 1
        return i

    def next_stg():
        i = cnt["stg"] % 3
        cnt["stg"] += 1
        return i

    def rstd_from_ss(ss_ap, out_ap, n, width, b):
        act(out_ap, ss_ap, AF.Sqrt, [b, b_const], [b], bias=epsb[:], scale=1.0 / width)
        P.add("dve", lambda e: e.reciprocal(out_ap, out_ap), [b], [b])

    def norm_transpose(x_src, g, hbuf, keep_x):
        for m in range(4):
            if x_src is not None:
                dma("sp", xres[:, m, :], x_src[m * 128:(m + 1) * 128, :], f"x{m}", writes=[b_x[m]])
            sbf = b_stat[m % 4]
            ss = stat[:, m:m + 1]
            xi = cnt["xn"] % 2
            cnt["xn"] += 1
            xn, b_xn = xnb[xi], b_xnb[xi]
            act(xn[:], xres[:, m, :], AF.Square, [b_x[m]], [b_xn, sbf], accum=ss)
            rstd_from_ss(ss, ss, 1, D, sbf)
            ts("dve", xn[:], xres[:, m, :], ss, None, ALU.mult, None, [b_x[m], sbf], [b_xn])
            pv = [psb(0), psb(1)]
            for c in range(16):
                tr(pv[c // 8][:, (c % 8) * 128:(c % 8 + 1) * 128], xn[:, c * 128:(c + 1) * 128], [b_xn], [b_ps[c // 8]])
            for hf in range(2):
                tt("dve", hT[hbuf][:, hf * 8:(hf + 1) * 8, m * 128:(m + 1) * 128],
                   pv[hf].rearrange("p (c t) -> p c t", t=128),
                   gcol[g][:, hf * 8:(hf + 1) * 8].unsqueeze(2).to_broadcast([128, 8, 128]),
                   ALU.mult, [b_ps[hf], b_const], [b_hT[hbuf]])

    def load_rope(c_src, s_src):
        dma("sp", ropeC[:], c_src.rearrange("(m p) d -> p m d", p=128), "ropeC", writes=[b_rope])
        dma("sp", ropeS[:], s_src.rearrange("(m p) d -> p m d", p=128), "ropeS", writes=[b_rope])

    def nr_compute(ps_i, nh, col0, m, g_bc, kb):
        sbf = b_stat[4 + kb]
        ssq = stat[:, 8 + 8 * kb: 8 + 8 * kb + nh]
        for h in range(nh):
            act(t1[:, h, :], psum[ps_i][:, col0 + h * 128: col0 + (h + 1) * 128], AF.Square,
                [b_ps[ps_i]], [b_t1, sbf], accum=ssq[:, h:h + 1])
        rstd_from_ss(ssq, ssq, nh, 128, sbf)
        for h in range(nh):
            stt("dve", kn[:, h, :], psum[ps_i][:, col0 + h * 128: col0 + (h + 1) * 128], ssq[:, h:h + 1], g_bc[:],
                ALU.mult, ALU.mult, [b_ps[ps_i], sbf, b_const], [b_kn])
        Cb = ropeC[:, m, :].unsqueeze(1).to_broadcast([128, nh, 128])
        tt("dve", t1[:, 0:nh, :], kn[:, 0:nh, :], Cb, ALU.mult, [b_kn, b_rope], [b_t1])
        knv = kn[:, 0:nh, :].rearrange("p h (a b c) -> p h a b c", a=2, b=2)
        t2v = t2[:, 0:nh, :].rearrange("p h (a b c) -> p h a b c", a=2, b=2)
        Sv = ropeS[:, m, :].rearrange("p (a b c) -> p a b c", a=2, b=2)
        for hf in range(2):
            tt("pool", t2v[:, :, :, hf, :], knv[:, :, :, 1 - hf, :], Sv[:, :, hf, :].unsqueeze(1).to_broadcast([128, nh, 2, 32]),
               ALU.mult, [b_kn, b_rope], [b_t2])
        tt("dve", krb[kb][:, 0:nh, :], t1[:, 0:nh, :], t2[:, 0:nh, :], ALU.add, [b_t1, b_t2], [b_krb[kb]])

    def nr_transpose(nh, m, kb, dst, dst_b):
        pb = psb(6 + kb)
        for h in range(nh):
            tr(pb[:, h * 128:(h + 1) * 128], krb[kb][:, h, :], [b_krb[kb]], [b_ps[6 + kb]])
        cp("dve", dst[:, 0:nh, m * 128:(m + 1) * 128], pb[:, 0:nh * 128].rearrange("p (h t) -> p h t", t=128),
           [b_ps[6 + kb]], [dst_b])

    cnt["kr"] = 0

    def norm_rope_T(ps_i, nh, col0, m, g_bc, dst, dst_b):
        kb = cnt["kr"] % 2
        cnt["kr"] += 1
        nr_compute(ps_i, nh, col0, m, g_bc, kb)
        return lambda: nr_transpose(nh, m, kb, dst, dst_b)

    o2 = 0
    tA, o2 = carve(o2, [128, 8192], F32)
    tB, o2 = carve(o2, [128, 8, 1024], BF16)
    tC, o2 = carve(o2, [128, 8, 1024], BF16)
    tD, o2 = carve(o2, [128, 8, 1024], BF16)
    tE, o2 = carve(o2, [128, 8, 1024], BF16)
    tM, o2 = carve(o2, [128, 2, 1024], F32)
    b_tA, b_tB, b_tC, b_tD, b_tE, b_tM = P.bufs(6, "tab")
    dma("sp", tA[:], rpbx_d, "tA", writes=[b_tA])
    dma("sp", tM[:, 0, :], mfull_d, "tM", writes=[b_tM])
    dma("sp", tM[:, 1, :], mint_d, "tM", writes=[b_tM])
    act(tA[:], tA[:], AF.Exp, [b_tA], [b_tA])
    tA3 = tA.rearrange("p (h x) -> p h x", h=8)
    tt("dve", tB[:], tA3, tM[:, 0, :].unsqueeze(1).to_broadcast([128, 8, 1024]), ALU.mult, [b_tA, b_tM], [b_tB])
    tt("dve", tC[:], tA3, tM[:, 1, :].unsqueeze(1).to_broadcast([128, 8, 1024]), ALU.mult, [b_tA, b_tM], [b_tC])
    tt("dve", tD[:], tB[:], tC[:], ALU.subtract, [b_tB, b_tC], [b_tD])
    dma("sp", tabs_d[0].rearrange("p (h x) -> p h x", h=8), tC[:], "tabst", reads=[b_tC])
    dma("sp", tabs_d[1].rearrange("p (h x) -> p h x", h=8), tB[:], "tabst", reads=[b_tB])
    for q in range(4):
        stt("dve", tE[:], tD[:], sel[:, q:q + 1], tC[:], ALU.mult, ALU.add, [b_tD, b_tC, b_const], [b_tE])
        dma("sp", tabs_d[2 + q * 2 + 1].rearrange("p (h x) -> p h x", h=8), tE[:], "tabsE", reads=[b_tE])
        ts("dve", tE[:], tC[:], nsel[:, q:q + 1], None, ALU.mult, None, [b_tC, b_stat[7]], [b_tE])
        dma("sp", tabs_d[2 + q * 2 + 0].rearrange("p (h x) -> p h x", h=8), tE[:], "tabsE", reads=[b_tE])

    P.barrier()
    for ji, jb in enumerate(J):
        for half in range(2):
            src = jb["mem"][half * 128:(half + 1) * 128, :]
            dma("sp", xres[:, half, :], src, f"x{half}", writes=[b_x[half]])
            sbf = b_stat[half]
            ss = stat[:, half:half + 1]
            xn, b_xn = xnb[half], b_xnb[half]
            act(xn[:], xres[:, half, :], AF.Square, [b_x[half]], [b_xn, sbf], accum=ss)
            rstd_from_ss(ss, ss, 1, D, sbf)
            ts("dve", xn[:], xres[:, half, :], ss, None, ALU.mult, None, [b_x[half], sbf], [b_xn])
            pv = [psb(0), psb(1)]
            for c in range(16):
                tr(pv[c // 8][:, (c % 8) * 128:(c % 8 + 1) * 128], xn[:, c * 128:(c + 1) * 128], [b_xn], [b_ps[c // 8]])
            for hf in range(2):
                tt("dve", hT[0][:, hf * 8:(hf + 1) * 8, half * 128:(half + 1) * 128],
                   pv[hf].rearrange("p (c t) -> p c t", t=128),
                   gcol["g_mem_c"][:, hf * 8:(hf + 1) * 8].unsqueeze(2).to_broadcast([128, 8, 128]),
                   ALU.mult, [b_ps[hf], b_const], [b_hT[0]])
        si = load_slab("w_ckv", 0, 16, 0)
        for h in range(4):
            pi = next_ps()
            for c in range(16):
                mm(psum[pi][:, 0:NMEM], slab[si][:, c, h * 128:(h + 1) * 128], hT[0][:, c, 0:NMEM], c == 0, c == 15,
                   [b_slab[si], b_hT[0]], [b_ps[pi]])
            cp("dve", KcT[ji][:, h, :], psum[pi][:, 0:NMEM], [b_ps[pi]], [b_kc[ji]])
        si = load_slab("w_ckv", 0, 16, 512)
        memset("dve", Vc1[ji][:, :, :, 128:129], 1.0, [b_kc[ji]])
        for half in range(2):
            pi = next_ps()
            for c in range(16):
                mm(psum[pi][:], hT[0][:, c, half * 128:(half + 1) * 128], slab[si][:, c, :], c == 0, c == 15,
                   [b_slab[si], b_hT[0]], [b_ps[pi]])
            cp("dve", Vc1[ji][:, half, :, 0:128], psum[pi][:].rearrange("p (h d) -> p h d", d=128), [b_ps[pi]], [b_kc[ji]])

    o2 = OFF_P13
    kvslab, o2 = carve(o2, [128, 16, 512], BF16)
    kT, o2 = carve(o2, [128, 2, 512], BF16)
    vt, o2 = carve(o2, [128, 4, 256], BF16)
    b_kvslab, b_kT, b_vt = P.bufs(3, "kv")
    dma("sp", kvslab[:], WB["w_kv"].rearrange("(c p) n -> p c n", p=128), "kvslab",
        reads=[WBUF["w_kv"]], writes=[b_kvslab])
    deferred_cast = [True]
    cnt["tA"] = 0

    def norm_transpose_m(x_rows, g, hbuf, m):
        dma("sp", xres[:, m, :], x_rows, f"x{m}", writes=[b_x[m]])
        sbf = b_stat[m % 4]
        ss = stat[:, m:m + 1]
        xi = cnt["xn"] % 2
        cnt["xn"] += 1
        xn, b_xn = xnb[xi], b_xnb[xi]
        act(xn[:], xres[:, m, :], AF.Square, [b_x[m]], [b_xn, sbf], accum=ss)
        rstd_from_ss(ss, ss, 1, D, sbf)
        act(xn[:], xres[:, m, :], AF.Copy, [b_x[m], sbf], [b_xn], scale=ss)
        bk = 2 * (cnt["tA"] % 2)
        cnt["tA"] += 1
        pv = [psb(bk), psb(bk + 1)]
        for c in range(16):
            tr(pv[c // 8][:, (c % 8) * 128:(c % 8 + 1) * 128], xn[:, c * 128:(c + 1) * 128], [b_xn], [b_ps[bk + c // 8]])

        def evac():
            for hf in range(2):
                tt("dve", hT[hbuf][:, hf * 8:(hf + 1) * 8, m * 128:(m + 1) * 128],
                   pv[hf].rearrange("p (c t) -> p c t", t=128),
                   gcol[g][:, hf * 8:(hf + 1) * 8].unsqueeze(2).to_broadcast([128, 8, 128]),
                   ALU.mult, [b_ps[bk + hf], b_const], [b_hT[hbuf]])
        return evac

    kTb, vtb = [kT], [vt]
    b_kTb, b_vtb = [b_kT], [b_vt]
    v_, o2 = carve(o2, [128, 2, 512], BF16)
    kTb.append(v_)
    v_, o2 = carve(o2, [128, 4, 256], BF16)
    vtb.append(v_)
    b_kTb.append(P.buf("kT1"))
    b_vtb.append(P.buf("vt1"))
    ropeCb, ropeSb, b_ropeb = [ropeC], [ropeS], [b_rope]
    v_, o2 = carve(o2, [128, 4, 128], F32)
    ropeCb.append(v_)
    v_, o2 = carve(o2, [128, 4, 128], F32)
    ropeSb.append(v_)
    b_ropeb.append(P.buf("rope1"))
    units = []
    for jb in J:
        for t in range(jb["S"] // 512):
            for m in range(4):
                units.append((jb, t, m))
    ust = {}

    def st_A(u):
        jb, t, m = units[u]
        if m == 0:
            ust[(id(jb), t)] = dict(hb=cnt["hT"] % 2, tb=(cnt["hT"] // 1) % 2)
            cnt["hT"] += 1
            tb = ust[(id(jb), t)]["tb"]
            dma("sp", ropeCb[tb][:], jb["rk_c"][t * 512:(t + 1) * 512, :].rearrange("(m p) d -> p m d", p=128), f"ropeC{tb}",
                writes=[b_ropeb[tb]])
            dma("sp", ropeSb[tb][:], jb["rk_s"][t * 512:(t + 1) * 512, :].rearrange("(m p) d -> p m d", p=128), f"ropeS{tb}",
                writes=[b_ropeb[tb]])
        us = ust[(id(jb), t)]
        return norm_transpose_m(jb["x_seq"][t * 512 + m * 128:t * 512 + (m + 1) * 128, :], "g_mix_c", us["hb"], m)

    def st_B1(u):
        jb, t, m = units[u]
        us = ust[(id(jb), t)]
        hb = us["hb"]
        pi = next_ps(4, 2)
        us[("pi", m)] = pi
        for c in range(16):
            mm(psum[pi][:], hT[hb][:, c, m * 128:(m + 1) * 128], kvslab[:, c, :], c == 0, c == 15,
               [b_kvslab, b_hT[hb]], [b_ps[pi]])

    def st_B2(u):
        nonlocal ropeC, ropeS, b_rope
        jb, t, m = units[u]
        us = ust[(id(jb), t)]
        tb = us["tb"]
        pi = us[("pi", m)]
        cp("act", vtb[tb][:, m, :], psum[pi][:, 256:512], [b_ps[pi]], [b_vtb[tb]])
        ropeC, ropeS, b_rope = ropeCb[tb], ropeSb[tb], b_ropeb[tb]
        us[("post", m)] = norm_rope_T(pi, 2, 0, m, gk_bc, kTb[tb], b_kTb[tb])
        ropeC, ropeS, b_rope = ropeCb[0], ropeSb[0], b_ropeb[0]

    def st_C(u):
        jb, t, m = units[u]
        us = ust[(id(jb), t)]
        tb = us["tb"]
        us[("post", m)]()
        if m == 3:
            dma("sp", jb["KbT"][:, t * 512:(t + 1) * 512].rearrange("(g p) s -> p g s", p=128), kTb[tb][:], f"kTst{tb}", reads=[b_kTb[tb]])
            dma("sp", jb["Vb"][t * 512:(t + 1) * 512, :].rearrange("(m p) c -> p m c", p=128), vtb[tb][:], f"vtst{tb}", reads=[b_vtb[tb]])

    NU = len(units)
    cast_list = []
    for k in ("w_in",):
        for r0 in range(0, wshape[k][0], 256):
            cast_list.append((k, r0))
    every = 2
    ci = 0
    for i in range(NU + 2):
        if 0 <= i - 1 < NU:
            st_B1(i - 1)
        ev_a = st_A(i) if i < NU else None
        if 0 <= i - 1 < NU:
            st_B2(i - 1)
        if 0 <= i - 2 < NU:
            st_C(i - 2)
        if ev_a is not None:
            ev_a()
        if i >= 4 and (i - 4) % every == 0 and ci < len(cast_list):
            k, r0 = cast_list[ci]
            ci += 1
            dma("pool", WB[k][r0:r0 + 256, :], W[k][r0:r0 + 256, :], "cast_" + k, writes=[WBUF[k]])
    while ci < len(cast_list):
        k, r0 = cast_list[ci]
        ci += 1
        dma("pool", WB[k][r0:r0 + 256, :], W[k][r0:r0 + 256, :], "cast_" + k, writes=[WBUF[k]])

    def fm_slab(si, hb, evac):
        for cc in range(4):
            pi = next_ps()
            for c in range(16):
                mm(psum[pi][:], slab[si][:, c, cc * 128:(cc + 1) * 128], hT[hb][:, c, :], c == 0, c == 15,
                   [b_slab[si], b_hT[hb]], [b_ps[pi]])
            evac(cc, pi)

    def tm_slab(si, hb, evac, nk=16, src=None, src_b=None):
        src = hT[hb] if src is None else src
        src_b = [b_hT[hb]] if src_b is None else (src_b if isinstance(src_b, list) else [src_b])
        pend = None
        for m in range(4):
            pi = next_ps()
            for c in range(nk):
                mm(psum[pi][:], src[:, c, m * 128:(m + 1) * 128], slab[si][:, c, :], c == 0, c == nk - 1,
                   [b_slab[si]] + src_b, [b_ps[pi]])
            if pend is not None:
                pend()
            pend = evac(m, pi)
        if pend is not None:
            pend()

    ev_rr = [0]

    def evac_copy(dst, src, rd, wr):
        e = ("act", "dve")[ev_rr[0] % 2]
        ev_rr[0] += 1
        cp(e, dst, src, rd, wr)

    def p1b_body(si, s, jb, kind, t, hb):
        T = jb["T"]
        gi = next_stg()

        def store_fm(dst_rows, col0, ncol=512):
            dma("sp", dst_rows[:, col0:col0 + ncol].rearrange("(c p) s -> p c s", p=128), stg[gi][:, :, 0:ncol],
                f"stg{gi}", reads=[b_stg[gi]])

        if s in (0, 1, 2, 3):
            def ev(cc, pi):
                evac_copy(stg[gi][:, cc, :], psum[pi][:], [b_ps[pi]], [b_stg[gi]])
            fm_slab(si, hb, ev)
            if s < 2:
                store_fm(jb["QaT"][s * 512:(s + 1) * 512, :], t * 512)
            elif kind == "own":
                store_fm(jb["KaT"][(s - 2) * 512:(s - 1) * 512, :], 256 + t * 512)
            else:
                dma("sp", jb["KaT"][(s - 2) * 512:(s - 1) * 512, 0:256].rearrange("(c p) s -> p c s", p=128),
                    stg[gi][:, :, 0:256], f"stg{gi}", reads=[b_stg[gi]])
                dma("sp", jb["KaT"][(s - 2) * 512:(s - 1) * 512, 256 + T:512 + T].rearrange("(c p) s -> p c s", p=128),
                    stg[gi][:, :, 256:512], f"stg{gi}", reads=[b_stg[gi]])
        elif s in (4, 5):
            def ev(m, pi):
                evac_copy(stg[gi][:, m, :], psum[pi][:], [b_ps[pi]], [b_stg[gi]])
            tm_slab(si, hb, ev)
            cs = (s - 4) * 512
            if kind == "own":
                dma("sp", jb["Va"][256 + t * 512:256 + (t + 1) * 512, cs:cs + 512].rearrange("(m p) c -> p m c", p=128),
                    stg[gi][:], f"stg{gi}", reads=[b_stg[gi]])
            else:
                dma("sp", jb["Va"][0:256, cs:cs + 512].rearrange("(m p) c -> p m c", p=128),
                    stg[gi][:, 0:2, :], f"stg{gi}", reads=[b_stg[gi]])
                dma("sp", jb["Va"][256 + T:512 + T, cs:cs + 512].rearrange("(m p) c -> p m c", p=128),
                    stg[gi][:, 2:4, :], f"stg{gi}", reads=[b_stg[gi]])
        elif s in (6, 7):
            def ev(m, pi):
                return norm_rope_T(pi, 4, 0, m, gq_bc, stg[gi], b_stg[gi])
            tm_slab(si, hb, ev)
            store_fm(jb["QbT"][(s - 6) * 512:(s - 5) * 512, :], t * 512)
        else:
            def ev(cc, pi):
                act(stg[gi][:, cc, :], psum[pi][:], AF.Sigmoid, [b_ps[pi]], [b_stg[gi]])
            fm_slab(si, hb, ev)
            store_fm(jb["sgT"][(s - 9) * 512:(s - 8) * 512, :], t * 512)

    per_tile = []
    for jb in J:
        T = jb["T"]
        for kind, t in [("own", t) for t in range(T // 512)] + [("halo", 0)]:
            ctx = {}

            def prep(_slot, jb=jb, kind=kind, t=t, ctx=ctx):
                ctx["hb"] = cnt["hT"] % 2
                cnt["hT"] += 1
                if kind == "own":
                    xs = jb["x_own"][t * 512:(t + 1) * 512, :]
                    load_rope(jb["rq_c"][t * 512:(t + 1) * 512, :], jb["rq_s"][t * 512:(t + 1) * 512, :])
                else:
                    xs = jb["x_halo"]
                norm_transpose(xs, "g_mix_c", ctx["hb"], False)

            slabs = [0, 1, 2, 3, 4, 5, 6, 7, 9, 10, 11, 12, 13, 14, 15, 16] if kind == "own" else [2, 3, 4, 5]
            sj = [((lambda s=s: load_slab("w_in", 0, 16, s * 512)),
                   (lambda si, s=s, jb=jb, kind=kind, t=t, ctx=ctx: p1b_body(si, s, jb, kind, t, ctx["hb"]))) for s in slabs]
            per_tile.append(((None, prep), sj))
    jobs = []
    carry = []
    for (pj, sj) in per_tile:
        jobs.append(pj)
        jobs.extend(carry)
        k = max(0, len(sj) - 4)
        jobs.extend(sj[:k])
        carry = sj[k:]
    jobs.extend(carry)
    late_casts = []
    for k in ("w_pa", "w_pb", "w_o", "w_cq", "w_co", "w_up", "w_down"):
        for r0 in range(0, wshape[k][0], 256):
            late_casts.append((k, r0))
    nslabjobs = sum(1 for j in jobs if j[0] is not None)
    stride = max(1, (nslabjobs - 8) // len(late_casts))
    jobs2 = []
    seen = 0
    for j in jobs:
        jobs2.append(j)
        if j[0] is not None:
            seen += 1
            if seen % stride == 0 and late_casts:
                k, r0 = late_casts.pop(0)
                jobs2.append((None, (lambda _s, k=k, r0=r0: dma("pool", WB[k][r0:r0 + 256, :], W[k][r0:r0 + 256, :],
                                                                "cast_" + k, writes=[WBUF[k]]))))
    for (k, r0) in late_casts:
        jobs2.append((None, (lambda _s, k=k, r0=r0: dma("pool", WB[k][r0:r0 + 256, :], W[k][r0:r0 + 256, :],
                                                        "cast_" + k, writes=[WBUF[k]]))))
    run_pipe(jobs2)
    P.barrier()

    o2 = 0
    pT = []
    for i in range(8):
        v, o2 = carve(o2, [128, 2, 512], BF16)
        pT.append(v)
    tsum = []
    for i in range(6):
        v, o2 = carve(o2, [128, 1024], BF16)
        tsum.append(v)
    b_tsum = P.bufs(6, "tsum")
    cnt["ts"] = 0
    dacc = []
    for i in range(4):
        v, o2 = carve(o2, [128, 1024], F32)
        dacc.append(v)
    dtot, o2 = carve(o2, [128, 512], F32)
    drec, o2 = carve(o2, [128, 512], F32)
    ones16, o2 = carve(o2, [128, 128], BF16)
    dhi, o2 = carve(o2, [128, 512], BF16)
    dlo, o2 = carve(o2, [128, 512], BF16)
    b_dhl = P.buf("dhl")
    oT, o2 = carve(o2, [128, 2, 512], BF16)
    b_pT = P.bufs(8, "pT")
    b_dacc = P.bufs(4, "dacc")
    b_dtot, b_drec, b_ones = P.bufs(3, "den")
    b_oT = P.bufs(2, "oT")
    OFF_ATT = (o2 + 31) // 32 * 32
    cnt["pT"] = 0
    cnt["oT"] = 0
    cnt["stb"] = 0
    cnt["hd"] = 0
    STB = [[0, 1], [2, 3]]
    OACC = [4, 5]
    memset("dve", ones16[:], 1.0, [b_ones])

    def den_matmul(src_ap, src_b):
        cp("dve", dhi[:], src_ap, src_b, [b_dhl])
        tt("dve", dlo[:], src_ap, dhi[:], ALU.subtract, list(src_b) + [b_dhl], [b_dhl])
        mm(psum[6][:], ones16[:], dhi[:], True, False, [b_ones, b_dhl], [b_ps[6]])
        mm(psum[6][:], ones16[:], dlo[:], False, True, [b_ones, b_dhl], [b_ps[6]])

    def att_finish(par, dstT, row0, col0, used_pool, used_dve=True):
        src = par
        if used_pool and used_dve:
            tt("dve", dacc[par][:], dacc[par][:], dacc[2 + par][:], ALU.add, [b_dacc[par], b_dacc[2 + par]], [b_dacc[par]])
        elif used_pool:
            src = 2 + par
        tt("dve", dtot[:], dacc[src][:, 0:512], dacc[src][:, 512:1024], ALU.add, [b_dacc[src]], [b_dtot])
        den_matmul(dtot[:], [b_dtot])
        P.add("dve", lambda e: e.reciprocal(drec[:], psum[6][:]), [b_ps[6]], [b_drec])
        oi = cnt["oT"] % 2
        cnt["oT"] += 1
        tt("dve", oT[:, oi, :], psum[OACC[par]][:], drec[:], ALU.mult, [b_ps[OACC[par]], b_drec], [b_oT[oi]])
        dma("sp!", dstT[row0:row0 + 128, col0:col0 + 512], oT[:, oi, :], f"oT{oi}", reads=[b_oT[oi]])

    for jb in J:
        S, T = jb["S"], jb["T"]
        NCH = S // 128
        NP = NCH // 2
        o3 = OFF_ATT
        KTg, o3 = carve(o3, [128, S], BF16)
        Vg, o3 = carve(o3, [128, NCH, 128], BF16)
        qb, o3 = carve(o3, [128, 2, 512], BF16)
        b_KTg, b_Vg = P.bufs(2, "kvg")
        b_qb = P.bufs(2, "qb")
        for g in range(2):
            dma("sp", KTg[:], jb["KbT"][g * 128:(g + 1) * 128, :], "KTg", writes=[b_KTg])
            dma("sp", Vg[:], jb["Vb"][:, g * 128:(g + 1) * 128].rearrange("(c p) d -> p c d", p=128), "V1g",
                writes=[b_Vg])
            heads = [(qt, hh) for qt in range(T // 512) for hh in range(4)]
            steps = [(hi, p) for hi in range(len(heads)) for p in range(NP)]
            hstate = {}

            def stage_a(st_):
                hi, p = st_
                if p == 0:
                    par = cnt["hd"] % 2
                    cnt["hd"] += 1
                    qi = par
                    qt, hh = heads[hi]
                    h = g * 4 + hh
                    dma("sp", qb[:, qi, :], jb["QbT"][h * 128:(h + 1) * 128, qt * 512:(qt + 1) * 512], f"qb{qi}", writes=[b_qb[qi]])
                    hstate[hi] = dict(par=par, qi=qi, h=h, qt=qt, npool=0, ndve=0)
                hs = hstate[hi]
                sb_ = STB[cnt["stb"] % 2]
                cnt["stb"] += 1
                pi = cnt["pT"] % 8
                cnt["pT"] += 1
                for k in range(2):
                    ch = p * 2 + k
                    mm(psum[sb_[k]][:], KTg[:, ch * 128:(ch + 1) * 128], qb[:, hs["qi"], :], True, True,
                       [b_KTg, b_qb[hs["qi"]]], [b_ps[sb_[k]]])
                act(pT[pi][:].rearrange("p k s -> p (k s)"), psbig[sb_[0] // 2][:], AF.Exp,
                    [b_ps[sb_[0]], b_ps[sb_[1]]], [b_pT[pi]], scale=SCALE)
                return pi

            def stage_b(st_, pi):
                hi, p = st_
                hs = hstate[hi]
                par = hs["par"]
                for k in range(2):
                    ch = p * 2 + k
                    mm(psum[OACC[par]][:], Vg[:, ch, :], pT[pi][:, k, :], ch == 0, ch == NCH - 1,
                       [b_pT[pi], b_Vg], [b_ps[OACC[par]]])
                pflat = pT[pi][:].rearrange("p k s -> p (k s)")
                if p % 2 == 0:
                    hs["prev_pi"] = pi
                else:
                    ppi = hs["prev_pi"]
                    ti = cnt["ts"] % 6
                    cnt["ts"] += 1
                    tt("dve", tsum[ti][:], pT[ppi][:].rearrange("p k s -> p (k s)"), pflat, ALU.add,
                       [b_pT[ppi], b_pT[pi]], [b_tsum[ti]])
                    if p % 4 == 1:
                        hs["prev_ti"] = ti
                    else:
                        pti = hs["prev_ti"]
                        t2i = cnt["ts"] % 6
                        cnt["ts"] += 1
                        tt("dve", tsum[t2i][:], tsum[pti][:], tsum[ti][:], ALU.add, [b_tsum[pti], b_tsum[ti]], [b_tsum[t2i]])
                        if (p // 4) % 4 == 3:
                            e_, ai, key = "dve", par, "ndve"
                        else:
                            e_, ai, key = "pool", 2 + par, "npool"
                        if hs[key] == 0:
                            cp(e_, dacc[ai][:], tsum[t2i][:], [b_tsum[t2i]], [b_dacc[ai]])
                        else:
                            tt(e_, dacc[ai][:], dacc[ai][:], tsum[t2i][:], ALU.add, [b_tsum[t2i], b_dacc[ai]], [b_dacc[ai]])
                        hs[key] += 1
                if p == NP - 1:
                    att_finish(par, jb["ObT"], hs["h"] * 128, hs["qt"] * 512, hs["npool"] > 0, hs["ndve"] > 0)

            prev = None
            for st_ in steps:
                pi = stage_a(st_)
                if prev is not None:
                    stage_b(*prev)
                prev = (st_, pi)
            stage_b(*prev)
        P.barrier()

    o3 = OFF_ATT
    tabI, o3 = carve(o3, [128, 8, 16, 64], BF16)
    tabX, o3 = carve(o3, [128, 4, 8 * 16 * 64], BF16)
    qa, o3 = carve(o3, [128, 8, 512], BF16)
    ka, o3 = carve(o3, [128, 8, 1024], BF16)
    Va_, o3 = carve(o3, [128, 8, 1024], BF16)
    b_tabI, b_tabX, b_qa, b_ka, b_Va = P.bufs(5, "na")
    dma("sp", tabI[:].rearrange("p h s q -> p (h s q)"), tabs_d[0], "tabI", writes=[b_tabI])
    tabXv = tabX.rearrange("p v (h s q) -> p v h s q", h=8, s=16)
    for ji, jb in enumerate(J):
        T = jb["T"]
        nblk = T // 512
        for v in range(4):
            cls, var = v // 2, v % 2
            dma("sp", tabX[:, v, :], tabs_d[2 + (ji * 2 + cls) * 2 + var], "tabX", writes=[b_tabX])
        for b in range(nblk):
            segs_all = na_segments(b == 0, b == nblk - 1)
            dma("sp", qa[:], jb["QaT"][:, b * 512:(b + 1) * 512].rearrange("(h p) s -> p h s", p=128), "qa", writes=[b_qa])
            dma("sp", ka[:], jb["KaT"][:, b * 512:b * 512 + 1024].rearrange("(h p) s -> p h s", p=128), "ka", writes=[b_ka])
            dma("sp", Va_[:], jb["Va"][b * 512:b * 512 + 1024, :].rearrange("(c p) x -> p c x", p=128), "V1a", writes=[b_Va])
            steps = [(h, j) for h in range(8) for j in range(8)]
            hstate = {}

            def na_a(st_):
                h, j = st_
                if j == 0:
                    par = cnt["hd"] % 2
                    cnt["hd"] += 1
                    hstate[h] = dict(par=par)
                    memset("pool", dacc[par][:, 0:512], 0.0, [b_dacc[par]])
                m_lo, m_hi, segs = segs_all[j]
                c0, c1 = m_lo * 128, (m_hi + 1) * 128
                sbk = STB[cnt["stb"] % 2][cnt["stb"] // 2 % 2]
                cnt["stb"] += 1
                pi = cnt["pT"] % 8
                cnt["pT"] += 1
                mm(psum[sbk][:, c0:c1], ka[:, h, j * 128:(j + 1) * 128], qa[:, h, c0:c1], True, True,
                   [b_ka, b_qa], [b_ps[sbk]])
                act(pT[pi][:, 0, c0:c1], psum[sbk][:, c0:c1], AF.Exp, [b_ps[sbk]], [b_pT[pi]], scale=SCALE)
                eng_ = "pool" if (j % 2) else "dve"
                for (ra, rb, tb, s0) in segs:
                    n = rb - ra
                    if tb == "int":
                        tv = tabI[:, h, s0:s0 + n, :]
                        tbuf = b_tabI
                    else:
                        vi = {"first0": 0, "first1": 1, "last0": 2, "last1": 3}[tb]
                        tv = tabXv[:, vi, h, s0:s0 + n, :]
                        tbuf = b_tabX
                    pv_ = pT[pi][:, 0, ra * 64:rb * 64].rearrange("p (r q) -> p r q", q=64)
                    tt(eng_, pv_, pv_, tv, ALU.mult, [b_pT[pi], tbuf], [b_pT[pi]])
                return pi

            def na_b(st_, pi):
                h, j = st_
                par = hstate[h]["par"]
                m_lo, m_hi, segs = segs_all[j]
                c0, c1 = m_lo * 128, (m_hi + 1) * 128
                mm(psum[OACC[par]][:, c0:c1], Va_[:, j, h * 128:(h + 1) * 128], pT[pi][:, 0, c0:c1], j == 0, j == 7,
                   [b_pT[pi], b_Va], [b_ps[OACC[par]]], skip=True)
                tt("dve" if (j % 2) else "pool", dacc[par][:, c0:c1], dacc[par][:, c0:c1], pT[pi][:, 0, c0:c1], ALU.add,
                   [b_pT[pi], b_dacc[par]], [b_dacc[par]])
                if j == 7:
                    den_matmul(dacc[par][:, 0:512], [b_dacc[par]])
                    P.add("dve", lambda e: e.reciprocal(drec[:], psum[6][:]), [b_ps[6]], [b_drec])
                    oi = cnt["oT"] % 2
                    cnt["oT"] += 1
                    tt("dve", oT[:, oi, :], psum[OACC[par]][:], drec[:], ALU.mult, [b_ps[OACC[par]], b_drec], [b_oT[oi]])
                    dma("sp!", jb["OaT"][h * 128:(h + 1) * 128, b * 512:(b + 1) * 512], oT[:, oi, :], f"oT{oi}", reads=[b_oT[oi]])

            pend = []
            for st_ in steps:
                pi = na_a(st_)
                pend.append((st_, pi))
                if len(pend) > 6:
                    na_b(*pend.pop(0))
            while pend:
                na_b(*pend.pop(0))
    P.barrier()

    o2 = OFF_COMMON
    aT = []
    o_a = o2
    for i in range(2):
        v, o2 = carve(o2, [128, 16, 512], BF16)
        aT.append(v)
    OabT, o_a = carve(o_a, [128, 16, 512], BF16)
    sgs = []
    for i in range(2):
        v, o_a = carve(o_a, [128, 8, 512], BF16)
        sgs.append(v)
    tmpA, o2 = carve(o2, [128, 512], F32)
    tmpB, o2 = carve(o2, [128, 512], F32)
    qcT, o2 = carve(o2, [128, 4, 512], BF16)
    ocs, o2 = carve(o2, [128, 4, 512], BF16)
    ocT, o2 = carve(o2, [128, 4, 512], BF16)
    pTc, o2 = carve(o2, [128, 2, 512], BF16)
    gfin, o2 = carve(o2, [128, D], F32)
    b_Oab = P.buf("Oab")
    b_sgs = P.bufs(2, "sgs")
    b_aT = [[b_Oab], [b_sgs[0], b_sgs[1]]]
    b_tmpA, b_tmpB, b_qcT, b_ocs, b_ocT, b_gfin = P.bufs(6, "p3")
    b_pTc = P.bufs(2, "pTc")
    dma("sp", gfin[:], gfin_d, "gfin", writes=[b_gfin])

    def resid_add(m, pi, n):
        tt("dve", xres[:, m, n * 512:(n + 1) * 512], xres[:, m, n * 512:(n + 1) * 512], psum[pi][:], ALU.add,
           [b_ps[pi], b_x[m]], [b_x[m]])

    def p3_tile_jobs(ji, jb, t):
        tsl = slice(t * 512, (t + 1) * 512)
        ctx = {}
        jobs = []

        def xload(_):
            for m in range(4):
                dma("sp", xres[:, m, :], jb["x_own"][t * 512 + m * 128:t * 512 + (m + 1) * 128, :], f"x{m}", writes=[b_x[m]])

        def start(_):
            dma("sp", OabT[:, 0:8, :], jb["OaT"][:, tsl].rearrange("(c p) s -> p c s", p=128), "Oab", writes=[b_Oab])
            dma("sp", OabT[:, 8:16, :], jb["ObT"][:, tsl].rearrange("(c p) s -> p c s", p=128), "Oab", writes=[b_Oab])
            ctx["mixb"] = cnt["hT"] % 2
            cnt["hT"] += 1
        jobs.append((None, start))

        def ld_merge(n):
            si = next_slab()
            dma("sp", slab[si][:, 0:8, :], WB["w_pa"][:, n * 512:(n + 1) * 512].rearrange("(c p) n -> p c n", p=128),
                f"slab{si}", reads=[WBUF["w_pa"]], writes=[b_slab[si]])
            dma("sp", slab[si][:, 8:16, :], WB["w_pb"][:, n * 512:(n + 1) * 512].rearrange("(c p) n -> p c n", p=128),
                f"slab{si}", reads=[WBUF["w_pb"]], writes=[b_slab[si]])
            return si

        def merge(si, n):
            mixb = ctx["mixb"]
            gi = n % 2
            dma("sp", sgs[gi][:, 0:4, :], jb["sgT"][n * 512:(n + 1) * 512, tsl].rearrange("(c p) s -> p c s", p=128),
                f"sgs{gi}", writes=[b_sgs[gi]])
            dma("sp", sgs[gi][:, 4:8, :], jb["sgT"][2048 + n * 512:2048 + (n + 1) * 512, tsl].rearrange("(c p) s -> p c s", p=128),
                f"sgs{gi}", writes=[b_sgs[gi]])
            for cc in range(4):
                pa = next_ps()
                for c in range(8):
                    mm(psum[pa][:], slab[si][:, c, cc * 128:(cc + 1) * 128], OabT[:, c, :], c == 0, c == 7,
                       [b_slab[si], b_Oab], [b_ps[pa]])
                pbk = next_ps()
                for c in range(8):
                    mm(psum[pbk][:], slab[si][:, 8 + c, cc * 128:(cc + 1) * 128], OabT[:, 8 + c, :], c == 0, c == 7,
                       [b_slab[si], b_Oab], [b_ps[pbk]])
                tt("dve", tmpA[:], psum[pa][:], sgs[gi][:, cc, :], ALU.mult, [b_ps[pa], b_sgs[gi]], [b_tmpA])
                tt("dve", tmpB[:], psum[pbk][:], sgs[gi][:, 4 + cc, :], ALU.mult, [b_ps[pbk], b_sgs[gi]], [b_tmpB])
                tt("pool", hT[mixb][:, n * 4 + cc, :], tmpA[:], tmpB[:], ALU.add, [b_tmpA, b_tmpB], [b_hT[mixb]])
        for n in range(4):
            jobs.append(((lambda n=n: ld_merge(n)), (lambda si, n=n: merge(si, n))))
        jobs.insert(3, (None, xload))
        for n in range(4):
            jobs.append(((lambda n=n: load_slab("w_o", 0, 16, n * 512)),
                         (lambda si, n=n: tm_slab(si, ctx["mixb"], lambda m, pi: resid_add(m, pi, n)))))

        def cq(si):
            hb = cnt["hT"] % 2
            cnt["hT"] += 1
            norm_transpose(None, "g_cross_c", hb, True)

            def ev_q(cc, pi):
                cp("act", qcT[:, cc, :], psum[pi][:], [b_ps[pi]], [b_qcT])
            fm_slab(si, hb, ev_q)
            for h in range(4):
                for k in range(2):
                    pi = next_ps()
                    mm(psum[pi][:], KcT[ji][:, h, k * 128:(k + 1) * 128], qcT[:, h, :], True, True, [b_kc[ji], b_qcT], [b_ps[pi]])
                    act(pTc[:, k, :], psum[pi][:], AF.Exp, [b_ps[pi]], [b_pTc[0]], scale=SCALE)
                for m in range(4):
                    pi = 6 + m % 2
                    for k in range(2):
                        mm(psum[pi][:, 0:129], pTc[:, k, m * 128:(m + 1) * 128], Vc1[ji][:, k, h, 0:129], k == 0, k == 1,
                           [b_pTc[0], b_kc[ji]], [b_ps[pi]])
                    P.add("dve", lambda e, m=m, pi=pi: e.reciprocal(rden[:, m:m + 1], psum[pi][:, 128:129]), [b_ps[pi]], [b_rden])
                    ts("dve", ocs[:, m, h * 128:(h + 1) * 128], psum[pi][:, 0:128], rden[:, m:m + 1], None, ALU.mult, None,
                       [b_ps[pi], b_rden], [b_ocs])
            for m in range(4):
                pb = psb(6 + m % 2)
                for h in range(4):
                    tr(pb[:, h * 128:(h + 1) * 128], ocs[:, m, h * 128:(h + 1) * 128], [b_ocs], [b_ps[6 + m % 2]])
                cp("dve", ocT[:, :, m * 128:(m + 1) * 128], pb[:, 0:512].rearrange("p (h t) -> p h t", t=128),
                   [b_ps[6 + m % 2]], [b_ocT])
        jobs.append(((lambda: load_slab("w_cq", 0, 16, 0)), cq))
        for n in range(4):
            jobs.append(((lambda n=n: load_slab("w_co", 0, 4, n * 512)),
                         (lambda si, n=n: tm_slab(si, None, lambda m, pi: resid_add(m, pi, n), nk=4, src=ocT, src_b=b_ocT))))

        def mlp_norm(_):
            ctx["hb"] = cnt["hT"] % 2
            cnt["hT"] += 1
            norm_transpose(None, "g_mlp_c", ctx["hb"], True)
        jobs.append((None, mlp_norm))

        def up(si, n, ab):
            def ev_up(cc, pi):
                act(tmpA[:], psum[pi][:], AF.Relu, [b_ps[pi]], [b_tmpA])
                tt("dve", aT[ab][:, n * 4 + cc, :], tmpA[:], tmpA[:], ALU.mult, [b_tmpA], b_aT[ab])
            fm_slab(si, ctx["hb"], ev_up)
        for kg in range(4):
            ab = kg % 2
            for n in range(4):
                jobs.append(((lambda kg=kg, n=n: load_slab("w_up", 0, 16, (kg * 4 + n) * 512)),
                             (lambda si, n=n, ab=ab: up(si, n, ab))))
            for n in range(4):
                jobs.append(((lambda kg=kg, n=n: load_slab("w_down", kg * 16, 16, n * 512)),
                             (lambda si, n=n, ab=ab: tm_slab(si, None, lambda m, pi: resid_add(m, pi, n), src=aT[ab], src_b=b_aT[ab]))))

        def fin(_):
            for m in range(4):
                sbf = b_stat[m % 4]
                ss = stat[:, m:m + 1]
                xn, b_xn = xnb[m % 2], b_xnb[m % 2]
                act(xn[:], xres[:, m, :], AF.Square, [b_x[m]], [b_xn, sbf], accum=ss)
                rstd_from_ss(ss, ss, 1, D, sbf)
                stt("dve", xres[:, m, :], xres[:, m, :], ss, gfin[:], ALU.mult, ALU.mult, [b_x[m], sbf, b_gfin], [b_x[m]])
                dma("sp", jb["y"][t * 512 + m * 128:t * 512 + (m + 1) * 128, :], xres[:, m, :], f"xo{m}", reads=[b_x[m]])
        jobs.append((None, fin))
        return jobs

    jobs = []
    for ji, jb in enumerate(J):
        for t in range(jb["T"] // 512):
            jobs.extend(p3_tile_jobs(ji, jb, t))
    run_pipe(jobs)

    P.emit()
    st.close()
    return nc, P


def _rope_tables(S):
    t = np.arange(S)
    row = (t // 64).astype(np.float32)
    col = (t % 64).astype(np.float32)
    inv = (10000.0 ** (-np.arange(32, dtype=np.float32) / 32)).astype(np.float32)
    ar = row[:, None] * inv[None, :]
    ac = col[:, None] * inv[None, :]
    C = np.concatenate([np.cos(ar), np.cos(ar), np.cos(ac), np.cos(ac)], axis=1).astype(np.float32)
    Sn = np.concatenate([-np.sin(ar), np.sin(ar), -np.sin(ac), np.sin(ac)], axis=1).astype(np.float32)
    return C, Sn


def _na_consts(rpb):
    rp = np.asarray(rpb, np.float32)[0]
    e = np.arange(128) // 64
    kc = np.arange(128) % 64
    s = np.arange(16)
    qc = np.arange(64)
    ap = s[None, :] - e[:, None]
    a = 14 - ap
    slot_ok = (ap >= 0) & (ap <= 14)
    dc = kc[:, None] - qc[None, :]
    cs = np.clip(qc - 8, 0, 48)
    col_ok = (kc[:, None] >= cs[None, :]) & (kc[:, None] < cs[None, :] + 16)
    bi = np.clip(dc + 15, 0, 30)
    A = np.clip(a, 0, 14)
    g = rp[:, A[:, :, None], bi[:, None, :]]
    ok = slot_ok[:, :, None] & col_ok[:, None, :]
    g = np.where(ok[None], g, 0.0).astype(np.float32)
    rpb_exp = np.ascontiguousarray(g.transpose(1, 0, 2, 3)).reshape(128, 8 * 16 * 64)
    mfull = ok.astype(np.float32).reshape(128, 1024)
    mint = (ok & ((ap >= 4) & (ap <= 11))[:, :, None]).astype(np.float32).reshape(128, 1024)
    return rpb_exp, mfull, mint


def _gcol(g):
    return np.ascontiguousarray(np.asarray(g, np.float32).reshape(16, 128).T)


_CACHE = {}


def _run(inputs, S_p, S_s, debug=False):
    cfg = Cfg(S_p, S_s)
    key = (S_p, S_s, debug)
    if key not in _CACHE:
        _CACHE[key] = build(cfg, debug)
    nc, P = _CACHE[key]
    f = lambda k: np.asarray(inputs[k], np.float32)
    xp, xs, mp, ms = f("x_prompt"), f("x_sample"), f("mem_prompt"), f("mem_sample")
    shared = {k: np.ascontiguousarray(f(k)[0]) for k in ("w_in", "w_pa", "w_pb", "w_o", "w_cq", "w_ckv", "w_co", "w_up", "w_down")}
    shared["g_mix_c"] = _gcol(f("g_mix")[0])
    shared["g_cross_c"] = _gcol(f("g_cross")[0])
    shared["g_mem_c"] = _gcol(f("g_mem")[0])
    shared["g_mlp_c"] = _gcol(f("g_mlp")[0])
    shared["gq_bc"] = np.ascontiguousarray(np.broadcast_to(f("g_q")[0][None, :], (128, 128)))
    shared["gk_bc"] = np.ascontiguousarray(np.broadcast_to(f("g_k")[0][None, :], (128, 128)))
    shared["gfin_bc"] = np.ascontiguousarray(np.broadcast_to(f("g_final")[None, :], (128, D)))
    shared["ident"] = np.eye(128, dtype=np.float32)
    shared["rpb_exp"], shared["mask_full"], shared["mask_int"] = _na_consts(f("rpb"))
    Cp, Sp = _rope_tables(S_p)
    Cs, Ss = _rope_tables(S_s)
    Tp, Ts = S_p // 2, S_s // 8
    zeros256 = np.zeros((256, D), np.float32)
    in_maps = []
    for c in range(8):
        b, hf = c // 2, c % 2
        m = dict(shared)
        o0 = hf * Tp
        m["xp_seq"] = np.ascontiguousarray(xp[b])
        m["xp_own"] = np.ascontiguousarray(xp[b, o0:o0 + Tp])
        pre = xp[b, o0 - 256:o0] if o0 > 0 else zeros256
        post = xp[b, o0 + Tp:o0 + Tp + 256] if o0 + Tp < S_p else zeros256
        m["xp_halo"] = np.ascontiguousarray(np.concatenate([pre, post], 0))
        m["memp"] = np.ascontiguousarray(mp[b])
        m["rkp_c"], m["rkp_s"] = Cp, Sp
        m["rqp_c"], m["rqp_s"] = np.ascontiguousarray(Cp[o0:o0 + Tp]), np.ascontiguousarray(Sp[o0:o0 + Tp])
        s0 = c * Ts
        m["xs_seq"] = np.ascontiguousarray(xs[0])
        m["xs_own"] = np.ascontiguousarray(xs[0, s0:s0 + Ts])
        pre = xs[0, s0 - 256:s0] if s0 > 0 else zeros256
        post = xs[0, s0 + Ts:s0 + Ts + 256] if s0 + Ts < S_s else zeros256
        m["xs_halo"] = np.ascontiguousarray(np.concatenate([pre, post], 0))
        m["mems"] = np.ascontiguousarray(ms[0])
        m["rks_c"], m["rks_s"] = Cs, Ss
        m["rqs_c"], m["rqs_s"] = np.ascontiguousarray(Cs[s0:s0 + Ts]), np.ascontiguousarray(Ss[s0:s0 + Ts])
        selv = np.array([hf == 0, hf == 1, c == 0, c == 7], np.float32)
        m["sel"] = np.ascontiguousarray(np.broadcast_to(selv[None, :], (128, 4)))
        in_maps.append(m)
    res = run_bass_kernel_spmd(nc, in_maps, core_ids=list(range(8)))
    yp = np.zeros((4, S_p, D), np.float32)
    ys = np.zeros((1, S_s, D), np.float32)
    for c in range(8):
        b, hf = c // 2, c % 2
        yp[b, hf * Tp:(hf + 1) * Tp] = res.results[c]["yp"]
        ys[0, c * Ts:(c + 1) * Ts] = res.results[c]["ys"]
    return (yp, ys), res


def kernel(**inputs):
    (yp, ys), _ = _run(inputs, 8192, 16384)
    return (yp, ys)
```

```python
import numpy as np
from contextlib import ExitStack
import concourse.bass as bass
import concourse.mybir as mybir
from concourse.bass_utils import run_bass_kernel_spmd

F32 = mybir.dt.float32
BF16 = mybir.dt.bfloat16
U8 = mybir.dt.uint8
ALU = mybir.AluOpType
AF = mybir.ActivationFunctionType

D = 2048
DIN = 8704
DFF = 8192
NMEM = 256
EPS = 1e-6
SCALE = 128 ** -0.5


class Buf:
    __slots__ = ("name", "writers", "readers")

    def __init__(self, name):
        self.name = name
        self.writers = {}
        self.readers = {}


class Op:
    __slots__ = ("eng", "fn", "waits", "signal", "sigval", "idx", "dma")

    def __init__(self, eng, fn):
        self.eng = eng
        self.fn = fn
        self.waits = []
        self.signal = False
        self.sigval = None
        self.idx = None
        self.dma = None


ENGS = ("pe", "act", "dve", "pool", "sp")
STRICT = ("act", "dve", "pool")


class Prog:
    def __init__(self, nc):
        self.nc = nc
        self.ops = {e: [] for e in ENGS}
        self.waited = {e: {} for e in ENGS}
        self.dma_count = {}

    def buf(self, name="b"):
        return Buf(name)

    def bufs(self, n, name="b"):
        return [Buf(f"{name}{i}") for i in range(n)]

    def add(self, eng, fn, reads=(), writes=(), dma=None):
        mykey = dma if dma is not None else eng
        deps = []
        for b in reads:
            for k, t in b.writers.items():
                if k != mykey or (dma is None and eng in STRICT):
                    deps.append(t)
        for b in writes:
            for k, t in b.writers.items():
                if k != mykey:
                    deps.append(t)
            for k, t in b.readers.items():
                if k != mykey:
                    deps.append(t)
        op = Op(eng, fn)
        w = self.waited[eng]
        for t in deps:
            if t[0] == "c":
                p = t[1]
                if w.get(p.eng, -1) >= p.idx:
                    continue
                w[p.eng] = p.idx
                p.signal = True
                op.waits.append(t)
            else:
                sk = t[1]
                v = self.dma_count[sk]
                if w.get(sk, 0) >= v:
                    continue
                w[sk] = v
                op.waits.append(("d", sk, v))
        op.idx = len(self.ops[eng])
        self.ops[eng].append(op)
        if dma is not None:
            self.dma_count[dma] = self.dma_count.get(dma, 0) + 16
            op.dma = (dma, self.dma_count[dma])
            tok = ("d", dma, self.dma_count[dma])
        else:
            tok = ("c", op)
        for b in reads:
            b.readers[mykey] = tok
        for b in writes:
            b.writers = {mykey: tok}
            b.readers = {}
        return op

    def barrier(self):
        lasts = {}
        for e in ("pe", "act", "dve", "pool"):
            cl = [o for o in self.ops[e] if o.dma is None and o.fn is not None]
            if cl:
                lasts[e] = cl[-1]
                cl[-1].signal = True
        for f in ENGS:
            op = Op(f, None)
            w = self.waited[f]
            for e, l in lasts.items():
                if e != f and w.get(e, -1) < l.idx:
                    w[e] = l.idx
                    op.waits.append(("c", l))
            for sk, v in self.dma_count.items():
                if w.get(sk, 0) < v:
                    w[sk] = v
                    op.waits.append(("d", sk, v))
            op.idx = len(self.ops[f])
            self.ops[f].append(op)

    def emit(self):
        nc = self.nc
        self.barrier()
        for e in ENGS:
            c = 0
            for op in self.ops[e]:
                if op.signal and op.dma is None:
                    c += 1
                    op.sigval = c
        with ExitStack() as st:
            esem = {e: st.enter_context(nc.semaphore(f"s_{e}")) for e in ENGS}
            dsem = {}
            for i, sk in enumerate(self.dma_count):
                dsem[sk] = st.enter_context(nc.semaphore(f"d_{i}"))
            block = st.enter_context(nc.Block())

            def run(e, eng):
                for op in self.ops[e]:
                    for t in op.waits:
                        if t[0] == "c":
                            eng.wait_ge(esem[t[1].eng], t[1].sigval)
                        else:
                            eng.wait_ge(dsem[t[1]], t[2])
                    if op.fn is None:
                        continue
                    ins = op.fn(eng)
                    if op.dma is not None:
                        ins.then_inc(dsem[op.dma[0]], 16)
                    elif op.signal:
                        ins.then_inc(esem[e], 1)

            block.tensor(lambda eng: run("pe", eng))
            block.scalar(lambda eng: run("act", eng))
            block.vector(lambda eng: run("dve", eng))
            block.gpsimd(lambda eng: run("pool", eng))
            block.sync(lambda eng: run("sp", eng))
        self.stats = {e: len(self.ops[e]) for e in ENGS}


def na_segments(first, last):
    out = []
    for j in range(8):
        m_lo, m_hi = max(0, j - 4), min(3, j)
        if 2 <= j <= 5:
            if first:
                m_lo = 0
            if last:
                m_hi = 3
        r_lo, r_hi = 2 * m_lo, 2 * m_hi + 2
        segs = []
        r = r_lo
        while r < r_hi:
            if first and r < 4:
                e_ = min(4, r_hi)
                tb = "first1" if 2 <= j <= 5 else "first0"
            elif last and r >= 5:
                e_ = r_hi
                tb = "last1" if 2 <= j <= 5 else "last0"
            else:
                e_ = r_hi
                if last:
                    e_ = min(e_, 5)
                tb = "int"
            segs.append((r, e_, tb, 11 - 2 * j + r))
            r = e_
        for (a, b, tb, s0) in segs:
            assert 0 <= s0 and s0 + (b - a) <= 16, (j, segs)
        out.append((m_lo, m_hi, segs))
    return out


class Cfg:
    def __init__(self, S_p, S_s):
        self.jobs = [dict(n="p", S=S_p, T=S_p // 2), dict(n="s", S=S_s, T=S_s // 8)]


def build(cfg, debug=False):
    nc = bass.Bass("TRN2", target_bir_lowering=False)
    P = Prog(nc)
    st = ExitStack()
    dkind = "ExternalOutput" if debug else "Internal"

    def din(name, shape, dt=F32):
        return nc.dram_tensor(name, list(shape), dt, kind="ExternalInput").ap()

    def dscr(name, shape, dt=BF16):
        return nc.dram_tensor(name, list(shape), dt, kind=dkind).ap()

    W = {}
    wshape = dict(w_in=(D, DIN), w_pa=(1024, D), w_pb=(1024, D), w_o=(D, D), w_cq=(D, 512),
                  w_ckv=(D, 1024), w_co=(512, D), w_up=(D, DFF), w_down=(DFF, D))
    WB = {}
    WBUF = {}
    for k, s in wshape.items():
        W[k] = din(k, s)
        WB[k] = nc.dram_tensor("b_" + k, list(s), BF16, kind="Internal").ap()
        WBUF[k] = P.buf("wb_" + k)
    gcols = {k: din(k, (128, 16)) for k in ("g_mix_c", "g_cross_c", "g_mem_c", "g_mlp_c")}
    gq_bc_d = din("gq_bc", (128, 128))
    gk_bc_d = din("gk_bc", (128, 128))
    gfin_d = din("gfin_bc", (128, D))
    ident_d = din("ident", (128, 128))
    sel_d = din("sel", (128, 4))
    rpbx_d = din("rpb_exp", (128, 8 * 16 * 64))
    mfull_d = din("mask_full", (128, 16 * 64))
    mint_d = din("mask_int", (128, 16 * 64))
    tabs_d = dscr("tabs", (10, 128, 8192))
    J = cfg.jobs
    for jb in J:
        n, S, T = jb["n"], jb["S"], jb["T"]
        jb["x_seq"] = din(f"x{n}_seq", (S, D))
        jb["x_own"] = din(f"x{n}_own", (T, D))
        jb["x_halo"] = din(f"x{n}_halo", (512, D))
        jb["mem"] = din(f"mem{n}", (NMEM, D))
        jb["rk_c"] = din(f"rk{n}_c", (S, 128))
        jb["rk_s"] = din(f"rk{n}_s", (S, 128))
        jb["rq_c"] = din(f"rq{n}_c", (T, 128))
        jb["rq_s"] = din(f"rq{n}_s", (T, 128))
        jb["y"] = nc.dram_tensor(f"y{n}", [T, D], F32, kind="ExternalOutput").ap()
        jb["KbT"] = dscr(f"KbT{n}", (256, S))
        jb["Vb"] = dscr(f"Vb{n}", (S, 256))
        jb["QaT"] = dscr(f"QaT{n}", (1024, T))
        jb["KaT"] = dscr(f"KaT{n}", (1024, T + 512))
        jb["Va"] = dscr(f"Va{n}", (T + 512, 1024))
        jb["QbT"] = dscr(f"QbT{n}", (1024, T))
        jb["sgT"] = dscr(f"sgT{n}", (4096, T))
        jb["OaT"] = dscr(f"OaT{n}", (1024, T))
        jb["ObT"] = dscr(f"ObT{n}", (1024, T))

    def sb(name, shape, dt):
        return st.enter_context(nc.sbuf_tensor("s_" + name, list(shape), dt))

    ARENA = 186 * 1024
    arena = sb("arena", [128, ARENA], U8)

    def carve(off, shape, dt):
        esz = 4 if dt == F32 else 2
        n = int(np.prod(shape[1:]))
        assert off % 32 == 0 and off + n * esz <= ARENA, (off, shape)
        v = arena[:, off:off + n * esz].bitcast(dt)
        if len(shape) == 3:
            v = v.rearrange("p (a b) -> p a b", b=shape[2])
        elif len(shape) == 4:
            v = v.rearrange("p (a b c) -> p a b c", b=shape[2], c=shape[3])
        return v, off + n * esz

    ident = sb("ident", [128, 128], BF16)
    gcol = {k: sb(k, [128, 16], F32) for k in gcols}
    gq_bc = sb("gq_bc", [128, 128], F32)
    gk_bc = sb("gk_bc", [128, 128], F32)
    sel = sb("sel", [128, 4], F32)
    nsel = sb("nsel", [128, 4], F32)
    stat = sb("stat", [128, 64], F32)
    epsb = sb("epsb", [128, 1], F32)
    rden = sb("rden", [128, 8], F32)
    b_rden = P.buf("rden")
    KcT = [sb(f"KcT{i}", [128, 4, NMEM], BF16) for i in range(2)]
    Vc1 = [sb(f"Vc1{i}", [128, 2, 4, 136], BF16) for i in range(2)]
    b_const = P.buf("const")
    b_stat = P.bufs(8, "stat")
    b_kc = P.bufs(2, "kc")

    psbig = [st.enter_context(nc.psum_tensor(f"ps{i}", [128, 1024], F32)) for i in range(4)]
    psum = [psbig[i // 2][:, (i % 2) * 512:(i % 2 + 1) * 512] for i in range(8)]
    b_ps = P.bufs(8, "ps")

    def psb(i):
        return psum[i].bitcast(BF16)

    def mm(out, lhsT, rhs, start, stop, reads, writes, skip=False):
        if skip:
            P.add("pe", lambda e: e.matmul(out, lhsT, rhs, start=start, stop=stop, skip_group_check=True), reads, writes)
        else:
            P.add("pe", lambda e: e.matmul(out, lhsT, rhs, start=start, stop=stop), reads, writes)

    def tr(out, in_, reads, writes):
        P.add("pe", lambda e: e.transpose(out, in_, ident[:]), list(reads) + [b_const], writes)

    def act(out, in_, func, reads, writes, bias=0.0, scale=1.0, accum=None):
        if accum is None:
            P.add("act", lambda e: e.activation(out, in_, func, bias=bias, scale=scale), reads, writes)
        else:
            P.add("act", lambda e: e.activation(out, in_, func, bias=bias, scale=scale, accum_out=accum), reads, writes)

    def ts(eng, out, in0, s1, s2, op0, op1, reads, writes):
        if op1 is None:
            P.add(eng, lambda e: e.tensor_scalar(out, in0, s1, s2, op0), reads, writes)
        else:
            P.add(eng, lambda e: e.tensor_scalar(out, in0, s1, s2, op0, op1), reads, writes)

    def tt(eng, out, in0, in1, op, reads, writes):
        P.add(eng, lambda e: e.tensor_tensor(out, in0, in1, op), reads, writes)

    def stt(eng, out, in0, scalar, in1, op0, op1, reads, writes):
        P.add(eng, lambda e: e.scalar_tensor_tensor(out, in0, scalar, in1, op0, op1), reads, writes)

    def cp(eng, out, in_, reads, writes):
        if eng == "act":
            P.add("act", lambda e: e.copy(out, in_), reads, writes)
        else:
            P.add(eng, lambda e: e.tensor_copy(out, in_), reads, writes)

    def memset(eng, ap, val, writes):
        P.add(eng, lambda e: e.memset(ap, val), (), writes)

    def dma(q, out, in_, key, reads=(), writes=()):
        if q == "sp!":
            q = "sp"
        elif q == "sp" and len(writes) == 0:
            q = "pool"
        P.add(q, lambda e: e.dma_start(out=out, in_=in_), reads, writes, dma=key)

    def run_pipe(jobs, ahead=2):
        L = [i for i, j in enumerate(jobs) if j[0] is not None]
        slots = {}
        nl = 0
        seen = 0
        for i, (lf, cf) in enumerate(jobs):
            if lf is not None:
                seen += 1
            while nl < len(L) and nl < seen + ahead:
                slots[L[nl]] = jobs[L[nl]][0]()
                nl += 1
            cf(slots.get(i))

    dma("pool", ident[:], ident_d, "c_id", writes=[b_const])
    for k in gcols:
        dma("sp", gcol[k][:], gcols[k], "c_" + k, writes=[b_const])
    dma("sp", gq_bc[:], gq_bc_d, "c_gq", writes=[b_const])
    dma("sp", gk_bc[:], gk_bc_d, "c_gk", writes=[b_const])
    dma("sp", sel[:], sel_d, "c_sel", writes=[b_const])
    memset("dve", epsb[:], EPS, [b_const])
    ts("dve", nsel[:], sel[:], -1.0, 1.0, ALU.mult, ALU.add, [b_const], [b_stat[7]])
    def cast_weights(names):
        for k in names:
            rows = wshape[k][0]
            step = 256
            for r0 in range(0, rows, step):
                dma("pool", WB[k][r0:r0 + step, :], W[k][r0:r0 + step, :], "cast_" + k, writes=[WBUF[k]])
    WB["w_kv"] = nc.dram_tensor("b_w_kv", [D, 512], BF16, kind="Internal").ap()
    WBUF["w_kv"] = P.buf("wb_w_kv")
    dma("pool", WB["w_kv"], W["w_in"][:, 4096:4608], "cast_w_kv", writes=[WBUF["w_kv"]])
    cast_weights(["w_ckv"])

    off = 0
    slab = []
    for i in range(3):
        v, off = carve(off, [128, 16, 512], BF16)
        slab.append(v)
    hT = []
    for i in range(2):
        v, off = carve(off, [128, 16, 512], BF16)
        hT.append(v)
    xres, off = carve(off, [128, 4, D], F32)
    xnb = []
    for i in range(2):
        v, off = carve(off, [128, D], BF16)
        xnb.append(v)
    OFF_COMMON = off
    stg = []
    for i in range(3):
        v, off = carve(off, [128, 4, 512], BF16)
        stg.append(v)
    ropeC, off = carve(off, [128, 4, 128], F32)
    ropeS, off = carve(off, [128, 4, 128], F32)
    kn, off = carve(off, [128, 4, 128], F32)
    t1, off = carve(off, [128, 4, 128], F32)
    t2, off = carve(off, [128, 4, 128], F32)
    krb = []
    for i in range(2):
        v, off = carve(off, [128, 4, 128], BF16)
        krb.append(v)
    OFF_P13 = off
    b_slab = P.bufs(3, "slab")
    b_hT = P.bufs(2, "hT")
    b_x = P.bufs(4, "x")
    b_xnb = P.bufs(2, "xn")
    b_stg = P.bufs(3, "stg")
    b_rope = P.buf("rope")
    b_kn, b_t1, b_t2 = P.bufs(3, "ropetmp")
    b_krb = P.bufs(2, "krb")
    cnt = dict(slab=0, stg=0, hT=0, ps=0, st=0, xn=0)

    def next_slab():
        i = cnt["slab"] % 3
        cnt["slab"] += 1
        return i

    def load_slab(wname, k0, nk, c0, ncols=512):
        i = next_slab()
        src = WB[wname][k0 * 128:(k0 + nk) * 128, c0:c0 + ncols].rearrange("(c p) n -> p c n", p=128)
        dma("sp", slab[i][:, 0:nk, 0:ncols], src, f"slab{i}", reads=[WBUF[wname]], writes=[b_slab[i]])
        return i

    def next_ps(lo=2, n=4):
        i = lo + cnt["ps"] % n
        cnt["ps"] += 1
        return i

    def next_stg():
        i = cnt["stg"] % 3
        cnt["stg"] += 1
        return i

    def rstd_from_ss(ss_ap, out_ap, n, width, b):
        act(out_ap, ss_ap, AF.Sqrt, [b, b_const], [b], bias=epsb[:], scale=1.0 / width)
        P.add("dve", lambda e: e.reciprocal(out_ap, out_ap), [b], [b])

    def norm_transpose(x_src, g, hbuf, keep_x):
        for m in range(4):
            if x_src is not None:
                dma("sp", xres[:, m, :], x_src[m * 128:(m + 1) * 128, :], f"x{m}", writes=[b_x[m]])
            sbf = b_stat[m % 4]
            ss = stat[:, m:m + 1]
            xi = cnt["xn"] % 2
            cnt["xn"] += 1
            xn, b_xn = xnb[xi], b_xnb[xi]
            act(xn[:], xres[:, m, :], AF.Square, [b_x[m]], [b_xn, sbf], accum=ss)
            rstd_from_ss(ss, ss, 1, D, sbf)
            ts("dve", xn[:], xres[:, m, :], ss, None, ALU.mult, None, [b_x[m], sbf], [b_xn])
            pv = [psb(0), psb(1)]
            for c in range(16):
                tr(pv[c // 8][:, (c % 8) * 128:(c % 8 + 1) * 128], xn[:, c * 128:(c + 1) * 128], [b_xn], [b_ps[c // 8]])
            for hf in range(2):
                tt("dve", hT[hbuf][:, hf * 8:(hf + 1) * 8, m * 128:(m + 1) * 128],
                   pv[hf].rearrange("p (c t) -> p c t", t=128),
                   gcol[g][:, hf * 8:(hf + 1) * 8].unsqueeze(2).to_broadcast([128, 8, 128]),
                   ALU.mult, [b_ps[hf], b_const], [b_hT[hbuf]])

    def load_rope(c_src, s_src):
        dma("sp", ropeC[:], c_src.rearrange("(m p) d -> p m d", p=128), "ropeC", writes=[b_rope])
        dma("sp", ropeS[:], s_src.rearrange("(m p) d -> p m d", p=128), "ropeS", writes=[b_rope])

    def nr_compute(ps_i, nh, col0, m, g_bc, kb):
        sbf = b_stat[4 + kb]
        ssq = stat[:, 8 + 8 * kb: 8 + 8 * kb + nh]
        for h in range(nh):
            act(t1[:, h, :], psum[ps_i][:, col0 + h * 128: col0 + (h + 1) * 128], AF.Square,
                [b_ps[ps_i]], [b_t1, sbf], accum=ssq[:, h:h + 1])
        rstd_from_ss(ssq, ssq, nh, 128, sbf)
        for h in range(nh):
            stt("dve", kn[:, h, :], psum[ps_i][:, col0 + h * 128: col0 + (h + 1) * 128], ssq[:, h:h + 1], g_bc[:],
                ALU.mult, ALU.mult, [b_ps[ps_i], sbf, b_const], [b_kn])
        Cb = ropeC[:, m, :].unsqueeze(1).to_broadcast([128, nh, 128])
        tt("dve", t1[:, 0:nh, :], kn[:, 0:nh, :], Cb, ALU.mult, [b_kn, b_rope], [b_t1])
        knv = kn[:, 0:nh, :].rearrange("p h (a b c) -> p h a b c", a=2, b=2)
        t2v = t2[:, 0:nh, :].rearrange("p h (a b c) -> p h a b c", a=2, b=2)
        Sv = ropeS[:, m, :].rearrange("p (a b c) -> p a b c", a=2, b=2)
        for hf in range(2):
            tt("pool", t2v[:, :, :, hf, :], knv[:, :, :, 1 - hf, :], Sv[:, :, hf, :].unsqueeze(1).to_broadcast([128, nh, 2, 32]),
               ALU.mult, [b_kn, b_rope], [b_t2])
        tt("dve", krb[kb][:, 0:nh, :], t1[:, 0:nh, :], t2[:, 0:nh, :], ALU.add, [b_t1, b_t2], [b_krb[kb]])

    def nr_transpose(nh, m, kb, dst, dst_b):
        pb = psb(6 + kb)
        for h in range(nh):
            tr(pb[:, h * 128:(h + 1) * 128], krb[kb][:, h, :], [b_krb[kb]], [b_ps[6 + kb]])
        cp("dve", dst[:, 0:nh, m * 128:(m + 1) * 128], pb[:, 0:nh * 128].rearrange("p (h t) -> p h t", t=128),
           [b_ps[6 + kb]], [dst_b])

    cnt["kr"] = 0

    def norm_rope_T(ps_i, nh, col0, m, g_bc, dst, dst_b):
        kb = cnt["kr"] % 2
        cnt["kr"] += 1
        nr_compute(ps_i, nh, col0, m, g_bc, kb)
        return lambda: nr_transpose(nh, m, kb, dst, dst_b)

    o2 = 0
    tA, o2 = carve(o2, [128, 8192], F32)
    tB, o2 = carve(o2, [128, 8, 1024], BF16)
    tC, o2 = carve(o2, [128, 8, 1024], BF16)
    tD, o2 = carve(o2, [128, 8, 1024], BF16)
    tE, o2 = carve(o2, [128, 8, 1024], BF16)
    tM, o2 = carve(o2, [128, 2, 1024], F32)
    b_tA, b_tB, b_tC, b_tD, b_tE, b_tM = P.bufs(6, "tab")
    dma("sp", tA[:], rpbx_d, "tA", writes=[b_tA])
    dma("sp", tM[:, 0, :], mfull_d, "tM", writes=[b_tM])
    dma("sp", tM[:, 1, :], mint_d, "tM", writes=[b_tM])
    act(tA[:], tA[:], AF.Exp, [b_tA], [b_tA])
    tA3 = tA.rearrange("p (h x) -> p h x", h=8)
    tt("dve", tB[:], tA3, tM[:, 0, :].unsqueeze(1).to_broadcast([128, 8, 1024]), ALU.mult, [b_tA, b_tM], [b_tB])
    tt("dve", tC[:], tA3, tM[:, 1, :].unsqueeze(1).to_broadcast([128, 8, 1024]), ALU.mult, [b_tA, b_tM], [b_tC])
    tt("dve", tD[:], tB[:], tC[:], ALU.subtract, [b_tB, b_tC], [b_tD])
    dma("sp", tabs_d[0].rearrange("p (h x) -> p h x", h=8), tC[:], "tabst", reads=[b_tC])
    dma("sp", tabs_d[1].rearrange("p (h x) -> p h x", h=8), tB[:], "tabst", reads=[b_tB])
    for q in range(4):
        stt("dve", tE[:], tD[:], sel[:, q:q + 1], tC[:], ALU.mult, ALU.add, [b_tD, b_tC, b_const], [b_tE])
        dma("sp", tabs_d[2 + q * 2 + 1].rearrange("p (h x) -> p h x", h=8), tE[:], "tabsE", reads=[b_tE])
        ts("dve", tE[:], tC[:], nsel[:, q:q + 1], None, ALU.mult, None, [b_tC, b_stat[7]], [b_tE])
        dma("sp", tabs_d[2 + q * 2 + 0].rearrange("p (h x) -> p h x", h=8), tE[:], "tabsE", reads=[b_tE])

    P.barrier()
    for ji, jb in enumerate(J):
        for half in range(2):
            src = jb["mem"][half * 128:(half + 1) * 128, :]
            dma("sp", xres[:, half, :], src, f"x{half}", writes=[b_x[half]])
            sbf = b_stat[half]
            ss = stat[:, half:half + 1]
            xn, b_xn = xnb[half], b_xnb[half]
            act(xn[:], xres[:, half, :], AF.Square, [b_x[half]], [b_xn, sbf], accum=ss)
            rstd_from_ss(ss, ss, 1, D, sbf)
            ts("dve", xn[:], xres[:, half, :], ss, None, ALU.mult, None, [b_x[half], sbf], [b_xn])
            pv = [psb(0), psb(1)]
            for c in range(16):
                tr(pv[c // 8][:, (c % 8) * 128:(c % 8 + 1) * 128], xn[:, c * 128:(c + 1) * 128], [b_xn], [b_ps[c // 8]])
            for hf in range(2):
                tt("dve", hT[0][:, hf * 8:(hf + 1) * 8, half * 128:(half + 1) * 128],
                   pv[hf].rearrange("p (c t) -> p c t", t=128),
                   gcol["g_mem_c"][:, hf * 8:(hf + 1) * 8].unsqueeze(2).to_broadcast([128, 8, 128]),
                   ALU.mult, [b_ps[hf], b_const], [b_hT[0]])
        si = load_slab("w_ckv", 0, 16, 0)
        for h in range(4):
            pi = next_ps()
            for c in range(16):
                mm(psum[pi][:, 0:NMEM], slab[si][:, c, h * 128:(h + 1) * 128], hT[0][:, c, 0:NMEM], c == 0, c == 15,
                   [b_slab[si], b_hT[0]], [b_ps[pi]])
            cp("dve", KcT[ji][:, h, :], psum[pi][:, 0:NMEM], [b_ps[pi]], [b_kc[ji]])
        si = load_slab("w_ckv", 0, 16, 512)
        memset("dve", Vc1[ji][:, :, :, 128:129], 1.0, [b_kc[ji]])
        for half in range(2):
            pi = next_ps()
            for c in range(16):
                mm(psum[pi][:], hT[0][:, c, half * 128:(half + 1) * 128], slab[si][:, c, :], c == 0, c == 15,
                   [b_slab[si], b_hT[0]], [b_ps[pi]])
            cp("dve", Vc1[ji][:, half, :, 0:128], psum[pi][:].rearrange("p (h d) -> p h d", d=128), [b_ps[pi]], [b_kc[ji]])

    o2 = OFF_P13
    kvslab, o2 = carve(o2, [128, 16, 512], BF16)
    kT, o2 = carve(o2, [128, 2, 512], BF16)
    vt, o2 = carve(o2, [128, 4, 256], BF16)
    b_kvslab, b_kT, b_vt = P.bufs(3, "kv")
    dma("sp", kvslab[:], WB["w_kv"].rearrange("(c p) n -> p c n", p=128), "kvslab",
        reads=[WBUF["w_kv"]], writes=[b_kvslab])
    deferred_cast = [True]
    cnt["tA"] = 0

    def norm_transpose_m(x_rows, g, hbuf, m):
        dma("sp", xres[:, m, :], x_rows, f"x{m}", writes=[b_x[m]])
        sbf = b_stat[m % 4]
        ss = stat[:, m:m + 1]
        xi = cnt["xn"] % 2
        cnt["xn"] += 1
        xn, b_xn = xnb[xi], b_xnb[xi]
        act(xn[:], xres[:, m, :], AF.Square, [b_x[m]], [b_xn, sbf], accum=ss)
        rstd_from_ss(ss, ss, 1, D, sbf)
        act(xn[:], xres[:, m, :], AF.Copy, [b_x[m], sbf], [b_xn], scale=ss)
        bk = 2 * (cnt["tA"] % 2)
        cnt["tA"] += 1
        pv = [psb(bk), psb(bk + 1)]
        for c in range(16):
            tr(pv[c // 8][:, (c % 8) * 128:(c % 8 + 1) * 128], xn[:, c * 128:(c + 1) * 128], [b_xn], [b_ps[bk + c // 8]])

        def evac():
            for hf in range(2):
                tt("dve", hT[hbuf][:, hf * 8:(hf + 1) * 8, m * 128:(m + 1) * 128],
                   pv[hf].rearrange("p (c t) -> p c t", t=128),
                   gcol[g][:, hf * 8:(hf + 1) * 8].unsqueeze(2).to_broadcast([128, 8, 128]),
                   ALU.mult, [b_ps[bk + hf], b_const], [b_hT[hbuf]])
        return evac

    kTb, vtb = [kT], [vt]
    b_kTb, b_vtb = [b_kT], [b_vt]
    v_, o2 = carve(o2, [128, 2, 512], BF16)
    kTb.append(v_)
    v_, o2 = carve(o2, [128, 4, 256], BF16)
    vtb.append(v_)
    b_kTb.append(P.buf("kT1"))
    b_vtb.append(P.buf("vt1"))
    ropeCb, ropeSb, b_ropeb = [ropeC], [ropeS], [b_rope]
    v_, o2 = carve(o2, [128, 4, 128], F32)
    ropeCb.append(v_)
    v_, o2 = carve(o2, [128, 4, 128], F32)
    ropeSb.append(v_)
    b_ropeb.append(P.buf("rope1"))
    units = []
    for jb in J:
        for t in range(jb["S"] // 512):
            for m in range(4):
                units.append((jb, t, m))
    ust = {}

    def st_A(u):
        jb, t, m = units[u]
        if m == 0:
            ust[(id(jb), t)] = dict(hb=cnt["hT"] % 2, tb=(cnt["hT"] // 1) % 2)
            cnt["hT"] += 1
            tb = ust[(id(jb), t)]["tb"]
            dma("sp", ropeCb[tb][:], jb["rk_c"][t * 512:(t + 1) * 512, :].rearrange("(m p) d -> p m d", p=128), f"ropeC{tb}",
                writes=[b_ropeb[tb]])
            dma("sp", ropeSb[tb][:], jb["rk_s"][t * 512:(t + 1) * 512, :].rearrange("(m p) d -> p m d", p=128), f"ropeS{tb}",
                writes=[b_ropeb[tb]])
        us = ust[(id(jb), t)]
        return norm_transpose_m(jb["x_seq"][t * 512 + m * 128:t * 512 + (m + 1) * 128, :], "g_mix_c", us["hb"], m)

    def st_B1(u):
        jb, t, m = units[u]
        us = ust[(id(jb), t)]
        hb = us["hb"]
        pi = next_ps(4, 2)
        us[("pi", m)] = pi
        for c in range(16):
            mm(psum[pi][:], hT[hb][:, c, m * 128:(m + 1) * 128], kvslab[:, c, :], c == 0, c == 15,
               [b_kvslab, b_hT[hb]], [b_ps[pi]])

    def st_B2(u):
        nonlocal ropeC, ropeS, b_rope
        jb, t, m = units[u]
        us = ust[(id(jb), t)]
        tb = us["tb"]
        pi = us[("pi", m)]
        cp("act", vtb[tb][:, m, :], psum[pi][:, 256:512], [b_ps[pi]], [b_vtb[tb]])
        ropeC, ropeS, b_rope = ropeCb[tb], ropeSb[tb], b_ropeb[tb]
        us[("post", m)] = norm_rope_T(pi, 2, 0, m, gk_bc, kTb[tb], b_kTb[tb])
        ropeC, ropeS, b_rope = ropeCb[0], ropeSb[0], b_ropeb[0]

    def st_C(u):
        jb, t, m = units[u]
        us = ust[(id(jb), t)]
        tb = us["tb"]
        us[("post", m)]()
        if m == 3:
            dma("sp", jb["KbT"][:, t * 512:(t + 1) * 512].rearrange("(g p) s -> p g s", p=128), kTb[tb][:], f"kTst{tb}", reads=[b_kTb[tb]])
            dma("sp", jb["Vb"][t * 512:(t + 1) * 512, :].rearrange("(m p) c -> p m c", p=128), vtb[tb][:], f"vtst{tb}", reads=[b_vtb[tb]])

    NU = len(units)
    cast_list = []
    for k in ("w_in",):
        for r0 in range(0, wshape[k][0], 256):
            cast_list.append((k, r0))
    every = 2
    ci = 0
    for i in range(NU + 3):
        if 0 <= i - 2 < NU:
            st_B1(i - 2)
        ev_a = st_A(i) if i < NU else None
        if 0 <= i - 2 < NU:
            st_B2(i - 2)
        if 0 <= i - 3 < NU:
            st_C(i - 3)
        if ev_a is not None:
            ev_a()
        if i >= 4 and (i - 4) % every == 0 and ci < len(cast_list):
            k, r0 = cast_list[ci]
            ci += 1
            dma("pool", WB[k][r0:r0 + 256, :], W[k][r0:r0 + 256, :], "cast_" + k, writes=[WBUF[k]])
    while ci < len(cast_list):
        k, r0 = cast_list[ci]
        ci += 1
        dma("pool", WB[k][r0:r0 + 256, :], W[k][r0:r0 + 256, :], "cast_" + k, writes=[WBUF[k]])

    def fm_slab(si, hb, evac):
        for cc in range(4):
            pi = next_ps()
            for c in range(16):
                mm(psum[pi][:], slab[si][:, c, cc * 128:(cc + 1) * 128], hT[hb][:, c, :], c == 0, c == 15,
                   [b_slab[si], b_hT[hb]], [b_ps[pi]])
            evac(cc, pi)

    def tm_slab(si, hb, evac, nk=16, src=None, src_b=None):
        src = hT[hb] if src is None else src
        src_b = [b_hT[hb]] if src_b is None else (src_b if isinstance(src_b, list) else [src_b])
        pend = None
        for m in range(4):
            pi = next_ps()
            for c in range(nk):
                mm(psum[pi][:], src[:, c, m * 128:(m + 1) * 128], slab[si][:, c, :], c == 0, c == nk - 1,
                   [b_slab[si]] + src_b, [b_ps[pi]])
            if pend is not None:
                pend()
            pend = evac(m, pi)
        if pend is not None:
            pend()

    ev_rr = [0]

    def evac_copy(dst, src, rd, wr):
        e = ("act", "dve")[ev_rr[0] % 2]
        ev_rr[0] += 1
        cp(e, dst, src, rd, wr)

    def p1b_body(si, s, jb, kind, t, hb):
        T = jb["T"]
        gi = next_stg()

        def store_fm(dst_rows, col0, ncol=512):
            dma("sp", dst_rows[:, col0:col0 + ncol].rearrange("(c p) s -> p c s", p=128), stg[gi][:, :, 0:ncol],
                f"stg{gi}", reads=[b_stg[gi]])

        if s in (0, 1, 2, 3):
            def ev(cc, pi):
                evac_copy(stg[gi][:, cc, :], psum[pi][:], [b_ps[pi]], [b_stg[gi]])
            fm_slab(si, hb, ev)
            if s < 2:
                store_fm(jb["QaT"][s * 512:(s + 1) * 512, :], t * 512)
            elif kind == "own":
                store_fm(jb["KaT"][(s - 2) * 512:(s - 1) * 512, :], 256 + t * 512)
            else:
                dma("sp", jb["KaT"][(s - 2) * 512:(s - 1) * 512, 0:256].rearrange("(c p) s -> p c s", p=128),
                    stg[gi][:, :, 0:256], f"stg{gi}", reads=[b_stg[gi]])
                dma("sp", jb["KaT"][(s - 2) * 512:(s - 1) * 512, 256 + T:512 + T].rearrange("(c p) s -> p c s", p=128),
                    stg[gi][:, :, 256:512], f"stg{gi}", reads=[b_stg[gi]])
        elif s in (4, 5):
            def ev(m, pi):
                evac_copy(stg[gi][:, m, :], psum[pi][:], [b_ps[pi]], [b_stg[gi]])
            tm_slab(si, hb, ev)
            cs = (s - 4) * 512
            if kind == "own":
                dma("sp", jb["Va"][256 + t * 512:256 + (t + 1) * 512, cs:cs + 512].rearrange("(m p) c -> p m c", p=128),
                    stg[gi][:], f"stg{gi}", reads=[b_stg[gi]])
            else:
                dma("sp", jb["Va"][0:256, cs:cs + 512].rearrange("(m p) c -> p m c", p=128),
                    stg[gi][:, 0:2, :], f"stg{gi}", reads=[b_stg[gi]])
                dma("sp", jb["Va"][256 + T:512 + T, cs:cs + 512].rearrange("(m p) c -> p m c", p=128),
                    stg[gi][:, 2:4, :], f"stg{gi}", reads=[b_stg[gi]])
        elif s in (6, 7):
            def ev(m, pi):
                return norm_rope_T(pi, 4, 0, m, gq_bc, stg[gi], b_stg[gi])
            tm_slab(si, hb, ev)
            store_fm(jb["QbT"][(s - 6) * 512:(s - 5) * 512, :], t * 512)
        else:
            def ev(cc, pi):
                act(stg[gi][:, cc, :], psum[pi][:], AF.Sigmoid, [b_ps[pi]], [b_stg[gi]])
            fm_slab(si, hb, ev)
            store_fm(jb["sgT"][(s - 9) * 512:(s - 8) * 512, :], t * 512)

    per_tile = []
    for jb in J:
        T = jb["T"]
        for kind, t in [("own", t) for t in range(T // 512)] + [("halo", 0)]:
            ctx = {}

            def prep(_slot, jb=jb, kind=kind, t=t, ctx=ctx):
                ctx["hb"] = cnt["hT"] % 2
                cnt["hT"] += 1
                if kind == "own":
                    xs = jb["x_own"][t * 512:(t + 1) * 512, :]
                    load_rope(jb["rq_c"][t * 512:(t + 1) * 512, :], jb["rq_s"][t * 512:(t + 1) * 512, :])
                else:
                    xs = jb["x_halo"]
                norm_transpose(xs, "g_mix_c", ctx["hb"], False)

            slabs = [0, 1, 2, 3, 4, 5, 6, 7, 9, 10, 11, 12, 13, 14, 15, 16] if kind == "own" else [2, 3, 4, 5]
            sj = [((lambda s=s: load_slab("w_in", 0, 16, s * 512)),
                   (lambda si, s=s, jb=jb, kind=kind, t=t, ctx=ctx: p1b_body(si, s, jb, kind, t, ctx["hb"]))) for s in slabs]
            per_tile.append(((None, prep), sj))
    jobs = []
    carry = []
    for (pj, sj) in per_tile:
        jobs.append(pj)
        jobs.extend(carry)
        k = max(0, len(sj) - 4)
        jobs.extend(sj[:k])
        carry = sj[k:]
    jobs.extend(carry)
    late_casts = []
    for k in ("w_pa", "w_pb", "w_o", "w_cq", "w_co", "w_up", "w_down"):
        for r0 in range(0, wshape[k][0], 256):
            late_casts.append((k, r0))
    nslabjobs = sum(1 for j in jobs if j[0] is not None)
    stride = max(1, (nslabjobs - 8) // len(late_casts))
    jobs2 = []
    seen = 0
    for j in jobs:
        jobs2.append(j)
        if j[0] is not None:
            seen += 1
            if seen % stride == 0 and late_casts:
                k, r0 = late_casts.pop(0)
                jobs2.append((None, (lambda _s, k=k, r0=r0: dma("pool", WB[k][r0:r0 + 256, :], W[k][r0:r0 + 256, :],
                                                                "cast_" + k, writes=[WBUF[k]]))))
    for (k, r0) in late_casts:
        jobs2.append((None, (lambda _s, k=k, r0=r0: dma("pool", WB[k][r0:r0 + 256, :], W[k][r0:r0 + 256, :],
                                                        "cast_" + k, writes=[WBUF[k]]))))
    run_pipe(jobs2)
    P.barrier()

    o2 = 0
    pT = []
    for i in range(8):
        v, o2 = carve(o2, [128, 2, 512], BF16)
        pT.append(v)
    tsum = []
    for i in range(6):
        v, o2 = carve(o2, [128, 1024], BF16)
        tsum.append(v)
    b_tsum = P.bufs(6, "tsum")
    cnt["ts"] = 0
    dacc = []
    for i in range(4):
        v, o2 = carve(o2, [128, 1024], F32)
        dacc.append(v)
    dtot, o2 = carve(o2, [128, 512], F32)
    drec, o2 = carve(o2, [128, 512], F32)
    ones16, o2 = carve(o2, [128, 128], BF16)
    dhi, o2 = carve(o2, [128, 512], BF16)
    dlo, o2 = carve(o2, [128, 512], BF16)
    b_dhl = P.buf("dhl")
    oT, o2 = carve(o2, [128, 2, 512], BF16)
    b_pT = P.bufs(8, "pT")
    b_dacc = P.bufs(4, "dacc")
    b_dtot, b_drec, b_ones = P.bufs(3, "den")
    b_oT = P.bufs(2, "oT")
    OFF_ATT = (o2 + 31) // 32 * 32
    cnt["pT"] = 0
    cnt["oT"] = 0
    cnt["stb"] = 0
    cnt["hd"] = 0
    STB = [[0, 1], [2, 3]]
    OACC = [4, 5]
    memset("dve", ones16[:], 1.0, [b_ones])

    def den_matmul(src_ap, src_b):
        cp("dve", dhi[:], src_ap, src_b, [b_dhl])
        tt("dve", dlo[:], src_ap, dhi[:], ALU.subtract, list(src_b) + [b_dhl], [b_dhl])
        mm(psum[6][:], ones16[:], dhi[:], True, False, [b_ones, b_dhl], [b_ps[6]])
        mm(psum[6][:], ones16[:], dlo[:], False, True, [b_ones, b_dhl], [b_ps[6]])

    def att_finish(par, dstT, row0, col0, used_pool, used_dve=True):
        src = par
        if used_pool and used_dve:
            tt("dve", dacc[par][:], dacc[par][:], dacc[2 + par][:], ALU.add, [b_dacc[par], b_dacc[2 + par]], [b_dacc[par]])
        elif used_pool:
            src = 2 + par
        tt("dve", dtot[:], dacc[src][:, 0:512], dacc[src][:, 512:1024], ALU.add, [b_dacc[src]], [b_dtot])
        den_matmul(dtot[:], [b_dtot])
        P.add("dve", lambda e: e.reciprocal(drec[:], psum[6][:]), [b_ps[6]], [b_drec])
        oi = cnt["oT"] % 2
        cnt["oT"] += 1
        tt("dve", oT[:, oi, :], psum[OACC[par]][:], drec[:], ALU.mult, [b_ps[OACC[par]], b_drec], [b_oT[oi]])
        dma("sp!", dstT[row0:row0 + 128, col0:col0 + 512], oT[:, oi, :], f"oT{oi}", reads=[b_oT[oi]])

    for jb in J:
        S, T = jb["S"], jb["T"]
        NCH = S // 128
        NP = NCH // 2
        o3 = OFF_ATT
        KTg, o3 = carve(o3, [128, S], BF16)
        Vg, o3 = carve(o3, [128, NCH, 128], BF16)
        qb, o3 = carve(o3, [128, 2, 512], BF16)
        b_KTg, b_Vg = P.bufs(2, "kvg")
        b_qb = P.bufs(2, "qb")
        for g in range(2):
            dma("sp", KTg[:], jb["KbT"][g * 128:(g + 1) * 128, :], "KTg", writes=[b_KTg])
            dma("sp", Vg[:], jb["Vb"][:, g * 128:(g + 1) * 128].rearrange("(c p) d -> p c d", p=128), "V1g",
                writes=[b_Vg])
            heads = [(qt, hh) for qt in range(T // 512) for hh in range(4)]
            steps = [(hi, p) for hi in range(len(heads)) for p in range(NP)]
            hstate = {}

            def stage_a(st_):
                hi, p = st_
                if p == 0:
                    par = cnt["hd"] % 2
                    cnt["hd"] += 1
                    qi = par
                    qt, hh = heads[hi]
                    h = g * 4 + hh
                    dma("sp", qb[:, qi, :], jb["QbT"][h * 128:(h + 1) * 128, qt * 512:(qt + 1) * 512], f"qb{qi}", writes=[b_qb[qi]])
                    hstate[hi] = dict(par=par, qi=qi, h=h, qt=qt, npool=0, ndve=0)
                hs = hstate[hi]
                sb_ = STB[cnt["stb"] % 2]
                cnt["stb"] += 1
                pi = cnt["pT"] % 8
                cnt["pT"] += 1
                for k in range(2):
                    ch = p * 2 + k
                    mm(psum[sb_[k]][:], KTg[:, ch * 128:(ch + 1) * 128], qb[:, hs["qi"], :], True, True,
                       [b_KTg, b_qb[hs["qi"]]], [b_ps[sb_[k]]])
                act(pT[pi][:].rearrange("p k s -> p (k s)"), psbig[sb_[0] // 2][:], AF.Exp,
                    [b_ps[sb_[0]], b_ps[sb_[1]]], [b_pT[pi]], scale=SCALE)
                return pi

            def stage_b(st_, pi):
                hi, p = st_
                hs = hstate[hi]
                par = hs["par"]
                for k in range(2):
                    ch = p * 2 + k
                    mm(psum[OACC[par]][:], Vg[:, ch, :], pT[pi][:, k, :], ch == 0, ch == NCH - 1,
                       [b_pT[pi], b_Vg], [b_ps[OACC[par]]])
                pflat = pT[pi][:].rearrange("p k s -> p (k s)")
                if p % 2 == 0:
                    hs["prev_pi"] = pi
                else:
                    ppi = hs["prev_pi"]
                    ti = cnt["ts"] % 6
                    cnt["ts"] += 1
                    tt("dve", tsum[ti][:], pT[ppi][:].rearrange("p k s -> p (k s)"), pflat, ALU.add,
                       [b_pT[ppi], b_pT[pi]], [b_tsum[ti]])
                    if p % 4 == 1:
                        hs["prev_ti"] = ti
                    else:
                        pti = hs["prev_ti"]
                        t2i = cnt["ts"] % 6
                        cnt["ts"] += 1
                        tt("dve", tsum[t2i][:], tsum[pti][:], tsum[ti][:], ALU.add, [b_tsum[pti], b_tsum[ti]], [b_tsum[t2i]])
                        if (p // 4) % 4 == 3:
                            e_, ai, key = "dve", par, "ndve"
                        else:
                            e_, ai, key = "pool", 2 + par, "npool"
                        if hs[key] == 0:
                            cp(e_, dacc[ai][:], tsum[t2i][:], [b_tsum[t2i]], [b_dacc[ai]])
                        else:
                            tt(e_, dacc[ai][:], dacc[ai][:], tsum[t2i][:], ALU.add, [b_tsum[t2i], b_dacc[ai]], [b_dacc[ai]])
                        hs[key] += 1
                if p == NP - 1:
                    att_finish(par, jb["ObT"], hs["h"] * 128, hs["qt"] * 512, hs["npool"] > 0, hs["ndve"] > 0)

            prev = None
            for st_ in steps:
                pi = stage_a(st_)
                if prev is not None:
                    stage_b(*prev)
                prev = (st_, pi)
            stage_b(*prev)
        P.barrier()

    o3 = OFF_ATT
    tabI, o3 = carve(o3, [128, 8, 16, 64], BF16)
    tabX, o3 = carve(o3, [128, 4, 8 * 16 * 64], BF16)
    qa, o3 = carve(o3, [128, 8, 512], BF16)
    ka, o3 = carve(o3, [128, 8, 1024], BF16)
    Va_, o3 = carve(o3, [128, 8, 1024], BF16)
    b_tabI, b_tabX, b_qa, b_ka, b_Va = P.bufs(5, "na")
    dma("sp", tabI[:].rearrange("p h s q -> p (h s q)"), tabs_d[0], "tabI", writes=[b_tabI])
    tabXv = tabX.rearrange("p v (h s q) -> p v h s q", h=8, s=16)
    for ji, jb in enumerate(J):
        T = jb["T"]
        nblk = T // 512
        for v in range(4):
            cls, var = v // 2, v % 2
            dma("sp", tabX[:, v, :], tabs_d[2 + (ji * 2 + cls) * 2 + var], "tabX", writes=[b_tabX])
        for b in range(nblk):
            segs_all = na_segments(b == 0, b == nblk - 1)
            dma("sp", qa[:], jb["QaT"][:, b * 512:(b + 1) * 512].rearrange("(h p) s -> p h s", p=128), "qa", writes=[b_qa])
            dma("sp", ka[:], jb["KaT"][:, b * 512:b * 512 + 1024].rearrange("(h p) s -> p h s", p=128), "ka", writes=[b_ka])
            dma("sp", Va_[:], jb["Va"][b * 512:b * 512 + 1024, :].rearrange("(c p) x -> p c x", p=128), "V1a", writes=[b_Va])
            steps = [(h, j) for h in range(8) for j in range(8)]
            hstate = {}

            def na_a(st_):
                h, j = st_
                if j == 0:
                    par = cnt["hd"] % 2
                    cnt["hd"] += 1
                    hstate[h] = dict(par=par)
                    memset("pool", dacc[par][:, 0:512], 0.0, [b_dacc[par]])
                m_lo, m_hi, segs = segs_all[j]
                c0, c1 = m_lo * 128, (m_hi + 1) * 128
                sbk = STB[cnt["stb"] % 2][cnt["stb"] // 2 % 2]
                cnt["stb"] += 1
                pi = cnt["pT"] % 8
                cnt["pT"] += 1
                mm(psum[sbk][:, c0:c1], ka[:, h, j * 128:(j + 1) * 128], qa[:, h, c0:c1], True, True,
                   [b_ka, b_qa], [b_ps[sbk]])
                act(pT[pi][:, 0, c0:c1], psum[sbk][:, c0:c1], AF.Exp, [b_ps[sbk]], [b_pT[pi]], scale=SCALE)
                eng_ = "pool" if (j % 2) else "dve"
                for (ra, rb, tb, s0) in segs:
                    n = rb - ra
                    if tb == "int":
                        tv = tabI[:, h, s0:s0 + n, :]
                        tbuf = b_tabI
                    else:
                        vi = {"first0": 0, "first1": 1, "last0": 2, "last1": 3}[tb]
                        tv = tabXv[:, vi, h, s0:s0 + n, :]
                        tbuf = b_tabX
                    pv_ = pT[pi][:, 0, ra * 64:rb * 64].rearrange("p (r q) -> p r q", q=64)
                    tt(eng_, pv_, pv_, tv, ALU.mult, [b_pT[pi], tbuf], [b_pT[pi]])
                return pi

            def na_b(st_, pi):
                h, j = st_
                par = hstate[h]["par"]
                m_lo, m_hi, segs = segs_all[j]
                c0, c1 = m_lo * 128, (m_hi + 1) * 128
                mm(psum[OACC[par]][:, c0:c1], Va_[:, j, h * 128:(h + 1) * 128], pT[pi][:, 0, c0:c1], j == 0, j == 7,
                   [b_pT[pi], b_Va], [b_ps[OACC[par]]], skip=True)
                tt("dve" if (j % 2) else "pool", dacc[par][:, c0:c1], dacc[par][:, c0:c1], pT[pi][:, 0, c0:c1], ALU.add,
                   [b_pT[pi], b_dacc[par]], [b_dacc[par]])
                if j == 7:
                    den_matmul(dacc[par][:, 0:512], [b_dacc[par]])
                    P.add("dve", lambda e: e.reciprocal(drec[:], psum[6][:]), [b_ps[6]], [b_drec])
                    oi = cnt["oT"] % 2
                    cnt["oT"] += 1
                    tt("dve", oT[:, oi, :], psum[OACC[par]][:], drec[:], ALU.mult, [b_ps[OACC[par]], b_drec], [b_oT[oi]])
                    dma("sp!", jb["OaT"][h * 128:(h + 1) * 128, b * 512:(b + 1) * 512], oT[:, oi, :], f"oT{oi}", reads=[b_oT[oi]])

            pend = []
            for st_ in steps:
                pi = na_a(st_)
                pend.append((st_, pi))
                if len(pend) > 6:
                    na_b(*pend.pop(0))
            while pend:
                na_b(*pend.pop(0))
    P.barrier()

    o2 = OFF_COMMON
    aT = []
    o_a = o2
    for i in range(2):
        v, o2 = carve(o2, [128, 16, 512], BF16)
        aT.append(v)
    OabT, o_a = carve(o_a, [128, 16, 512], BF16)
    sgs = []
    for i in range(2):
        v, o_a = carve(o_a, [128, 8, 512], BF16)
        sgs.append(v)
    tmpA, o2 = carve(o2, [128, 512], F32)
    tmpB, o2 = carve(o2, [128, 512], F32)
    qcT, o2 = carve(o2, [128, 4, 512], BF16)
    ocs, o2 = carve(o2, [128, 4, 512], BF16)
    ocT, o2 = carve(o2, [128, 4, 512], BF16)
    pTc, o2 = carve(o2, [128, 2, 512], BF16)
    gfin, o2 = carve(o2, [128, D], F32)
    b_Oab = P.buf("Oab")
    b_sgs = P.bufs(2, "sgs")
    b_aT = [[b_Oab], [b_sgs[0], b_sgs[1]]]
    b_tmpA, b_tmpB, b_qcT, b_ocs, b_ocT, b_gfin = P.bufs(6, "p3")
    b_pTc = P.bufs(2, "pTc")
    dma("sp", gfin[:], gfin_d, "gfin", writes=[b_gfin])

    def resid_add(m, pi, n):
        tt("dve", xres[:, m, n * 512:(n + 1) * 512], xres[:, m, n * 512:(n + 1) * 512], psum[pi][:], ALU.add,
           [b_ps[pi], b_x[m]], [b_x[m]])

    def p3_tile_jobs(ji, jb, t):
        tsl = slice(t * 512, (t + 1) * 512)
        ctx = {}
        jobs = []

        def xload(_):
            for m in range(4):
                dma("sp", xres[:, m, :], jb["x_own"][t * 512 + m * 128:t * 512 + (m + 1) * 128, :], f"x{m}", writes=[b_x[m]])

        def start(_):
            dma("sp", OabT[:, 0:8, :], jb["OaT"][:, tsl].rearrange("(c p) s -> p c s", p=128), "Oab", writes=[b_Oab])
            dma("sp", OabT[:, 8:16, :], jb["ObT"][:, tsl].rearrange("(c p) s -> p c s", p=128), "Oab", writes=[b_Oab])
            ctx["mixb"] = cnt["hT"] % 2
            cnt["hT"] += 1
        jobs.append((None, start))

        def ld_merge(n):
            si = next_slab()
            dma("sp", slab[si][:, 0:8, :], WB["w_pa"][:, n * 512:(n + 1) * 512].rearrange("(c p) n -> p c n", p=128),
                f"slab{si}", reads=[WBUF["w_pa"]], writes=[b_slab[si]])
            dma("sp", slab[si][:, 8:16, :], WB["w_pb"][:, n * 512:(n + 1) * 512].rearrange("(c p) n -> p c n", p=128),
                f"slab{si}", reads=[WBUF["w_pb"]], writes=[b_slab[si]])
            return si

        def merge(si, n):
            mixb = ctx["mixb"]
            gi = n % 2
            dma("sp", sgs[gi][:, 0:4, :], jb["sgT"][n * 512:(n + 1) * 512, tsl].rearrange("(c p) s -> p c s", p=128),
                f"sgs{gi}", writes=[b_sgs[gi]])
            dma("sp", sgs[gi][:, 4:8, :], jb["sgT"][2048 + n * 512:2048 + (n + 1) * 512, tsl].rearrange("(c p) s -> p c s", p=128),
                f"sgs{gi}", writes=[b_sgs[gi]])
            for cc in range(4):
                pa = next_ps()
                for c in range(8):
                    mm(psum[pa][:], slab[si][:, c, cc * 128:(cc + 1) * 128], OabT[:, c, :], c == 0, c == 7,
                       [b_slab[si], b_Oab], [b_ps[pa]])
                pbk = next_ps()
                for c in range(8):
                    mm(psum[pbk][:], slab[si][:, 8 + c, cc * 128:(cc + 1) * 128], OabT[:, 8 + c, :], c == 0, c == 7,
                       [b_slab[si], b_Oab], [b_ps[pbk]])
                tt("dve", tmpA[:], psum[pa][:], sgs[gi][:, cc, :], ALU.mult, [b_ps[pa], b_sgs[gi]], [b_tmpA])
                tt("dve", tmpB[:], psum[pbk][:], sgs[gi][:, 4 + cc, :], ALU.mult, [b_ps[pbk], b_sgs[gi]], [b_tmpB])
                tt("pool", hT[mixb][:, n * 4 + cc, :], tmpA[:], tmpB[:], ALU.add, [b_tmpA, b_tmpB], [b_hT[mixb]])
        for n in range(4):
            jobs.append(((lambda n=n: ld_merge(n)), (lambda si, n=n: merge(si, n))))
        jobs.insert(3, (None, xload))
        for n in range(4):
            jobs.append(((lambda n=n: load_slab("w_o", 0, 16, n * 512)),
                         (lambda si, n=n: tm_slab(si, ctx["mixb"], lambda m, pi: resid_add(m, pi, n)))))

        def cq(si):
            hb = cnt["hT"] % 2
            cnt["hT"] += 1
            norm_transpose(None, "g_cross_c", hb, True)

            def ev_q(cc, pi):
                cp("act", qcT[:, cc, :], psum[pi][:], [b_ps[pi]], [b_qcT])
            fm_slab(si, hb, ev_q)
            for h in range(4):
                for k in range(2):
                    pi = next_ps()
                    mm(psum[pi][:], KcT[ji][:, h, k * 128:(k + 1) * 128], qcT[:, h, :], True, True, [b_kc[ji], b_qcT], [b_ps[pi]])
                    act(pTc[:, k, :], psum[pi][:], AF.Exp, [b_ps[pi]], [b_pTc[0]], scale=SCALE)
                for m in range(4):
                    pi = 6 + m % 2
                    for k in range(2):
                        mm(psum[pi][:, 0:129], pTc[:, k, m * 128:(m + 1) * 128], Vc1[ji][:, k, h, 0:129], k == 0, k == 1,
                           [b_pTc[0], b_kc[ji]], [b_ps[pi]])
                    P.add("dve", lambda e, m=m, pi=pi: e.reciprocal(rden[:, m:m + 1], psum[pi][:, 128:129]), [b_ps[pi]], [b_rden])
                    ts("dve", ocs[:, m, h * 128:(h + 1) * 128], psum[pi][:, 0:128], rden[:, m:m + 1], None, ALU.mult, None,
                       [b_ps[pi], b_rden], [b_ocs])
            for m in range(4):
                pb = psb(6 + m % 2)
                for h in range(4):
                    tr(pb[:, h * 128:(h + 1) * 128], ocs[:, m, h * 128:(h + 1) * 128], [b_ocs], [b_ps[6 + m % 2]])
                cp("dve", ocT[:, :, m * 128:(m + 1) * 128], pb[:, 0:512].rearrange("p (h t) -> p h t", t=128),
                   [b_ps[6 + m % 2]], [b_ocT])
        jobs.append(((lambda: load_slab("w_cq", 0, 16, 0)), cq))
        for n in range(4):
            jobs.append(((lambda n=n: load_slab("w_co", 0, 4, n * 512)),
                         (lambda si, n=n: tm_slab(si, None, lambda m, pi: resid_add(m, pi, n), nk=4, src=ocT, src_b=b_ocT))))

        def mlp_norm(_):
            ctx["hb"] = cnt["hT"] % 2
            cnt["hT"] += 1
            norm_transpose(None, "g_mlp_c", ctx["hb"], True)
        jobs.append((None, mlp_norm))

        def up(si, n, ab):
            def ev_up(cc, pi):
                act(tmpA[:], psum[pi][:], AF.Relu, [b_ps[pi]], [b_tmpA])
                tt("dve", aT[ab][:, n * 4 + cc, :], tmpA[:], tmpA[:], ALU.mult, [b_tmpA], b_aT[ab])
            fm_slab(si, ctx["hb"], ev_up)
        for kg in range(4):
            ab = kg % 2
            for n in range(4):
                jobs.append(((lambda kg=kg, n=n: load_slab("w_up", 0, 16, (kg * 4 + n) * 512)),
                             (lambda si, n=n, ab=ab: up(si, n, ab))))
            for n in range(4):
                jobs.append(((lambda kg=kg, n=n: load_slab("w_down", kg * 16, 16, n * 512)),
                             (lambda si, n=n, ab=ab: tm_slab(si, None, lambda m, pi: resid_add(m, pi, n), src=aT[ab], src_b=b_aT[ab]))))

        def fin(_):
            for m in range(4):
                sbf = b_stat[m % 4]
                ss = stat[:, m:m + 1]
                xn, b_xn = xnb[m % 2], b_xnb[m % 2]
                act(xn[:], xres[:, m, :], AF.Square, [b_x[m]], [b_xn, sbf], accum=ss)
                rstd_from_ss(ss, ss, 1, D, sbf)
                stt("dve", xres[:, m, :], xres[:, m, :], ss, gfin[:], ALU.mult, ALU.mult, [b_x[m], sbf, b_gfin], [b_x[m]])
                dma("sp", jb["y"][t * 512 + m * 128:t * 512 + (m + 1) * 128, :], xres[:, m, :], f"xo{m}", reads=[b_x[m]])
        jobs.append((None, fin))
        return jobs

    jobs = []
    for ji, jb in enumerate(J):
        for t in range(jb["T"] // 512):
            jobs.extend(p3_tile_jobs(ji, jb, t))
    run_pipe(jobs)

    P.emit()
    st.close()
    return nc, P


def _rope_tables(S):
    t = np.arange(S)
    row = (t // 64).astype(np.float32)
    col = (t % 64).astype(np.float32)
    inv = (10000.0 ** (-np.arange(32, dtype=np.float32) / 32)).astype(np.float32)
    ar = row[:, None] * inv[None, :]
    ac = col[:, None] * inv[None, :]
    C = np.concatenate([np.cos(ar), np.cos(ar), np.cos(ac), np.cos(ac)], axis=1).astype(np.float32)
    Sn = np.concatenate([-np.sin(ar), np.sin(ar), -np.sin(ac), np.sin(ac)], axis=1).astype(np.float32)
    return C, Sn


def _na_consts(rpb):
    rp = np.asarray(rpb, np.float32)[0]
    e = np.arange(128) // 64
    kc = np.arange(128) % 64
    s = np.arange(16)
    qc = np.arange(64)
    ap = s[None, :] - e[:, None]
    a = 14 - ap
    slot_ok = (ap >= 0) & (ap <= 14)
    dc = kc[:, None] - qc[None, :]
    cs = np.clip(qc - 8, 0, 48)
    col_ok = (kc[:, None] >= cs[None, :]) & (kc[:, None] < cs[None, :] + 16)
    bi = np.clip(dc + 15, 0, 30)
    A = np.clip(a, 0, 14)
    g = rp[:, A[:, :, None], bi[:, None, :]]
    ok = slot_ok[:, :, None] & col_ok[:, None, :]
    g = np.where(ok[None], g, 0.0).astype(np.float32)
    rpb_exp = np.ascontiguousarray(g.transpose(1, 0, 2, 3)).reshape(128, 8 * 16 * 64)
    mfull = ok.astype(np.float32).reshape(128, 1024)
    mint = (ok & ((ap >= 4) & (ap <= 11))[:, :, None]).astype(np.float32).reshape(128, 1024)
    return rpb_exp, mfull, mint


def _gcol(g):
    return np.ascontiguousarray(np.asarray(g, np.float32).reshape(16, 128).T)


_CACHE = {}


def _run(inputs, S_p, S_s, debug=False):
    cfg = Cfg(S_p, S_s)
    key = (S_p, S_s, debug)
    if key not in _CACHE:
        _CACHE[key] = build(cfg, debug)
    nc, P = _CACHE[key]
    f = lambda k: np.asarray(inputs[k], np.float32)
    xp, xs, mp, ms = f("x_prompt"), f("x_sample"), f("mem_prompt"), f("mem_sample")
    shared = {k: np.ascontiguousarray(f(k)[0]) for k in ("w_in", "w_pa", "w_pb", "w_o", "w_cq", "w_ckv", "w_co", "w_up", "w_down")}
    shared["g_mix_c"] = _gcol(f("g_mix")[0])
    shared["g_cross_c"] = _gcol(f("g_cross")[0])
    shared["g_mem_c"] = _gcol(f("g_mem")[0])
    shared["g_mlp_c"] = _gcol(f("g_mlp")[0])
    shared["gq_bc"] = np.ascontiguousarray(np.broadcast_to(f("g_q")[0][None, :], (128, 128)))
    shared["gk_bc"] = np.ascontiguousarray(np.broadcast_to(f("g_k")[0][None, :], (128, 128)))
    shared["gfin_bc"] = np.ascontiguousarray(np.broadcast_to(f("g_final")[None, :], (128, D)))
    shared["ident"] = np.eye(128, dtype=np.float32)
    shared["rpb_exp"], shared["mask_full"], shared["mask_int"] = _na_consts(f("rpb"))
    Cp, Sp = _rope_tables(S_p)
    Cs, Ss = _rope_tables(S_s)
    Tp, Ts = S_p // 2, S_s // 8
    zeros256 = np.zeros((256, D), np.float32)
    in_maps = []
    for c in range(8):
        b, hf = c // 2, c % 2
        m = dict(shared)
        o0 = hf * Tp
        m["xp_seq"] = np.ascontiguousarray(xp[b])
        m["xp_own"] = np.ascontiguousarray(xp[b, o0:o0 + Tp])
        pre = xp[b, o0 - 256:o0] if o0 > 0 else zeros256
        post = xp[b, o0 + Tp:o0 + Tp + 256] if o0 + Tp < S_p else zeros256
        m["xp_halo"] = np.ascontiguousarray(np.concatenate([pre, post], 0))
        m["memp"] = np.ascontiguousarray(mp[b])
        m["rkp_c"], m["rkp_s"] = Cp, Sp
        m["rqp_c"], m["rqp_s"] = np.ascontiguousarray(Cp[o0:o0 + Tp]), np.ascontiguousarray(Sp[o0:o0 + Tp])
        s0 = c * Ts
        m["xs_seq"] = np.ascontiguousarray(xs[0])
        m["xs_own"] = np.ascontiguousarray(xs[0, s0:s0 + Ts])
        pre = xs[0, s0 - 256:s0] if s0 > 0 else zeros256
        post = xs[0, s0 + Ts:s0 + Ts + 256] if s0 + Ts < S_s else zeros256
        m["xs_halo"] = np.ascontiguousarray(np.concatenate([pre, post], 0))
        m["mems"] = np.ascontiguousarray(ms[0])
        m["rks_c"], m["rks_s"] = Cs, Ss
        m["rqs_c"], m["rqs_s"] = np.ascontiguousarray(Cs[s0:s0 + Ts]), np.ascontiguousarray(Ss[s0:s0 + Ts])
        selv = np.array([hf == 0, hf == 1, c == 0, c == 7], np.float32)
        m["sel"] = np.ascontiguousarray(np.broadcast_to(selv[None, :], (128, 4)))
        in_maps.append(m)
    res = run_bass_kernel_spmd(nc, in_maps, core_ids=list(range(8)))
    yp = np.zeros((4, S_p, D), np.float32)
    ys = np.zeros((1, S_s, D), np.float32)
    for c in range(8):
        b, hf = c // 2, c % 2
        yp[b, hf * Tp:(hf + 1) * Tp] = res.results[c]["yp"]
        ys[0, c * Ts:(c + 1) * Ts] = res.results[c]["ys"]
    return (yp, ys), res


def kernel(**inputs):
    (yp, ys), _ = _run(inputs, 8192, 16384)
    return (yp, ys)
```

```python
import numpy as np
from contextlib import ExitStack
import concourse.bass as bass
import concourse.mybir as mybir
from concourse.bass_utils import run_bass_kernel_spmd

F32 = mybir.dt.float32
BF16 = mybir.dt.bfloat16
U8 = mybir.dt.uint8
ALU = mybir.AluOpType
AF = mybir.ActivationFunctionType

D = 2048
DIN = 8704
DFF = 8192
NMEM = 256
EPS = 1e-6
SCALE = 128 ** -0.5


class Buf:
    __slots__ = ("name", "writers", "readers")

    def __init__(self, name):
        self.name = name
        self.writers = {}
        self.readers = {}


class Op:
    __slots__ = ("eng", "fn", "waits", "signal", "sigval", "idx", "dma")

    def __init__(self, eng, fn):
        self.eng = eng
        self.fn = fn
        self.waits = []
        self.signal = False
        self.sigval = None
        self.idx = None
        self.dma = None


ENGS = ("pe", "act", "dve", "pool", "sp")
STRICT = ("act", "dve", "pool")


class Prog:
    def __init__(self, nc):
        self.nc = nc
        self.ops = {e: [] for e in ENGS}
        self.waited = {e: {} for e in ENGS}
        self.dma_count = {}

    def buf(self, name="b"):
        return Buf(name)

    def bufs(self, n, name="b"):
        return [Buf(f"{name}{i}") for i in range(n)]

    def add(self, eng, fn, reads=(), writes=(), dma=None):
        mykey = dma if dma is not None else eng
        deps = []
        for b in reads:
            for k, t in b.writers.items():
                if k != mykey or (dma is None and eng in STRICT):
                    deps.append(t)
        for b in writes:
            for k, t in b.writers.items():
                if k != mykey:
                    deps.append(t)
            for k, t in b.readers.items():
                if k != mykey:
                    deps.append(t)
        op = Op(eng, fn)
        w = self.waited[eng]
        for t in deps:
            if t[0] == "c":
                p = t[1]
                if w.get(p.eng, -1) >= p.idx:
                    continue
                w[p.eng] = p.idx
                p.signal = True
                op.waits.append(t)
            else:
                sk = t[1]
                v = self.dma_count[sk]
                if w.get(sk, 0) >= v:
                    continue
                w[sk] = v
                op.waits.append(("d", sk, v))
        op.idx = len(self.ops[eng])
        self.ops[eng].append(op)
        if dma is not None:
            self.dma_count[dma] = self.dma_count.get(dma, 0) + 16
            op.dma = (dma, self.dma_count[dma])
            tok = ("d", dma, self.dma_count[dma])
        else:
            tok = ("c", op)
        for b in reads:
            b.readers[mykey] = tok
        for b in writes:
            b.writers = {mykey: tok}
            b.readers = {}
        return op

    def barrier(self):
        lasts = {}
        for e in ("pe", "act", "dve", "pool"):
            cl = [o for o in self.ops[e] if o.dma is None and o.fn is not None]
            if cl:
                lasts[e] = cl[-1]
                cl[-1].signal = True
        for f in ENGS:
            op = Op(f, None)
            w = self.waited[f]
            for e, l in lasts.items():
                if e != f and w.get(e, -1) < l.idx:
                    w[e] = l.idx
                    op.waits.append(("c", l))
            for sk, v in self.dma_count.items():
                if w.get(sk, 0) < v:
                    w[sk] = v
                    op.waits.append(("d", sk, v))
            op.idx = len(self.ops[f])
            self.ops[f].append(op)

    def emit(self):
        nc = self.nc
        self.barrier()
        for e in ENGS:
            c = 0
            for op in self.ops[e]:
                if op.signal and op.dma is None:
                    c += 1
                    op.sigval = c
        with ExitStack() as st:
            esem = {e: st.enter_context(nc.semaphore(f"s_{e}")) for e in ENGS}
            dsem = {}
            for i, sk in enumerate(self.dma_count):
                dsem[sk] = st.enter_context(nc.semaphore(f"d_{i}"))
            block = st.enter_context(nc.Block())

            def run(e, eng):
                for op in self.ops[e]:
                    for t in op.waits:
                        if t[0] == "c":
                            eng.wait_ge(esem[t[1].eng], t[1].sigval)
                        else:
                            eng.wait_ge(dsem[t[1]], t[2])
                    if op.fn is None:
                        continue
                    ins = op.fn(eng)
                    if op.dma is not None:
                        ins.then_inc(dsem[op.dma[0]], 16)
                    elif op.signal:
                        ins.then_inc(esem[e], 1)

            block.tensor(lambda eng: run("pe", eng))
            block.scalar(lambda eng: run("act", eng))
            block.vector(lambda eng: run("dve", eng))
            block.gpsimd(lambda eng: run("pool", eng))
            block.sync(lambda eng: run("sp", eng))
        self.stats = {e: len(self.ops[e]) for e in ENGS}


def na_segments(first, last):
    out = []
    for j in range(8):
        m_lo, m_hi = max(0, j - 4), min(3, j)
        if 2 <= j <= 5:
            if first:
                m_lo = 0
            if last:
                m_hi = 3
        r_lo, r_hi = 2 * m_lo, 2 * m_hi + 2
        segs = []
        r = r_lo
        while r < r_hi:
            if first and r < 4:
                e_ = min(4, r_hi)
                tb = "first1" if 2 <= j <= 5 else "first0"
            elif last and r >= 5:
                e_ = r_hi
                tb = "last1" if 2 <= j <= 5 else "last0"
            else:
                e_ = r_hi
                if last:
                    e_ = min(e_, 5)
                tb = "int"
            segs.append((r, e_, tb, 11 - 2 * j + r))
            r = e_
        for (a, b, tb, s0) in segs:
            assert 0 <= s0 and s0 + (b - a) <= 16, (j, segs)
        out.append((m_lo, m_hi, segs))
    return out


class Cfg:
    def __init__(self, S_p, S_s):
        self.jobs = [dict(n="p", S=S_p, T=S_p // 2), dict(n="s", S=S_s, T=S_s // 8)]


def build(cfg, debug=False):
    nc = bass.Bass("TRN2", target_bir_lowering=False)
    P = Prog(nc)
    st = ExitStack()
    dkind = "ExternalOutput" if debug else "Internal"

    def din(name, shape, dt=F32):
        return nc.dram_tensor(name, list(shape), dt, kind="ExternalInput").ap()

    def dscr(name, shape, dt=BF16):
        return nc.dram_tensor(name, list(shape), dt, kind=dkind).ap()

    W = {}
    wshape = dict(w_in=(D, DIN), w_pa=(1024, D), w_pb=(1024, D), w_o=(D, D), w_cq=(D, 512),
                  w_ckv=(D, 1024), w_co=(512, D), w_up=(D, DFF), w_down=(DFF, D))
    WB = {}
    WBUF = {}
    for k, s in wshape.items():
        W[k] = din(k, s)
        WB[k] = nc.dram_tensor("b_" + k, list(s), BF16, kind="Internal").ap()
        WBUF[k] = P.buf("wb_" + k)
    gcols = {k: din(k, (128, 16)) for k in ("g_mix_c", "g_cross_c", "g_mem_c", "g_mlp_c")}
    gq_bc_d = din("gq_bc", (128, 128))
    gk_bc_d = din("gk_bc", (128, 128))
    gfin_d = din("gfin_bc", (128, D))
    ident_d = din("ident", (128, 128))
    sel_d = din("sel", (128, 4))
    rpbx_d = din("rpb_exp", (128, 8 * 16 * 64))
    mfull_d = din("mask_full", (128, 16 * 64))
    mint_d = din("mask_int", (128, 16 * 64))
    tabs_d = dscr("tabs", (10, 128, 8192))
    J = cfg.jobs
    for jb in J:
        n, S, T = jb["n"], jb["S"], jb["T"]
        jb["x_seq"] = din(f"x{n}_seq", (S, D))
        jb["x_own"] = din(f"x{n}_own", (T, D))
        jb["x_halo"] = din(f"x{n}_halo", (512, D))
        jb["mem"] = din(f"mem{n}", (NMEM, D))
        jb["rk_c"] = din(f"rk{n}_c", (S, 128))
        jb["rk_s"] = din(f"rk{n}_s", (S, 128))
        jb["rq_c"] = din(f"rq{n}_c", (T, 128))
        jb["rq_s"] = din(f"rq{n}_s", (T, 128))
        jb["y"] = nc.dram_tensor(f"y{n}", [T, D], F32, kind="ExternalOutput").ap()
        jb["KbT"] = dscr(f"KbT{n}", (256, S))
        jb["Vb"] = dscr(f"Vb{n}", (S, 256))
        jb["QaT"] = dscr(f"QaT{n}", (1024, T))
        jb["KaT"] = dscr(f"KaT{n}", (1024, T + 512))
        jb["Va"] = dscr(f"Va{n}", (T + 512, 1024))
        jb["QbT"] = dscr(f"QbT{n}", (1024, T))
        jb["sgT"] = dscr(f"sgT{n}", (4096, T))
        jb["OaT"] = dscr(f"OaT{n}", (1024, T))
        jb["ObT"] = dscr(f"ObT{n}", (1024, T))

    def sb(name, shape, dt):
        return st.enter_context(nc.sbuf_tensor("s_" + name, list(shape), dt))

    ARENA = 186 * 1024
    arena = sb("arena", [128, ARENA], U8)

    def carve(off, shape, dt):
        esz = 4 if dt == F32 else 2
        n = int(np.prod(shape[1:]))
        assert off % 32 == 0 and off + n * esz <= ARENA, (off, shape)
        v = arena[:, off:off + n * esz].bitcast(dt)
        if len(shape) == 3:
            v = v.rearrange("p (a b) -> p a b", b=shape[2])
        elif len(shape) == 4:
            v = v.rearrange("p (a b c) -> p a b c", b=shape[2], c=shape[3])
        return v, off + n * esz

    ident = sb("ident", [128, 128], BF16)
    gcol = {k: sb(k, [128, 16], F32) for k in gcols}
    gq_bc = sb("gq_bc", [128, 128], F32)
    gk_bc = sb("gk_bc", [128, 128], F32)
    sel = sb("sel", [128, 4], F32)
    nsel = sb("nsel", [128, 4], F32)
    stat = sb("stat", [128, 64], F32)
    epsb = sb("epsb", [128, 1], F32)
    rden = sb("rden", [128, 8], F32)
    b_rden = P.buf("rden")
    KcT = [sb(f"KcT{i}", [128, 4, NMEM], BF16) for i in range(2)]
    Vc1 = [sb(f"Vc1{i}", [128, 2, 4, 136], BF16) for i in range(2)]
    b_const = P.buf("const")
    b_stat = P.bufs(8, "stat")
    b_kc = P.bufs(2, "kc")

    psbig = [st.enter_context(nc.psum_tensor(f"ps{i}", [128, 1024], F32)) for i in range(4)]
    psum = [psbig[i // 2][:, (i % 2) * 512:(i % 2 + 1) * 512] for i in range(8)]
    b_ps = P.bufs(8, "ps")

    def psb(i):
        return psum[i].bitcast(BF16)

    def mm(out, lhsT, rhs, start, stop, reads, writes, skip=False):
        if skip:
            P.add("pe", lambda e: e.matmul(out, lhsT, rhs, start=start, stop=stop, skip_group_check=True), reads, writes)
        else:
            P.add("pe", lambda e: e.matmul(out, lhsT, rhs, start=start, stop=stop), reads, writes)

    def tr(out, in_, reads, writes):
        P.add("pe", lambda e: e.transpose(out, in_, ident[:]), list(reads) + [b_const], writes)

    def act(out, in_, func, reads, writes, bias=0.0, scale=1.0, accum=None):
        if accum is None:
            P.add("act", lambda e: e.activation(out, in_, func, bias=bias, scale=scale), reads, writes)
        else:
            P.add("act", lambda e: e.activation(out, in_, func, bias=bias, scale=scale, accum_out=accum), reads, writes)

    def ts(eng, out, in0, s1, s2, op0, op1, reads, writes):
        if op1 is None:
            P.add(eng, lambda e: e.tensor_scalar(out, in0, s1, s2, op0), reads, writes)
        else:
            P.add(eng, lambda e: e.tensor_scalar(out, in0, s1, s2, op0, op1), reads, writes)

    def tt(eng, out, in0, in1, op, reads, writes):
        P.add(eng, lambda e: e.tensor_tensor(out, in0, in1, op), reads, writes)

    def stt(eng, out, in0, scalar, in1, op0, op1, reads, writes):
        P.add(eng, lambda e: e.scalar_tensor_tensor(out, in0, scalar, in1, op0, op1), reads, writes)

    def cp(eng, out, in_, reads, writes):
        if eng == "act":
            P.add("act", lambda e: e.copy(out, in_), reads, writes)
        else:
            P.add(eng, lambda e: e.tensor_copy(out, in_), reads, writes)

    def memset(eng, ap, val, writes):
        P.add(eng, lambda e: e.memset(ap, val), (), writes)

    def dma(q, out, in_, key, reads=(), writes=()):
        if q == "sp!":
            q = "sp"
        elif q == "sp" and len(writes) == 0:
            q = "pool"
        P.add(q, lambda e: e.dma_start(out=out, in_=in_), reads, writes, dma=key)

    def run_pipe(jobs, ahead=2):
        L = [i for i, j in enumerate(jobs) if j[0] is not None]
        slots = {}
        nl = 0
        seen = 0
        for i, (lf, cf) in enumerate(jobs):
            if lf is not None:
                seen += 1
            while nl < len(L) and nl < seen + ahead:
                slots[L[nl]] = jobs[L[nl]][0]()
                nl += 1
            cf(slots.get(i))

    dma("pool", ident[:], ident_d, "c_id", writes=[b_const])
    for k in gcols:
        dma("sp", gcol[k][:], gcols[k], "c_" + k, writes=[b_const])
    dma("sp", gq_bc[:], gq_bc_d, "c_gq", writes=[b_const])
    dma("sp", gk_bc[:], gk_bc_d, "c_gk", writes=[b_const])
    dma("sp", sel[:], sel_d, "c_sel", writes=[b_const])
    memset("dve", epsb[:], EPS, [b_const])
    ts("dve", nsel[:], sel[:], -1.0, 1.0, ALU.mult, ALU.add, [b_const], [b_stat[7]])
    def cast_weights(names):
        for k in names:
            rows = wshape[k][0]
            step = 256
            for r0 in range(0, rows, step):
                dma("pool", WB[k][r0:r0 + step, :], W[k][r0:r0 + step, :], "cast_" + k, writes=[WBUF[k]])
    WB["w_kv"] = nc.dram_tensor("b_w_kv", [D, 512], BF16, kind="Internal").ap()
    WBUF["w_kv"] = P.buf("wb_w_kv")
    dma("pool", WB["w_kv"], W["w_in"][:, 4096:4608], "cast_w_kv", writes=[WBUF["w_kv"]])
    cast_weights(["w_ckv"])

    off = 0
    slab = []
    for i in range(3):
        v, off = carve(off, [128, 16, 512], BF16)
        slab.append(v)
    hT = []
    for i in range(2):
        v, off = carve(off, [128, 16, 512], BF16)
        hT.append(v)
    xres, off = carve(off, [128, 4, D], F32)
    xnb = []
    for i in range(2):
        v, off = carve(off, [128, D], BF16)
        xnb.append(v)
    OFF_COMMON = off
    stg = []
    for i in range(3):
        v, off = carve(off, [128, 4, 512], BF16)
        stg.append(v)
    ropeC, off = carve(off, [128, 4, 128], F32)
    ropeS, off = carve(off, [128, 4, 128], F32)
    kn, off = carve(off, [128, 4, 128], F32)
    t1, off = carve(off, [128, 4, 128], F32)
    t2, off = carve(off, [128, 4, 128], F32)
    krb = []
    for i in range(2):
        v, off = carve(off, [128, 4, 128], BF16)
        krb.append(v)
    OFF_P13 = off
    b_slab = P.bufs(3, "slab")
    b_hT = P.bufs(2, "hT")
    b_x = P.bufs(4, "x")
    b_xnb = P.bufs(2, "xn")
    b_stg = P.bufs(3, "stg")
    b_rope = P.buf("rope")
    b_kn, b_t1, b_t2 = P.bufs(3, "ropetmp")
    b_krb = P.bufs(2, "krb")
    cnt = dict(slab=0, stg=0, hT=0, ps=0, st=0, xn=0)

    def next_slab():
        i = cnt["slab"] % 3
        cnt["slab"] += 1
        return i

    def load_slab(wname, k0, nk, c0, ncols=512):
        i = next_slab()
        src = WB[wname][k0 * 128:(k0 + nk) * 128, c0:c0 + ncols].rearrange("(c p) n -> p c n", p=128)
        dma("sp", slab[i][:, 0:nk, 0:ncols], src, f"slab{i}", reads=[WBUF[wname]], writes=[b_slab[i]])
        return i

    def next_ps(lo=2, n=4):
        i = lo + cnt["ps"] % n
        cnt["ps"] += 1
        return i

    def next_stg():
        i = cnt["stg"] % 3
        cnt["stg"] += 1
        return i

    def rstd_from_ss(ss_ap, out_ap, n, width, b):
        act(out_ap, ss_ap, AF.Sqrt, [b, b_const], [b], bias=epsb[:], scale=1.0 / width)
        P.add("dve", lambda e: e.reciprocal(out_ap, out_ap), [b], [b])

    def norm_transpose(x_src, g, hbuf, keep_x):
        for m in range(4):
            if x_src is not None:
                dma("sp", xres[:, m, :], x_src[m * 128:(m + 1) * 128, :], f"x{m}", writes=[b_x[m]])
            sbf = b_stat[m % 4]
            ss = stat[:, m:m + 1]
            xi = cnt["xn"] % 2
            cnt["xn"] += 1
            xn, b_xn = xnb[xi], b_xnb[xi]
            act(xn[:], xres[:, m, :], AF.Square, [b_x[m]], [b_xn, sbf], accum=ss)
            rstd_from_ss(ss, ss, 1, D, sbf)
            ts("dve", xn[:], xres[:, m, :], ss, None, ALU.mult, None, [b_x[m], sbf], [b_xn])
            pv = [psb(0), psb(1)]
            for c in range(16):
                tr(pv[c // 8][:, (c % 8) * 128:(c % 8 + 1) * 128], xn[:, c * 128:(c + 1) * 128], [b_xn], [b_ps[c // 8]])
            for hf in range(2):
                tt("dve", hT[hbuf][:, hf * 8:(hf + 1) * 8, m * 128:(m + 1) * 128],
                   pv[hf].rearrange("p (c t) -> p c t", t=128),
                   gcol[g][:, hf * 8:(hf + 1) * 8].unsqueeze(2).to_broadcast([128, 8, 128]),
                   ALU.mult, [b_ps[hf], b_const], [b_hT[hbuf]])

    def load_rope(c_src, s_src):
        dma("sp", ropeC[:], c_src.rearrange("(m p) d -> p m d", p=128), "ropeC", writes=[b_rope])
        dma("sp", ropeS[:], s_src.rearrange("(m p) d -> p m d", p=128), "ropeS", writes=[b_rope])

    def nr_compute(ps_i, nh, col0, m, g_bc, kb):
        sbf = b_stat[4 + kb]
        ssq = stat[:, 8 + 8 * kb: 8 + 8 * kb + nh]
        for h in range(nh):
            act(t1[:, h, :], psum[ps_i][:, col0 + h * 128: col0 + (h + 1) * 128], AF.Square,
                [b_ps[ps_i]], [b_t1, sbf], accum=ssq[:, h:h + 1])
        rstd_from_ss(ssq, ssq, nh, 128, sbf)
        for h in range(nh):
            stt("dve", kn[:, h, :], psum[ps_i][:, col0 + h * 128: col0 + (h + 1) * 128], ssq[:, h:h + 1], g_bc[:],
                ALU.mult, ALU.mult, [b_ps[ps_i], sbf, b_const], [b_kn])
        Cb = ropeC[:, m, :].unsqueeze(1).to_broadcast([128, nh, 128])
        tt("dve", t1[:, 0:nh, :], kn[:, 0:nh, :], Cb, ALU.mult, [b_kn, b_rope], [b_t1])
        knv = kn[:, 0:nh, :].rearrange("p h (a b c) -> p h a b c", a=2, b=2)
        t2v = t2[:, 0:nh, :].rearrange("p h (a b c) -> p h a b c", a=2, b=2)
        Sv = ropeS[:, m, :].rearrange("p (a b c) -> p a b c", a=2, b=2)
        for hf in range(2):
            tt("pool", t2v[:, :, :, hf, :], knv[:, :, :, 1 - hf, :], Sv[:, :, hf, :].unsqueeze(1).to_broadcast([128, nh, 2, 32]),
               ALU.mult, [b_kn, b_rope], [b_t2])
        tt("dve", krb[kb][:, 0:nh, :], t1[:, 0:nh, :], t2[:, 0:nh, :], ALU.add, [b_t1, b_t2], [b_krb[kb]])

    def nr_transpose(nh, m, kb, dst, dst_b):
        pb = psb(6 + kb)
        for h in range(nh):
            tr(pb[:, h * 128:(h + 1) * 128], krb[kb][:, h, :], [b_krb[kb]], [b_ps[6 + kb]])
        cp("dve", dst[:, 0:nh, m * 128:(m + 1) * 128], pb[:, 0:nh * 128].rearrange("p (h t) -> p h t", t=128),
           [b_ps[6 + kb]], [dst_b])

    cnt["kr"] = 0

    def norm_rope_T(ps_i, nh, col0, m, g_bc, dst, dst_b):
        kb = cnt["kr"] % 2
        cnt["kr"] += 1
        nr_compute(ps_i, nh, col0, m, g_bc, kb)
        return lambda: nr_transpose(nh, m, kb, dst, dst_b)

    o2 = 0
    tA, o2 = carve(o2, [128, 8192], F32)
    tB, o2 = carve(o2, [128, 8, 1024], BF16)
    tC, o2 = carve(o2, [128, 8, 1024], BF16)
    tD, o2 = carve(o2, [128, 8, 1024], BF16)
    tE, o2 = carve(o2, [128, 8, 1024], BF16)
    tM, o2 = carve(o2, [128, 2, 1024], F32)
    b_tA, b_tB, b_tC, b_tD, b_tE, b_tM = P.bufs(6, "tab")
    dma("sp", tA[:], rpbx_d, "tA", writes=[b_tA])
    dma("sp", tM[:, 0, :], mfull_d, "tM", writes=[b_tM])
    dma("sp", tM[:, 1, :], mint_d, "tM", writes=[b_tM])
    act(tA[:], tA[:], AF.Exp, [b_tA], [b_tA])
    tA3 = tA.rearrange("p (h x) -> p h x", h=8)
    tt("dve", tB[:], tA3, tM[:, 0, :].unsqueeze(1).to_broadcast([128, 8, 1024]), ALU.mult, [b_tA, b_tM], [b_tB])
    tt("dve", tC[:], tA3, tM[:, 1, :].unsqueeze(1).to_broadcast([128, 8, 1024]), ALU.mult, [b_tA, b_tM], [b_tC])
    tt("dve", tD[:], tB[:], tC[:], ALU.subtract, [b_tB, b_tC], [b_tD])
    dma("sp", tabs_d[0].rearrange("p (h x) -> p h x", h=8), tC[:], "tabst", reads=[b_tC])
    dma("sp", tabs_d[1].rearrange("p (h x) -> p h x", h=8), tB[:], "tabst", reads=[b_tB])
    for q in range(4):
        stt("dve", tE[:], tD[:], sel[:, q:q + 1], tC[:], ALU.mult, ALU.add, [b_tD, b_tC, b_const], [b_tE])
        dma("sp", tabs_d[2 + q * 2 + 1].rearrange("p (h x) -> p h x", h=8), tE[:], "tabsE", reads=[b_tE])
        ts("dve", tE[:], tC[:], nsel[:, q:q + 1], None, ALU.mult, None, [b_tC, b_stat[7]], [b_tE])
        dma("sp", tabs_d[2 + q * 2 + 0].rearrange("p (h x) -> p h x", h=8), tE[:], "tabsE", reads=[b_tE])

    P.barrier()
    for ji, jb in enumerate(J):
        for half in range(2):
            src = jb["mem"][half * 128:(half + 1) * 128, :]
            dma("sp", xres[:, half, :], src, f"x{half}", writes=[b_x[half]])
            sbf = b_stat[half]
            ss = stat[:, half:half + 1]
            xn, b_xn = xnb[half], b_xnb[half]
            act(xn[:], xres[:, half, :], AF.Square, [b_x[half]], [b_xn, sbf], accum=ss)
            rstd_from_ss(ss, ss, 1, D, sbf)
            ts("dve", xn[:], xres[:, half, :], ss, None, ALU.mult, None, [b_x[half], sbf], [b_xn])
            pv = [psb(0), psb(1)]
            for c in range(16):
                tr(pv[c // 8][:, (c % 8) * 128:(c % 8 + 1) * 128], xn[:, c * 128:(c + 1) * 128], [b_xn], [b_ps[c // 8]])
            for hf in range(2):
                tt("dve", hT[0][:, hf * 8:(hf + 1) * 8, half * 128:(half + 1) * 128],
                   pv[hf].rearrange("p (c t) -> p c t", t=128),
                   gcol["g_mem_c"][:, hf * 8:(hf + 1) * 8].unsqueeze(2).to_broadcast([128, 8, 128]),
                   ALU.mult, [b_ps[hf], b_const], [b_hT[0]])
        si = load_slab("w_ckv", 0, 16, 0)
        for h in range(4):
            pi = next_ps()
            for c in range(16):
                mm(psum[pi][:, 0:NMEM], slab[si][:, c, h * 128:(h + 1) * 128], hT[0][:, c, 0:NMEM], c == 0, c == 15,
                   [b_slab[si], b_hT[0]], [b_ps[pi]])
            cp("dve", KcT[ji][:, h, :], psum[pi][:, 0:NMEM], [b_ps[pi]], [b_kc[ji]])
        si = load_slab("w_ckv", 0, 16, 512)
        memset("dve", Vc1[ji][:, :, :, 128:129], 1.0, [b_kc[ji]])
        for half in range(2):
            pi = next_ps()
            for c in range(16):
                mm(psum[pi][:], hT[0][:, c, half * 128:(half + 1) * 128], slab[si][:, c, :], c == 0, c == 15,
                   [b_slab[si], b_hT[0]], [b_ps[pi]])
            cp("dve", Vc1[ji][:, half, :, 0:128], psum[pi][:].rearrange("p (h d) -> p h d", d=128), [b_ps[pi]], [b_kc[ji]])

    o2 = OFF_P13
    kvslab, o2 = carve(o2, [128, 16, 512], BF16)
    kT, o2 = carve(o2, [128, 2, 512], BF16)
    vt, o2 = carve(o2, [128, 4, 256], BF16)
    b_kvslab, b_kT, b_vt = P.bufs(3, "kv")
    dma("sp", kvslab[:], WB["w_kv"].rearrange("(c p) n -> p c n", p=128), "kvslab",
        reads=[WBUF["w_kv"]], writes=[b_kvslab])
    deferred_cast = [True]
    cnt["tA"] = 0

    def norm_transpose_m(x_rows, g, hbuf, m):
        dma("sp", xres[:, m, :], x_rows, f"x{m}", writes=[b_x[m]])
        sbf = b_stat[m % 4]
        ss = stat[:, m:m + 1]
        xi = cnt["xn"] % 2
        cnt["xn"] += 1
        xn, b_xn = xnb[xi], b_xnb[xi]
        act(xn[:], xres[:, m, :], AF.Square, [b_x[m]], [b_xn, sbf], accum=ss)
        rstd_from_ss(ss, ss, 1, D, sbf)
        act(xn[:], xres[:, m, :], AF.Copy, [b_x[m], sbf], [b_xn], scale=ss)
        bk = 2 * (cnt["tA"] % 2)
        cnt["tA"] += 1
        pv = [psb(bk), psb(bk + 1)]
        for c in range(16):
            tr(pv[c // 8][:, (c % 8) * 128:(c % 8 + 1) * 128], xn[:, c * 128:(c + 1) * 128], [b_xn], [b_ps[bk + c // 8]])

        def evac():
            for hf in range(2):
                tt("dve", hT[hbuf][:, hf * 8:(hf + 1) * 8, m * 128:(m + 1) * 128],
                   pv[hf].rearrange("p (c t) -> p c t", t=128),
                   gcol[g][:, hf * 8:(hf + 1) * 8].unsqueeze(2).to_broadcast([128, 8, 128]),
                   ALU.mult, [b_ps[bk + hf], b_const], [b_hT[hbuf]])
        return evac

    kTb, vtb = [kT], [vt]
    b_kTb, b_vtb = [b_kT], [b_vt]
    v_, o2 = carve(o2, [128, 2, 512], BF16)
    kTb.append(v_)
    v_, o2 = carve(o2, [128, 4, 256], BF16)
    vtb.append(v_)
    b_kTb.append(P.buf("kT1"))
    b_vtb.append(P.buf("vt1"))
    ropeCb, ropeSb, b_ropeb = [ropeC], [ropeS], [b_rope]
    v_, o2 = carve(o2, [128, 4, 128], F32)
    ropeCb.append(v_)
    v_, o2 = carve(o2, [128, 4, 128], F32)
    ropeSb.append(v_)
    b_ropeb.append(P.buf("rope1"))
    units = []
    for jb in J:
        for t in range(jb["S"] // 512):
            for m in range(4):
                units.append((jb, t, m))
    ust = {}

    def st_A(u):
        jb, t, m = units[u]
        if m == 0:
            ust[(id(jb), t)] = dict(hb=cnt["hT"] % 2, tb=(cnt["hT"] // 1) % 2)
            cnt["hT"] += 1
            tb = ust[(id(jb), t)]["tb"]
            dma("sp", ropeCb[tb][:], jb["rk_c"][t * 512:(t + 1) * 512, :].rearrange("(m p) d -> p m d", p=128), f"ropeC{tb}",
                writes=[b_ropeb[tb]])
            dma("sp", ropeSb[tb][:], jb["rk_s"][t * 512:(t + 1) * 512, :].rearrange("(m p) d -> p m d", p=128), f"ropeS{tb}",
                writes=[b_ropeb[tb]])
        us = ust[(id(jb), t)]
        return norm_transpose_m(jb["x_seq"][t * 512 + m * 128:t * 512 + (m + 1) * 128, :], "g_mix_c", us["hb"], m)

    def st_B1(u):
        jb, t, m = units[u]
        us = ust[(id(jb), t)]
        hb = us["hb"]
        pi = next_ps(4, 2)
        us[("pi", m)] = pi
        for c in range(16):
            mm(psum[pi][:], hT[hb][:, c, m * 128:(m + 1) * 128], kvslab[:, c, :], c == 0, c == 15,
               [b_kvslab, b_hT[hb]], [b_ps[pi]])

    def st_B2(u):
        nonlocal ropeC, ropeS, b_rope
        jb, t, m = units[u]
        us = ust[(id(jb), t)]
        tb = us["tb"]
        pi = us[("pi", m)]
        cp("act", vtb[tb][:, m, :], psum[pi][:, 256:512], [b_ps[pi]], [b_vtb[tb]])
        ropeC, ropeS, b_rope = ropeCb[tb], ropeSb[tb], b_ropeb[tb]
        us[("post", m)] = norm_rope_T(pi, 2, 0, m, gk_bc, kTb[tb], b_kTb[tb])
        ropeC, ropeS, b_rope = ropeCb[0], ropeSb[0], b_ropeb[0]

    def st_C(u):
        jb, t, m = units[u]
        us = ust[(id(jb), t)]
        tb = us["tb"]
        us[("post", m)]()
        if m == 3:
            dma("sp", jb["KbT"][:, t * 512:(t + 1) * 512].rearrange("(g p) s -> p g s", p=128), kTb[tb][:], f"kTst{tb}", reads=[b_kTb[tb]])
            dma("sp", jb["Vb"][t * 512:(t + 1) * 512, :].rearrange("(m p) c -> p m c", p=128), vtb[tb][:], f"vtst{tb}", reads=[b_vtb[tb]])

    NU = len(units)
    cast_list = []
    for k in ("w_in",):
        for r0 in range(0, wshape[k][0], 256):
            cast_list.append((k, r0))
    every = 2
    ci = 0
    for i in range(NU + 2):
        if 0 <= i - 1 < NU:
            st_B1(i - 1)
        ev_a = st_A(i) if i < NU else None
        if 0 <= i - 1 < NU:
            st_B2(i - 1)
        if 0 <= i - 2 < NU:
            st_C(i - 2)
        if ev_a is not None:
            ev_a()
        if i >= 4 and (i - 4) % every == 0 and ci < len(cast_list):
            k, r0 = cast_list[ci]
            ci += 1
            dma("pool", WB[k][r0:r0 + 256, :], W[k][r0:r0 + 256, :], "cast_" + k, writes=[WBUF[k]])
    while ci < len(cast_list):
        k, r0 = cast_list[ci]
        ci += 1
        dma("pool", WB[k][r0:r0 + 256, :], W[k][r0:r0 + 256, :], "cast_" + k, writes=[WBUF[k]])

    def fm_slab(si, hb, evac):
        for cc in range(4):
            pi = next_ps()
            for c in range(16):
                mm(psum[pi][:], slab[si][:, c, cc * 128:(cc + 1) * 128], hT[hb][:, c, :], c == 0, c == 15,
                   [b_slab[si], b_hT[hb]], [b_ps[pi]])
            evac(cc, pi)

    def tm_slab(si, hb, evac, nk=16, src=None, src_b=None):
        src = hT[hb] if src is None else src
        src_b = [b_hT[hb]] if src_b is None else (src_b if isinstance(src_b, list) else [src_b])
        pend = None
        for m in range(4):
            pi = next_ps()
            for c in range(nk):
                mm(psum[pi][:], src[:, c, m * 128:(m + 1) * 128], slab[si][:, c, :], c == 0, c == nk - 1,
                   [b_slab[si]] + src_b, [b_ps[pi]])
            if pend is not None:
                pend()
            pend = evac(m, pi)
        if pend is not None:
            pend()

    ev_rr = [0]

    def evac_copy(dst, src, rd, wr):
        e = ("act", "dve")[ev_rr[0] % 2]
        ev_rr[0] += 1
        cp(e, dst, src, rd, wr)

    def p1b_body(si, s, jb, kind, t, hb):
        T = jb["T"]
        gi = next_stg()

        def store_fm(dst_rows, col0, ncol=512):
            dma("sp", dst_rows[:, col0:col0 + ncol].rearrange("(c p) s -> p c s", p=128), stg[gi][:, :, 0:ncol],
                f"stg{gi}", reads=[b_stg[gi]])

        if s in (0, 1, 2, 3):
            def ev(cc, pi):
                evac_copy(stg[gi][:, cc, :], psum[pi][:], [b_ps[pi]], [b_stg[gi]])
            fm_slab(si, hb, ev)
            if s < 2:
                store_fm(jb["QaT"][s * 512:(s + 1) * 512, :], t * 512)
            elif kind == "own":
                store_fm(jb["KaT"][(s - 2) * 512:(s - 1) * 512, :], 256 + t * 512)
            else:
                dma("sp", jb["KaT"][(s - 2) * 512:(s - 1) * 512, 0:256].rearrange("(c p) s -> p c s", p=128),
                    stg[gi][:, :, 0:256], f"stg{gi}", reads=[b_stg[gi]])
                dma("sp", jb["KaT"][(s - 2) * 512:(s - 1) * 512, 256 + T:512 + T].rearrange("(c p) s -> p c s", p=128),
                    stg[gi][:, :, 256:512], f"stg{gi}", reads=[b_stg[gi]])
        elif s in (4, 5):
            def ev(m, pi):
                evac_copy(stg[gi][:, m, :], psum[pi][:], [b_ps[pi]], [b_stg[gi]])
            tm_slab(si, hb, ev)
            cs = (s - 4) * 512
            if kind == "own":
                dma("sp", jb["Va"][256 + t * 512:256 + (t + 1) * 512, cs:cs + 512].rearrange("(m p) c -> p m c", p=128),
                    stg[gi][:], f"stg{gi}", reads=[b_stg[gi]])
            else:
                dma("sp", jb["Va"][0:256, cs:cs + 512].rearrange("(m p) c -> p m c", p=128),
                    stg[gi][:, 0:2, :], f"stg{gi}", reads=[b_stg[gi]])
                dma("sp", jb["Va"][256 + T:512 + T, cs:cs + 512].rearrange("(m p) c -> p m c", p=128),
                    stg[gi][:, 2:4, :], f"stg{gi}", reads=[b_stg[gi]])
        elif s in (6, 7):
            def ev(m, pi):
                return norm_rope_T(pi, 4, 0, m, gq_bc, stg[gi], b_stg[gi])
            tm_slab(si, hb, ev)
            store_fm(jb["QbT"][(s - 6) * 512:(s - 5) * 512, :], t * 512)
        else:
            def ev(cc, pi):
                act(stg[gi][:, cc, :], psum[pi][:], AF.Sigmoid, [b_ps[pi]], [b_stg[gi]])
            fm_slab(si, hb, ev)
            store_fm(jb["sgT"][(s - 9) * 512:(s - 8) * 512, :], t * 512)

    per_tile = []
    for jb in J:
        T = jb["T"]
        for kind, t in [("own", t) for t in range(T // 512)] + [("halo", 0)]:
            ctx = {}

            def prep(_slot, jb=jb, kind=kind, t=t, ctx=ctx):
                ctx["hb"] = cnt["hT"] % 2
                cnt["hT"] += 1
                if kind == "own":
                    xs = jb["x_own"][t * 512:(t + 1) * 512, :]
                    load_rope(jb["rq_c"][t * 512:(t + 1) * 512, :], jb["rq_s"][t * 512:(t + 1) * 512, :])
                else:
                    xs = jb["x_halo"]
                norm_transpose(xs, "g_mix_c", ctx["hb"], False)

            slabs = [0, 1, 2, 3, 4, 5, 6, 7, 9, 10, 11, 12, 13, 14, 15, 16] if kind == "own" else [2, 3, 4, 5]
            sj = [((lambda s=s: load_slab("w_in", 0, 16, s * 512)),
                   (lambda si, s=s, jb=jb, kind=kind, t=t, ctx=ctx: p1b_body(si, s, jb, kind, t, ctx["hb"]))) for s in slabs]
            per_tile.append(((None, prep), sj))
    jobs = []
    carry = []
    for (pj, sj) in per_tile:
        jobs.append(pj)
        jobs.extend(carry)
        k = max(0, len(sj) - 4)
        jobs.extend(sj[:k])
        carry = sj[k:]
    jobs.extend(carry)
    late_casts = []
    for k in ("w_pa", "w_pb", "w_o", "w_cq", "w_co", "w_up", "w_down"):
        for r0 in range(0, wshape[k][0], 256):
            late_casts.append((k, r0))
    nslabjobs = sum(1 for j in jobs if j[0] is not None)
    stride = max(1, (nslabjobs - 8) // len(late_casts))
    jobs2 = []
    seen = 0
    for j in jobs:
        jobs2.append(j)
        if j[0] is not None:
            seen += 1
            if seen % stride == 0 and late_casts:
                k, r0 = late_casts.pop(0)
                jobs2.append((None, (lambda _s, k=k, r0=r0: dma("pool", WB[k][r0:r0 + 256, :], W[k][r0:r0 + 256, :],
                                                                "cast_" + k, writes=[WBUF[k]]))))
    for (k, r0) in late_casts:
        jobs2.append((None, (lambda _s, k=k, r0=r0: dma("pool", WB[k][r0:r0 + 256, :], W[k][r0:r0 + 256, :],
                                                        "cast_" + k, writes=[WBUF[k]]))))
    run_pipe(jobs2)
    P.barrier()

    o2 = 0
    pT = []
    for i in range(8):
        v, o2 = carve(o2, [128, 2, 512], BF16)
        pT.append(v)
    tsum = []
    for i in range(6):
        v, o2 = carve(o2, [128, 1024], BF16)
        tsum.append(v)
    b_tsum = P.bufs(6, "tsum")
    cnt["ts"] = 0
    dacc = []
    for i in range(4):
        v, o2 = carve(o2, [128, 1024], F32)
        dacc.append(v)
    dtot, o2 = carve(o2, [128, 512], F32)
    drec, o2 = carve(o2, [128, 512], F32)
    ones16, o2 = carve(o2, [128, 128], BF16)
    dhi, o2 = carve(o2, [128, 512], BF16)
    dlo, o2 = carve(o2, [128, 512], BF16)
    b_dhl = P.buf("dhl")
    oT, o2 = carve(o2, [128, 2, 512], BF16)
    b_pT = P.bufs(8, "pT")
    b_dacc = P.bufs(4, "dacc")
    b_dtot, b_drec, b_ones = P.bufs(3, "den")
    b_oT = P.bufs(2, "oT")
    OFF_ATT = (o2 + 31) // 32 * 32
    cnt["pT"] = 0
    cnt["oT"] = 0
    cnt["stb"] = 0
    cnt["hd"] = 0
    STB = [[0, 1], [2, 3], [4, 5]]
    OACC = [6, 7]
    DENB = [0]
    memset("dve", ones16[:], 1.0, [b_ones])

    def den_matmul(src_ap, src_b):
        cp("dve", dhi[:], src_ap, src_b, [b_dhl])
        tt("dve", dlo[:], src_ap, dhi[:], ALU.subtract, list(src_b) + [b_dhl], [b_dhl])
        db = DENB[0]
        mm(psum[db][:], ones16[:], dhi[:], True, False, [b_ones, b_dhl], [b_ps[db]])
        mm(psum[db][:], ones16[:], dlo[:], False, True, [b_ones, b_dhl], [b_ps[db]])

    def att_finish(par, dstT, row0, col0, used_pool, used_dve=True):
        src = par
        if used_pool and used_dve:
            tt("dve", dacc[par][:], dacc[par][:], dacc[2 + par][:], ALU.add, [b_dacc[par], b_dacc[2 + par]], [b_dacc[par]])
        elif used_pool:
            src = 2 + par
        tt("dve", dtot[:], dacc[src][:, 0:512], dacc[src][:, 512:1024], ALU.add, [b_dacc[src]], [b_dtot])
        den_matmul(dtot[:], [b_dtot])
        P.add("dve", lambda e, db=DENB[0]: e.reciprocal(drec[:], psum[db][:]), [b_ps[DENB[0]]], [b_drec])
        oi = cnt["oT"] % 2
        cnt["oT"] += 1
        tt("dve", oT[:, oi, :], psum[OACC[par]][:], drec[:], ALU.mult, [b_ps[OACC[par]], b_drec], [b_oT[oi]])
        dma("sp!", dstT[row0:row0 + 128, col0:col0 + 512], oT[:, oi, :], f"oT{oi}", reads=[b_oT[oi]])

    for jb in J:
        S, T = jb["S"], jb["T"]
        NCH = S // 128
        NP = NCH // 2
        o3 = OFF_ATT
        KTg, o3 = carve(o3, [128, S], BF16)
        Vg, o3 = carve(o3, [128, NCH, 128], BF16)
        qb, o3 = carve(o3, [128, 2, 512], BF16)
        b_KTg, b_Vg = P.bufs(2, "kvg")
        b_qb = P.bufs(2, "qb")
        for g in range(2):
            dma("sp", KTg[:], jb["KbT"][g * 128:(g + 1) * 128, :], "KTg", writes=[b_KTg])
            dma("sp", Vg[:], jb["Vb"][:, g * 128:(g + 1) * 128].rearrange("(c p) d -> p c d", p=128), "V1g",
                writes=[b_Vg])
            heads = [(qt, hh) for qt in range(T // 512) for hh in range(4)]
            steps = [(hi, p) for hi in range(len(heads)) for p in range(NP)]
            hstate = {}

            def stage_a(st_):
                hi, p = st_
                if p == 0:
                    par = cnt["hd"] % 2
                    cnt["hd"] += 1
                    qi = par
                    qt, hh = heads[hi]
                    h = g * 4 + hh
                    dma("sp", qb[:, qi, :], jb["QbT"][h * 128:(h + 1) * 128, qt * 512:(qt + 1) * 512], f"qb{qi}", writes=[b_qb[qi]])
                    hstate[hi] = dict(par=par, qi=qi, h=h, qt=qt, npool=0, ndve=0)
                hs = hstate[hi]
                sb_ = STB[cnt["stb"] % 3]
                cnt["stb"] += 1
                pi = cnt["pT"] % 8
                cnt["pT"] += 1
                for k in range(2):
                    ch = p * 2 + k
                    mm(psum[sb_[k]][:], KTg[:, ch * 128:(ch + 1) * 128], qb[:, hs["qi"], :], True, True,
                       [b_KTg, b_qb[hs["qi"]]], [b_ps[sb_[k]]])
                act(pT[pi][:].rearrange("p k s -> p (k s)"), psbig[sb_[0] // 2][:], AF.Exp,
                    [b_ps[sb_[0]], b_ps[sb_[1]]], [b_pT[pi]], scale=SCALE)
                return pi

            def stage_b(st_, pi):
                hi, p = st_
                hs = hstate[hi]
                par = hs["par"]
                for k in range(2):
                    ch = p * 2 + k
                    mm(psum[OACC[par]][:], Vg[:, ch, :], pT[pi][:, k, :], ch == 0, ch == NCH - 1,
                       [b_pT[pi], b_Vg], [b_ps[OACC[par]]])
                pflat = pT[pi][:].rearrange("p k s -> p (k s)")
                if p % 2 == 0:
                    hs["prev_pi"] = pi
                else:
                    ppi = hs["prev_pi"]
                    ti = cnt["ts"] % 6
                    cnt["ts"] += 1
                    tt("dve", tsum[ti][:], pT[ppi][:].rearrange("p k s -> p (k s)"), pflat, ALU.add,
                       [b_pT[ppi], b_pT[pi]], [b_tsum[ti]])
                    if p % 4 == 1:
                        hs["prev_ti"] = ti
                    else:
                        pti = hs["prev_ti"]
                        t2i = cnt["ts"] % 6
                        cnt["ts"] += 1
                        tt("dve", tsum[t2i][:], tsum[pti][:], tsum[ti][:], ALU.add, [b_tsum[pti], b_tsum[ti]], [b_tsum[t2i]])
                        if (p // 4) % 4 == 3:
                            e_, ai, key = "dve", par, "ndve"
                        else:
                            e_, ai, key = "pool", 2 + par, "npool"
                        if hs[key] == 0:
                            cp(e_, dacc[ai][:], tsum[t2i][:], [b_tsum[t2i]], [b_dacc[ai]])
                        else:
                            tt(e_, dacc[ai][:], dacc[ai][:], tsum[t2i][:], ALU.add, [b_tsum[t2i], b_dacc[ai]], [b_dacc[ai]])
                        hs[key] += 1
                if p == NP - 1:
                    att_finish(par, jb["ObT"], hs["h"] * 128, hs["qt"] * 512, hs["npool"] > 0, hs["ndve"] > 0)

            pend = []
            for st_ in steps:
                pi = stage_a(st_)
                pend.append((st_, pi))
                if len(pend) > 2:
                    stage_b(*pend.pop(0))
            while pend:
                stage_b(*pend.pop(0))
        P.barrier()

    DENB[0] = 4
    o3 = OFF_ATT
    tabI, o3 = carve(o3, [128, 8, 16, 64], BF16)
    tabX, o3 = carve(o3, [128, 4, 8 * 16 * 64], BF16)
    qa, o3 = carve(o3, [128, 8, 512], BF16)
    ka, o3 = carve(o3, [128, 8, 1024], BF16)
    Va_, o3 = carve(o3, [128, 8, 1024], BF16)
    b_tabI, b_tabX, b_qa, b_ka, b_Va = P.bufs(5, "na")
    dma("sp", tabI[:].rearrange("p h s q -> p (h s q)"), tabs_d[0], "tabI", writes=[b_tabI])
    tabXv = tabX.rearrange("p v (h s q) -> p v h s q", h=8, s=16)
    for ji, jb in enumerate(J):
        T = jb["T"]
        nblk = T // 512
        for v in range(4):
            cls, var = v // 2, v % 2
            dma("sp", tabX[:, v, :], tabs_d[2 + (ji * 2 + cls) * 2 + var], "tabX", writes=[b_tabX])
        for b in range(nblk):
            segs_all = na_segments(b == 0, b == nblk - 1)
            dma("sp", qa[:], jb["QaT"][:, b * 512:(b + 1) * 512].rearrange("(h p) s -> p h s", p=128), "qa", writes=[b_qa])
            dma("sp", ka[:], jb["KaT"][:, b * 512:b * 512 + 1024].rearrange("(h p) s -> p h s", p=128), "ka", writes=[b_ka])
            dma("sp", Va_[:], jb["Va"][b * 512:b * 512 + 1024, :].rearrange("(c p) x -> p c x", p=128), "V1a", writes=[b_Va])
            steps = [(h, j) for h in range(8) for j in range(8)]
            hstate = {}

            def na_a(st_):
                h, j = st_
                if j == 0:
                    par = cnt["hd"] % 2
                    cnt["hd"] += 1
                    hstate[h] = dict(par=par)
                    memset("pool", dacc[par][:, 0:512], 0.0, [b_dacc[par]])
                m_lo, m_hi, segs = segs_all[j]
                c0, c1 = m_lo * 128, (m_hi + 1) * 128
                sbk = STB[cnt["stb"] % 2][cnt["stb"] // 2 % 2]
                cnt["stb"] += 1
                pi = cnt["pT"] % 8
                cnt["pT"] += 1
                mm(psum[sbk][:, c0:c1], ka[:, h, j * 128:(j + 1) * 128], qa[:, h, c0:c1], True, True,
                   [b_ka, b_qa], [b_ps[sbk]])
                act(pT[pi][:, 0, c0:c1], psum[sbk][:, c0:c1], AF.Exp, [b_ps[sbk]], [b_pT[pi]], scale=SCALE)
                eng_ = "pool" if (j % 2) else "dve"
                for (ra, rb, tb, s0) in segs:
                    n = rb - ra
                    if tb == "int":
                        tv = tabI[:, h, s0:s0 + n, :]
                        tbuf = b_tabI
                    else:
                        vi = {"first0": 0, "first1": 1, "last0": 2, "last1": 3}[tb]
                        tv = tabXv[:, vi, h, s0:s0 + n, :]
                        tbuf = b_tabX
                    pv_ = pT[pi][:, 0, ra * 64:rb * 64].rearrange("p (r q) -> p r q", q=64)
                    tt(eng_, pv_, pv_, tv, ALU.mult, [b_pT[pi], tbuf], [b_pT[pi]])
                return pi

            def na_b(st_, pi):
                h, j = st_
                par = hstate[h]["par"]
                m_lo, m_hi, segs = segs_all[j]
                c0, c1 = m_lo * 128, (m_hi + 1) * 128
                mm(psum[OACC[par]][:, c0:c1], Va_[:, j, h * 128:(h + 1) * 128], pT[pi][:, 0, c0:c1], j == 0, j == 7,
                   [b_pT[pi], b_Va], [b_ps[OACC[par]]], skip=True)
                tt("dve" if (j % 2) else "pool", dacc[par][:, c0:c1], dacc[par][:, c0:c1], pT[pi][:, 0, c0:c1], ALU.add,
                   [b_pT[pi], b_dacc[par]], [b_dacc[par]])
                if j == 7:
                    den_matmul(dacc[par][:, 0:512], [b_dacc[par]])
                    P.add("dve", lambda e, db=DENB[0]: e.reciprocal(drec[:], psum[db][:]), [b_ps[DENB[0]]], [b_drec])
                    oi = cnt["oT"] % 2
                    cnt["oT"] += 1
                    tt("dve", oT[:, oi, :], psum[OACC[par]][:], drec[:], ALU.mult, [b_ps[OACC[par]], b_drec], [b_oT[oi]])
                    dma("sp!", jb["OaT"][h * 128:(h + 1) * 128, b * 512:(b + 1) * 512], oT[:, oi, :], f"oT{oi}", reads=[b_oT[oi]])

            pend = []
            for st_ in steps:
                pi = na_a(st_)
                pend.append((st_, pi))
                if len(pend) > 6:
                    na_b(*pend.pop(0))
            while pend:
                na_b(*pend.pop(0))
    P.barrier()

    o2 = OFF_COMMON
    aT = []
    o_a = o2
    for i in range(2):
        v, o2 = carve(o2, [128, 16, 512], BF16)
        aT.append(v)
    OabT, o_a = carve(o_a, [128, 16, 512], BF16)
    sgs = []
    for i in range(2):
        v, o_a = carve(o_a, [128, 8, 512], BF16)
        sgs.append(v)
    tmpA, o2 = carve(o2, [128, 512], F32)
    tmpB, o2 = carve(o2, [128, 512], F32)
    qcT, o2 = carve(o2, [128, 4, 512], BF16)
    ocs, o2 = carve(o2, [128, 4, 512], BF16)
    ocT, o2 = carve(o2, [128, 4, 512], BF16)
    pTc, o2 = carve(o2, [128, 2, 512], BF16)
    gfin, o2 = carve(o2, [128, D], F32)
    b_Oab = P.buf("Oab")
    b_sgs = P.bufs(2, "sgs")
    b_aT = [[b_Oab], [b_sgs[0], b_sgs[1]]]
    b_tmpA, b_tmpB, b_qcT, b_ocs, b_ocT, b_gfin = P.bufs(6, "p3")
    b_pTc = P.bufs(2, "pTc")
    dma("sp", gfin[:], gfin_d, "gfin", writes=[b_gfin])

    def resid_add(m, pi, n):
        tt("dve", xres[:, m, n * 512:(n + 1) * 512], xres[:, m, n * 512:(n + 1) * 512], psum[pi][:], ALU.add,
           [b_ps[pi], b_x[m]], [b_x[m]])

    def p3_tile_jobs(ji, jb, t):
        tsl = slice(t * 512, (t + 1) * 512)
        ctx = {}
        jobs = []

        def xload(_):
            for m in range(4):
                dma("sp", xres[:, m, :], jb["x_own"][t * 512 + m * 128:t * 512 + (m + 1) * 128, :], f"x{m}", writes=[b_x[m]])

        def start(_):
            dma("sp", OabT[:, 0:8, :], jb["OaT"][:, tsl].rearrange("(c p) s -> p c s", p=128), "Oab", writes=[b_Oab])
            dma("sp", OabT[:, 8:16, :], jb["ObT"][:, tsl].rearrange("(c p) s -> p c s", p=128), "Oab", writes=[b_Oab])
            ctx["mixb"] = cnt["hT"] % 2
            cnt["hT"] += 1
        jobs.append((None, start))

        def ld_merge(n):
            si = next_slab()
            dma("sp", slab[si][:, 0:8, :], WB["w_pa"][:, n * 512:(n + 1) * 512].rearrange("(c p) n -> p c n", p=128),
                f"slab{si}", reads=[WBUF["w_pa"]], writes=[b_slab[si]])
            dma("sp", slab[si][:, 8:16, :], WB["w_pb"][:, n * 512:(n + 1) * 512].rearrange("(c p) n -> p c n", p=128),
                f"slab{si}", reads=[WBUF["w_pb"]], writes=[b_slab[si]])
            return si

        def merge(si, n):
            mixb = ctx["mixb"]
            gi = n % 2
            dma("sp", sgs[gi][:, 0:4, :], jb["sgT"][n * 512:(n + 1) * 512, tsl].rearrange("(c p) s -> p c s", p=128),
                f"sgs{gi}", writes=[b_sgs[gi]])
            dma("sp", sgs[gi][:, 4:8, :], jb["sgT"][2048 + n * 512:2048 + (n + 1) * 512, tsl].rearrange("(c p) s -> p c s", p=128),
                f"sgs{gi}", writes=[b_sgs[gi]])
            for cc in range(4):
                pa = next_ps()
                for c in range(8):
                    mm(psum[pa][:], slab[si][:, c, cc * 128:(cc + 1) * 128], OabT[:, c, :], c == 0, c == 7,
                       [b_slab[si], b_Oab], [b_ps[pa]])
                pbk = next_ps()
                for c in range(8):
                    mm(psum[pbk][:], slab[si][:, 8 + c, cc * 128:(cc + 1) * 128], OabT[:, 8 + c, :], c == 0, c == 7,
                       [b_slab[si], b_Oab], [b_ps[pbk]])
                tt("dve", tmpA[:], psum[pa][:], sgs[gi][:, cc, :], ALU.mult, [b_ps[pa], b_sgs[gi]], [b_tmpA])
                tt("dve", tmpB[:], psum[pbk][:], sgs[gi][:, 4 + cc, :], ALU.mult, [b_ps[pbk], b_sgs[gi]], [b_tmpB])
                tt("pool", hT[mixb][:, n * 4 + cc, :], tmpA[:], tmpB[:], ALU.add, [b_tmpA, b_tmpB], [b_hT[mixb]])
        for n in range(4):
            jobs.append(((lambda n=n: ld_merge(n)), (lambda si, n=n: merge(si, n))))
        jobs.insert(3, (None, xload))
        for n in range(4):
            jobs.append(((lambda n=n: load_slab("w_o", 0, 16, n * 512)),
                         (lambda si, n=n: tm_slab(si, ctx["mixb"], lambda m, pi: resid_add(m, pi, n)))))

        def cq(si):
            hb = cnt["hT"] % 2
            cnt["hT"] += 1
            norm_transpose(None, "g_cross_c", hb, True)

            def ev_q(cc, pi):
                cp("act", qcT[:, cc, :], psum[pi][:], [b_ps[pi]], [b_qcT])
            fm_slab(si, hb, ev_q)
            for h in range(4):
                for k in range(2):
                    pi = next_ps()
                    mm(psum[pi][:], KcT[ji][:, h, k * 128:(k + 1) * 128], qcT[:, h, :], True, True, [b_kc[ji], b_qcT], [b_ps[pi]])
                    act(pTc[:, k, :], psum[pi][:], AF.Exp, [b_ps[pi]], [b_pTc[0]], scale=SCALE)
                for m in range(4):
                    pi = 6 + m % 2
                    for k in range(2):
                        mm(psum[pi][:, 0:129], pTc[:, k, m * 128:(m + 1) * 128], Vc1[ji][:, k, h, 0:129], k == 0, k == 1,
                           [b_pTc[0], b_kc[ji]], [b_ps[pi]])
                    P.add("dve", lambda e, m=m, pi=pi: e.reciprocal(rden[:, m:m + 1], psum[pi][:, 128:129]), [b_ps[pi]], [b_rden])
                    ts("dve", ocs[:, m, h * 128:(h + 1) * 128], psum[pi][:, 0:128], rden[:, m:m + 1], None, ALU.mult, None,
                       [b_ps[pi], b_rden], [b_ocs])
            for m in range(4):
                pb = psb(6 + m % 2)
                for h in range(4):
                    tr(pb[:, h * 128:(h + 1) * 128], ocs[:, m, h * 128:(h + 1) * 128], [b_ocs], [b_ps[6 + m % 2]])
                cp("dve", ocT[:, :, m * 128:(m + 1) * 128], pb[:, 0:512].rearrange("p (h t) -> p h t", t=128),
                   [b_ps[6 + m % 2]], [b_ocT])
        jobs.append(((lambda: load_slab("w_cq", 0, 16, 0)), cq))
        for n in range(4):
            jobs.append(((lambda n=n: load_slab("w_co", 0, 4, n * 512)),
                         (lambda si, n=n: tm_slab(si, None, lambda m, pi: resid_add(m, pi, n), nk=4, src=ocT, src_b=b_ocT))))

        def mlp_norm(_):
            ctx["hb"] = cnt["hT"] % 2
            cnt["hT"] += 1
            norm_transpose(None, "g_mlp_c", ctx["hb"], True)
        jobs.append((None, mlp_norm))

        def up(si, n, ab):
            def ev_up(cc, pi):
                act(tmpA[:], psum[pi][:], AF.Relu, [b_ps[pi]], [b_tmpA])
                tt("dve", aT[ab][:, n * 4 + cc, :], tmpA[:], tmpA[:], ALU.mult, [b_tmpA], b_aT[ab])
            fm_slab(si, ctx["hb"], ev_up)
        for kg in range(4):
            ab = kg % 2
            for n in range(4):
                jobs.append(((lambda kg=kg, n=n: load_slab("w_up", 0, 16, (kg * 4 + n) * 512)),
                             (lambda si, n=n, ab=ab: up(si, n, ab))))
            for n in range(4):
                jobs.append(((lambda kg=kg, n=n: load_slab("w_down", kg * 16, 16, n * 512)),
                             (lambda si, n=n, ab=ab: tm_slab(si, None, lambda m, pi: resid_add(m, pi, n), src=aT[ab], src_b=b_aT[ab]))))

        def fin(_):
            for m in range(4):
                sbf = b_stat[m % 4]
                ss = stat[:, m:m + 1]
                xn, b_xn = xnb[m % 2], b_xnb[m % 2]
                act(xn[:], xres[:, m, :], AF.Square, [b_x[m]], [b_xn, sbf], accum=ss)
                rstd_from_ss(ss, ss, 1, D, sbf)
                stt("dve", xres[:, m, :], xres[:, m, :], ss, gfin[:], ALU.mult, ALU.mult, [b_x[m], sbf, b_gfin], [b_x[m]])
                dma("sp", jb["y"][t * 512 + m * 128:t * 512 + (m + 1) * 128, :], xres[:, m, :], f"xo{m}", reads=[b_x[m]])
        jobs.append((None, fin))
        return jobs

    jobs = []
    for ji, jb in enumerate(J):
        for t in range(jb["T"] // 512):
            jobs.extend(p3_tile_jobs(ji, jb, t))
    run_pipe(jobs)

    P.emit()
    st.close()
    return nc, P


def _rope_tables(S):
    t = np.arange(S)
    row = (t // 64).astype(np.float32)
    col = (t % 64).astype(np.float32)
    inv = (10000.0 ** (-np.arange(32, dtype=np.float32) / 32)).astype(np.float32)
    ar = row[:, None] * inv[None, :]
    ac = col[:, None] * inv[None, :]
    C = np.concatenate([np.cos(ar), np.cos(ar), np.cos(ac), np.cos(ac)], axis=1).astype(np.float32)
    Sn = np.concatenate([-np.sin(ar), np.sin(ar), -np.sin(ac), np.sin(ac)], axis=1).astype(np.float32)
    return C, Sn


def _na_consts(rpb):
    rp = np.asarray(rpb, np.float32)[0]
    e = np.arange(128) // 64
    kc = np.arange(128) % 64
    s = np.arange(16)
    qc = np.arange(64)
    ap = s[None, :] - e[:, None]
    a = 14 - ap
    slot_ok = (ap >= 0) & (ap <= 14)
    dc = kc[:, None] - qc[None, :]
    cs = np.clip(qc - 8, 0, 48)
    col_ok = (kc[:, None] >= cs[None, :]) & (kc[:, None] < cs[None, :] + 16)
    bi = np.clip(dc + 15, 0, 30)
    A = np.clip(a, 0, 14)
    g = rp[:, A[:, :, None], bi[:, None, :]]
    ok = slot_ok[:, :, None] & col_ok[:, None, :]
    g = np.where(ok[None], g, 0.0).astype(np.float32)
    rpb_exp = np.ascontiguousarray(g.transpose(1, 0, 2, 3)).reshape(128, 8 * 16 * 64)
    mfull = ok.astype(np.float32).reshape(128, 1024)
    mint = (ok & ((ap >= 4) & (ap <= 11))[:, :, None]).astype(np.float32).reshape(128, 1024)
    return rpb_exp, mfull, mint


def _gcol(g):
    return np.ascontiguousarray(np.asarray(g, np.float32).reshape(16, 128).T)


_CACHE = {}


def _run(inputs, S_p, S_s, debug=False):
    cfg = Cfg(S_p, S_s)
    key = (S_p, S_s, debug)
    if key not in _CACHE:
        _CACHE[key] = build(cfg, debug)
    nc, P = _CACHE[key]
    f = lambda k: np.asarray(inputs[k], np.float32)
    xp, xs, mp, ms = f("x_prompt"), f("x_sample"), f("mem_prompt"), f("mem_sample")
    shared = {k: np.ascontiguousarray(f(k)[0]) for k in ("w_in", "w_pa", "w_pb", "w_o", "w_cq", "w_ckv", "w_co", "w_up", "w_down")}
    shared["g_mix_c"] = _gcol(f("g_mix")[0])
    shared["g_cross_c"] = _gcol(f("g_cross")[0])
    shared["g_mem_c"] = _gcol(f("g_mem")[0])
    shared["g_mlp_c"] = _gcol(f("g_mlp")[0])
    shared["gq_bc"] = np.ascontiguousarray(np.broadcast_to(f("g_q")[0][None, :], (128, 128)))
    shared["gk_bc"] = np.ascontiguousarray(np.broadcast_to(f("g_k")[0][None, :], (128, 128)))
    shared["gfin_bc"] = np.ascontiguousarray(np.broadcast_to(f("g_final")[None, :], (128, D)))
    shared["ident"] = np.eye(128, dtype=np.float32)
    shared["rpb_exp"], shared["mask_full"], shared["mask_int"] = _na_consts(f("rpb"))
    Cp, Sp = _rope_tables(S_p)
    Cs, Ss = _rope_tables(S_s)
    Tp, Ts = S_p // 2, S_s // 8
    zeros256 = np.zeros((256, D), np.float32)
    in_maps = []
    for c in range(8):
        b, hf = c // 2, c % 2
        m = dict(shared)
        o0 = hf * Tp
        m["xp_seq"] = np.ascontiguousarray(xp[b])
        m["xp_own"] = np.ascontiguousarray(xp[b, o0:o0 + Tp])
        pre = xp[b, o0 - 256:o0] if o0 > 0 else zeros256
        post = xp[b, o0 + Tp:o0 + Tp + 256] if o0 + Tp < S_p else zeros256
        m["xp_halo"] = np.ascontiguousarray(np.concatenate([pre, post], 0))
        m["memp"] = np.ascontiguousarray(mp[b])
        m["rkp_c"], m["rkp_s"] = Cp, Sp
        m["rqp_c"], m["rqp_s"] = np.ascontiguousarray(Cp[o0:o0 + Tp]), np.ascontiguousarray(Sp[o0:o0 + Tp])
        s0 = c * Ts
        m["xs_seq"] = np.ascontiguousarray(xs[0])
        m["xs_own"] = np.ascontiguousarray(xs[0, s0:s0 + Ts])
        pre = xs[0, s0 - 256:s0] if s0 > 0 else zeros256
        post = xs[0, s0 + Ts:s0 + Ts + 256] if s0 + Ts < S_s else zeros256
        m["xs_halo"] = np.ascontiguousarray(np.concatenate([pre, post], 0))
        m["mems"] = np.ascontiguousarray(ms[0])
        m["rks_c"], m["rks_s"] = Cs, Ss
        m["rqs_c"], m["rqs_s"] = np.ascontiguousarray(Cs[s0:s0 + Ts]), np.ascontiguousarray(Ss[s0:s0 + Ts])
        selv = np.array([hf == 0, hf == 1, c == 0, c == 7], np.float32)
        m["sel"] = np.ascontiguousarray(np.broadcast_to(selv[None, :], (128, 4)))
        in_maps.append(m)
    res = run_bass_kernel_spmd(nc, in_maps, core_ids=list(range(8)))
    yp = np.zeros((4, S_p, D), np.float32)
    ys = np.zeros((1, S_s, D), np.float32)
    for c in range(8):
        b, hf = c // 2, c % 2
        yp[b, hf * Tp:(hf + 1) * Tp] = res.results[c]["yp"]
        ys[0, c * Ts:(c + 1) * Ts] = res.results[c]["ys"]
    return (yp, ys), res


def kernel(**inputs):
    (yp, ys), _ = _run(inputs, 8192, 16384)
    return (yp, ys)
```

```python
import numpy as np
from contextlib import ExitStack
import concourse.bass as bass
import concourse.mybir as mybir
from concourse.bass_utils import run_bass_kernel_spmd

F32 = mybir.dt.float32
BF16 = mybir.dt.bfloat16
U8 = mybir.dt.uint8
ALU = mybir.AluOpType
AF = mybir.ActivationFunctionType

D = 2048
DIN = 8704
DFF = 8192
NMEM = 256
EPS = 1e-6
SCALE = 128 ** -0.5


class Buf:
    __slots__ = ("name", "writers", "readers")

    def __init__(self, name):
        self.name = name
        self.writers = {}
        self.readers = {}


class Op:
    __slots__ = ("eng", "fn", "waits", "signal", "sigval", "idx", "dma")

    def __init__(self, eng, fn):
        self.eng = eng
        self.fn = fn
        self.waits = []
        self.signal = False
        self.sigval = None
        self.idx = None
        self.dma = None


ENGS = ("pe", "act", "dve", "pool", "sp")
STRICT = ("act", "dve", "pool")


class Prog:
    def __init__(self, nc):
        self.nc = nc
        self.ops = {e: [] for e in ENGS}
        self.waited = {e: {} for e in ENGS}
        self.dma_count = {}

    def buf(self, name="b"):
        return Buf(name)

    def bufs(self, n, name="b"):
        return [Buf(f"{name}{i}") for i in range(n)]

    def add(self, eng, fn, reads=(), writes=(), dma=None):
        mykey = dma if dma is not None else eng
        deps = []
        for b in reads:
            for k, t in b.writers.items():
                if k != mykey or (dma is None and eng in STRICT):
                    deps.append(t)
        for b in writes:
            for k, t in b.writers.items():
                if k != mykey:
                    deps.append(t)
            for k, t in b.readers.items():
                if k != mykey:
                    deps.append(t)
        op = Op(eng, fn)
        w = self.waited[eng]
        for t in deps:
            if t[0] == "c":
                p = t[1]
                if w.get(p.eng, -1) >= p.idx:
                    continue
                w[p.eng] = p.idx
                p.signal = True
                op.waits.append(t)
            else:
                sk = t[1]
                v = self.dma_count[sk]
                if w.get(sk, 0) >= v:
                    continue
                w[sk] = v
                op.waits.append(("d", sk, v))
        op.idx = len(self.ops[eng])
        self.ops[eng].append(op)
        if dma is not None:
            self.dma_count[dma] = self.dma_count.get(dma, 0) + 16
            op.dma = (dma, self.dma_count[dma])
            tok = ("d", dma, self.dma_count[dma])
        else:
            tok = ("c", op)
        for b in reads:
            b.readers[mykey] = tok
        for b in writes:
            b.writers = {mykey: tok}
            b.readers = {}
        return op

    def barrier(self):
        lasts = {}
        for e in ("pe", "act", "dve", "pool"):
            cl = [o for o in self.ops[e] if o.dma is None and o.fn is not None]
            if cl:
                lasts[e] = cl[-1]
                cl[-1].signal = True
        for f in ENGS:
            op = Op(f, None)
            w = self.waited[f]
            for e, l in lasts.items():
                if e != f and w.get(e, -1) < l.idx:
                    w[e] = l.idx
                    op.waits.append(("c", l))
            for sk, v in self.dma_count.items():
                if w.get(sk, 0) < v:
                    w[sk] = v
                    op.waits.append(("d", sk, v))
            op.idx = len(self.ops[f])
            self.ops[f].append(op)

    def emit(self):
        nc = self.nc
        self.barrier()
        for e in ENGS:
            c = 0
            for op in self.ops[e]:
                if op.signal and op.dma is None:
                    c += 1
                    op.sigval = c
        with ExitStack() as st:
            esem = {e: st.enter_context(nc.semaphore(f"s_{e}")) for e in ENGS}
            dsem = {}
            for i, sk in enumerate(self.dma_count):
                dsem[sk] = st.enter_context(nc.semaphore(f"d_{i}"))
            block = st.enter_context(nc.Block())

            def run(e, eng):
                for op in self.ops[e]:
                    for t in op.waits:
                        if t[0] == "c":
                            eng.wait_ge(esem[t[1].eng], t[1].sigval)
                        else:
                            eng.wait_ge(dsem[t[1]], t[2])
                    if op.fn is None:
                        continue
                    ins = op.fn(eng)
                    if op.dma is not None:
                        ins.then_inc(dsem[op.dma[0]], 16)
                    elif op.signal:
                        ins.then_inc(esem[e], 1)

            block.tensor(lambda eng: run("pe", eng))
            block.scalar(lambda eng: run("act", eng))
            block.vector(lambda eng: run("dve", eng))
            block.gpsimd(lambda eng: run("pool", eng))
            block.sync(lambda eng: run("sp", eng))
        self.stats = {e: len(self.ops[e]) for e in ENGS}


def na_segments(first, last):
    out = []
    for j in range(8):
        m_lo, m_hi = max(0, j - 4), min(3, j)
        if 2 <= j <= 5:
            if first:
                m_lo = 0
            if last:
                m_hi = 3
        r_lo, r_hi = 2 * m_lo, 2 * m_hi + 2
        segs = []
        r = r_lo
        while r < r_hi:
            if first and r < 4:
                e_ = min(4, r_hi)
                tb = "first1" if 2 <= j <= 5 else "first0"
            elif last and r >= 5:
                e_ = r_hi
                tb = "last1" if 2 <= j <= 5 else "last0"
            else:
                e_ = r_hi
                if last:
                    e_ = min(e_, 5)
                tb = "int"
            segs.append((r, e_, tb, 11 - 2 * j + r))
            r = e_
        for (a, b, tb, s0) in segs:
            assert 0 <= s0 and s0 + (b - a) <= 16, (j, segs)
        out.append((m_lo, m_hi, segs))
    return out


class Cfg:
    def __init__(self, S_p, S_s):
        self.jobs = [dict(n="p", S=S_p, T=S_p // 2), dict(n="s", S=S_s, T=S_s // 8)]


def build(cfg, debug=False):
    nc = bass.Bass("TRN2", target_bir_lowering=False)
    P = Prog(nc)
    st = ExitStack()
    dkind = "ExternalOutput" if debug else "Internal"

    def din(name, shape, dt=F32):
        return nc.dram_tensor(name, list(shape), dt, kind="ExternalInput").ap()

    def dscr(name, shape, dt=BF16):
        return nc.dram_tensor(name, list(shape), dt, kind=dkind).ap()

    W = {}
    wshape = dict(w_in=(D, DIN), w_pa=(1024, D), w_pb=(1024, D), w_o=(D, D), w_cq=(D, 512),
                  w_ckv=(D, 1024), w_co=(512, D), w_up=(D, DFF), w_down=(DFF, D))
    WB = {}
    WBUF = {}
    for k, s in wshape.items():
        W[k] = din(k, s)
        WB[k] = nc.dram_tensor("b_" + k, list(s), BF16, kind="Internal").ap()
        WBUF[k] = P.buf("wb_" + k)
    gcols = {k: din(k, (128, 16)) for k in ("g_mix_c", "g_cross_c", "g_mem_c", "g_mlp_c")}
    gq_bc_d = din("gq_bc", (128, 128))
    gk_bc_d = din("gk_bc", (128, 128))
    gfin_d = din("gfin_bc", (128, D))
    ident_d = din("ident", (128, 128))
    sel_d = din("sel", (128, 4))
    rpbx_d = din("rpb_exp", (128, 8 * 16 * 64))
    mfull_d = din("mask_full", (128, 16 * 64))
    mint_d = din("mask_int", (128, 16 * 64))
    tabs_d = dscr("tabs", (10, 128, 8192))
    J = cfg.jobs
    for jb in J:
        n, S, T = jb["n"], jb["S"], jb["T"]
        jb["x_seq"] = din(f"x{n}_seq", (S, D))
        jb["x_own"] = din(f"x{n}_own", (T, D))
        jb["x_halo"] = din(f"x{n}_halo", (512, D))
        jb["mem"] = din(f"mem{n}", (NMEM, D))
        jb["rk_c"] = din(f"rk{n}_c", (S, 128))
        jb["rk_s"] = din(f"rk{n}_s", (S, 128))
        jb["rq_c"] = din(f"rq{n}_c", (T, 128))
        jb["rq_s"] = din(f"rq{n}_s", (T, 128))
        jb["y"] = nc.dram_tensor(f"y{n}", [T, D], F32, kind="ExternalOutput").ap()
        jb["KbT"] = dscr(f"KbT{n}", (256, S))
        jb["Vb"] = dscr(f"Vb{n}", (S, 256))
        jb["QaT"] = dscr(f"QaT{n}", (1024, T))
        jb["KaT"] = dscr(f"KaT{n}", (1024, T + 512))
        jb["Va"] = dscr(f"Va{n}", (T + 512, 1024))
        jb["QbT"] = dscr(f"QbT{n}", (1024, T))
        jb["sgT"] = dscr(f"sgT{n}", (4096, T))
        jb["OaT"] = dscr(f"OaT{n}", (1024, T))
        jb["ObT"] = dscr(f"ObT{n}", (1024, T))

    def sb(name, shape, dt):
        return st.enter_context(nc.sbuf_tensor("s_" + name, list(shape), dt))

    ARENA = 186 * 1024
    arena = sb("arena", [128, ARENA], U8)

    def carve(off, shape, dt):
        esz = 4 if dt == F32 else 2
        n = int(np.prod(shape[1:]))
        assert off % 32 == 0 and off + n * esz <= ARENA, (off, shape)
        v = arena[:, off:off + n * esz].bitcast(dt)
        if len(shape) == 3:
            v = v.rearrange("p (a b) -> p a b", b=shape[2])
        elif len(shape) == 4:
            v = v.rearrange("p (a b c) -> p a b c", b=shape[2], c=shape[3])
        return v, off + n * esz

    ident = sb("ident", [128, 128], BF16)
    gcol = {k: sb(k, [128, 16], F32) for k in gcols}
    gq_bc = sb("gq_bc", [128, 128], F32)
    gk_bc = sb("gk_bc", [128, 128], F32)
    sel = sb("sel", [128, 4], F32)
    nsel = sb("nsel", [128, 4], F32)
    stat = sb("stat", [128, 64], F32)
    epsb = sb("epsb", [128, 1], F32)
    rden = sb("rden", [128, 8], F32)
    b_rden = P.buf("rden")
    KcT = [sb(f"KcT{i}", [128, 4, NMEM], BF16) for i in range(2)]
    Vc1 = [sb(f"Vc1{i}", [128, 2, 4, 136], BF16) for i in range(2)]
    b_const = P.buf("const")
    b_stat = P.bufs(8, "stat")
    b_kc = P.bufs(2, "kc")

    psbig = [st.enter_context(nc.psum_tensor(f"ps{i}", [128, 1024], F32)) for i in range(4)]
    psum = [psbig[i // 2][:, (i % 2) * 512:(i % 2 + 1) * 512] for i in range(8)]
    b_ps = P.bufs(8, "ps")

    def psb(i):
        return psum[i].bitcast(BF16)

    def mm(out, lhsT, rhs, start, stop, reads, writes, skip=False):
        if skip:
            P.add("pe", lambda e: e.matmul(out, lhsT, rhs, start=start, stop=stop, skip_group_check=True), reads, writes)
        else:
            P.add("pe", lambda e: e.matmul(out, lhsT, rhs, start=start, stop=stop), reads, writes)

    def tr(out, in_, reads, writes):
        P.add("pe", lambda e: e.transpose(out, in_, ident[:]), list(reads) + [b_const], writes)

    def act(out, in_, func, reads, writes, bias=0.0, scale=1.0, accum=None):
        if accum is None:
            P.add("act", lambda e: e.activation(out, in_, func, bias=bias, scale=scale), reads, writes)
        else:
            P.add("act", lambda e: e.activation(out, in_, func, bias=bias, scale=scale, accum_out=accum), reads, writes)

    def ts(eng, out, in0, s1, s2, op0, op1, reads, writes):
        if op1 is None:
            P.add(eng, lambda e: e.tensor_scalar(out, in0, s1, s2, op0), reads, writes)
        else:
            P.add(eng, lambda e: e.tensor_scalar(out, in0, s1, s2, op0, op1), reads, writes)

    def tt(eng, out, in0, in1, op, reads, writes):
        P.add(eng, lambda e: e.tensor_tensor(out, in0, in1, op), reads, writes)

    def stt(eng, out, in0, scalar, in1, op0, op1, reads, writes):
        P.add(eng, lambda e: e.scalar_tensor_tensor(out, in0, scalar, in1, op0, op1), reads, writes)

    def cp(eng, out, in_, reads, writes):
        if eng == "act":
            P.add("act", lambda e: e.copy(out, in_), reads, writes)
        else:
            P.add(eng, lambda e: e.tensor_copy(out, in_), reads, writes)

    def memset(eng, ap, val, writes):
        P.add(eng, lambda e: e.memset(ap, val), (), writes)

    def dma(q, out, in_, key, reads=(), writes=()):
        if q == "sp!":
            q = "sp"
        elif q == "sp" and len(writes) == 0:
            q = "pool"
        P.add(q, lambda e: e.dma_start(out=out, in_=in_), reads, writes, dma=key)

    def run_pipe(jobs, ahead=2):
        L = [i for i, j in enumerate(jobs) if j[0] is not None]
        slots = {}
        nl = 0
        seen = 0
        for i, (lf, cf) in enumerate(jobs):
            if lf is not None:
                seen += 1
            while nl < len(L) and nl < seen + ahead:
                slots[L[nl]] = jobs[L[nl]][0]()
                nl += 1
            cf(slots.get(i))

    dma("pool", ident[:], ident_d, "c_id", writes=[b_const])
    for k in gcols:
        dma("sp", gcol[k][:], gcols[k], "c_" + k, writes=[b_const])
    dma("sp", gq_bc[:], gq_bc_d, "c_gq", writes=[b_const])
    dma("sp", gk_bc[:], gk_bc_d, "c_gk", writes=[b_const])
    dma("sp", sel[:], sel_d, "c_sel", writes=[b_const])
    memset("dve", epsb[:], EPS, [b_const])
    ts("dve", nsel[:], sel[:], -1.0, 1.0, ALU.mult, ALU.add, [b_const], [b_stat[7]])
    def cast_weights(names):
        for k in names:
            rows = wshape[k][0]
            step = 256
            for r0 in range(0, rows, step):
                dma("pool", WB[k][r0:r0 + step, :], W[k][r0:r0 + step, :], "cast_" + k, writes=[WBUF[k]])
    WB["w_kv"] = nc.dram_tensor("b_w_kv", [D, 512], BF16, kind="Internal").ap()
    WBUF["w_kv"] = P.buf("wb_w_kv")
    dma("pool", WB["w_kv"], W["w_in"][:, 4096:4608], "cast_w_kv", writes=[WBUF["w_kv"]])
    cast_weights(["w_ckv"])

    off = 0
    slab = []
    for i in range(3):
        v, off = carve(off, [128, 16, 512], BF16)
        slab.append(v)
    hT = []
    for i in range(2):
        v, off = carve(off, [128, 16, 512], BF16)
        hT.append(v)
    xres, off = carve(off, [128, 4, D], F32)
    xnb = []
    for i in range(2):
        v, off = carve(off, [128, D], BF16)
        xnb.append(v)
    OFF_COMMON = off
    stg = []
    for i in range(3):
        v, off = carve(off, [128, 4, 512], BF16)
        stg.append(v)
    ropeC, off = carve(off, [128, 4, 128], F32)
    ropeS, off = carve(off, [128, 4, 128], F32)
    kn, off = carve(off, [128, 4, 128], F32)
    t1, off = carve(off, [128, 4, 128], F32)
    t2, off = carve(off, [128, 4, 128], F32)
    krb = []
    for i in range(2):
        v, off = carve(off, [128, 4, 128], BF16)
        krb.append(v)
    OFF_P13 = off
    b_slab = P.bufs(3, "slab")
    b_hT = P.bufs(2, "hT")
    b_x = P.bufs(4, "x")
    b_xnb = P.bufs(2, "xn")
    b_stg = P.bufs(3, "stg")
    b_rope = P.buf("rope")
    b_kn, b_t1, b_t2 = P.bufs(3, "ropetmp")
    b_krb = P.bufs(2, "krb")
    cnt = dict(slab=0, stg=0, hT=0, ps=0, st=0, xn=0)

    def next_slab():
        i = cnt["slab"] % 3
        cnt["slab"] += 1
        return i

    def load_slab(wname, k0, nk, c0, ncols=512):
        i = next_slab()
        src = WB[wname][k0 * 128:(k0 + nk) * 128, c0:c0 + ncols].rearrange("(c p) n -> p c n", p=128)
        dma("sp", slab[i][:, 0:nk, 0:ncols], src, f"slab{i}", reads=[WBUF[wname]], writes=[b_slab[i]])
        return i

    def next_ps(lo=2, n=4):
        i = lo + cnt["ps"] % n
        cnt["ps"] += 1
        return i

    def next_stg():
        i = cnt["stg"] % 3
        cnt["stg"] += 1
        return i

    def rstd_from_ss(ss_ap, out_ap, n, width, b):
        act(out_ap, ss_ap, AF.Sqrt, [b, b_const], [b], bias=epsb[:], scale=1.0 / width)
        P.add("dve", lambda e: e.reciprocal(out_ap, out_ap), [b], [b])

    def norm_transpose(x_src, g, hbuf, keep_x):
        for m in range(4):
            if x_src is not None:
                dma("sp", xres[:, m, :], x_src[m * 128:(m + 1) * 128, :], f"x{m}", writes=[b_x[m]])
            sbf = b_stat[m % 4]
            ss = stat[:, m:m + 1]
            xi = cnt["xn"] % 2
            cnt["xn"] += 1
            xn, b_xn = xnb[xi], b_xnb[xi]
            act(xn[:], xres[:, m, :], AF.Square, [b_x[m]], [b_xn, sbf], accum=ss)
            rstd_from_ss(ss, ss, 1, D, sbf)
            ts("dve", xn[:], xres[:, m, :], ss, None, ALU.mult, None, [b_x[m], sbf], [b_xn])
            pv = [psb(0), psb(1)]
            for c in range(16):
                tr(pv[c // 8][:, (c % 8) * 128:(c % 8 + 1) * 128], xn[:, c * 128:(c + 1) * 128], [b_xn], [b_ps[c // 8]])
            for hf in range(2):
                tt("dve", hT[hbuf][:, hf * 8:(hf + 1) * 8, m * 128:(m + 1) * 128],
                   pv[hf].rearrange("p (c t) -> p c t", t=128),
                   gcol[g][:, hf * 8:(hf + 1) * 8].unsqueeze(2).to_broadcast([128, 8, 128]),
                   ALU.mult, [b_ps[hf], b_const], [b_hT[hbuf]])

    def load_rope(c_src, s_src):
        dma("sp", ropeC[:], c_src.rearrange("(m p) d -> p m d", p=128), "ropeC", writes=[b_rope])
        dma("sp", ropeS[:], s_src.rearrange("(m p) d -> p m d", p=128), "ropeS", writes=[b_rope])

    def nr_compute(ps_i, nh, col0, m, g_bc, kb):
        sbf = b_stat[4 + kb]
        ssq = stat[:, 8 + 8 * kb: 8 + 8 * kb + nh]
        for h in range(nh):
            act(t1[:, h, :], psum[ps_i][:, col0 + h * 128: col0 + (h + 1) * 128], AF.Square,
                [b_ps[ps_i]], [b_t1, sbf], accum=ssq[:, h:h + 1])
        rstd_from_ss(ssq, ssq, nh, 128, sbf)
        for h in range(nh):
            stt("dve", kn[:, h, :], psum[ps_i][:, col0 + h * 128: col0 + (h + 1) * 128], ssq[:, h:h + 1], g_bc[:],
                ALU.mult, ALU.mult, [b_ps[ps_i], sbf, b_const], [b_kn])
        Cb = ropeC[:, m, :].unsqueeze(1).to_broadcast([128, nh, 128])
        tt("dve", t1[:, 0:nh, :], kn[:, 0:nh, :], Cb, ALU.mult, [b_kn, b_rope], [b_t1])
        knv = kn[:, 0:nh, :].rearrange("p h (a b c) -> p h a b c", a=2, b=2)
        t2v = t2[:, 0:nh, :].rearrange("p h (a b c) -> p h a b c", a=2, b=2)
        Sv = ropeS[:, m, :].rearrange("p (a b c) -> p a b c", a=2, b=2)
        for hf in range(2):
            tt("pool", t2v[:, :, :, hf, :], knv[:, :, :, 1 - hf, :], Sv[:, :, hf, :].unsqueeze(1).to_broadcast([128, nh, 2, 32]),
               ALU.mult, [b_kn, b_rope], [b_t2])
        tt("dve", krb[kb][:, 0:nh, :], t1[:, 0:nh, :], t2[:, 0:nh, :], ALU.add, [b_t1, b_t2], [b_krb[kb]])

    def nr_transpose(nh, m, kb, dst, dst_b):
        pb = psb(6 + kb)
        for h in range(nh):
            tr(pb[:, h * 128:(h + 1) * 128], krb[kb][:, h, :], [b_krb[kb]], [b_ps[6 + kb]])
        cp("dve", dst[:, 0:nh, m * 128:(m + 1) * 128], pb[:, 0:nh * 128].rearrange("p (h t) -> p h t", t=128),
           [b_ps[6 + kb]], [dst_b])

    cnt["kr"] = 0

    def norm_rope_T(ps_i, nh, col0, m, g_bc, dst, dst_b):
        kb = cnt["kr"] % 2
        cnt["kr"] += 1
        nr_compute(ps_i, nh, col0, m, g_bc, kb)
        return lambda: nr_transpose(nh, m, kb, dst, dst_b)

    o2 = 0
    tA, o2 = carve(o2, [128, 8192], F32)
    tB, o2 = carve(o2, [128, 8, 1024], BF16)
    tC, o2 = carve(o2, [128, 8, 1024], BF16)
    tD, o2 = carve(o2, [128, 8, 1024], BF16)
    tE, o2 = carve(o2, [128, 8, 1024], BF16)
    tM, o2 = carve(o2, [128, 2, 1024], F32)
    b_tA, b_tB, b_tC, b_tD, b_tE, b_tM = P.bufs(6, "tab")
    dma("sp", tA[:], rpbx_d, "tA", writes=[b_tA])
    dma("sp", tM[:, 0, :], mfull_d, "tM", writes=[b_tM])
    dma("sp", tM[:, 1, :], mint_d, "tM", writes=[b_tM])
    act(tA[:], tA[:], AF.Exp, [b_tA], [b_tA])
    tA3 = tA.rearrange("p (h x) -> p h x", h=8)
    tt("dve", tB[:], tA3, tM[:, 0, :].unsqueeze(1).to_broadcast([128, 8, 1024]), ALU.mult, [b_tA, b_tM], [b_tB])
    tt("dve", tC[:], tA3, tM[:, 1, :].unsqueeze(1).to_broadcast([128, 8, 1024]), ALU.mult, [b_tA, b_tM], [b_tC])
    tt("dve", tD[:], tB[:], tC[:], ALU.subtract, [b_tB, b_tC], [b_tD])
    dma("sp", tabs_d[0].rearrange("p (h x) -> p h x", h=8), tC[:], "tabst", reads=[b_tC])
    dma("sp", tabs_d[1].rearrange("p (h x) -> p h x", h=8), tB[:], "tabst", reads=[b_tB])
    for q in range(4):
        stt("dve", tE[:], tD[:], sel[:, q:q + 1], tC[:], ALU.mult, ALU.add, [b_tD, b_tC, b_const], [b_tE])
        dma("sp", tabs_d[2 + q * 2 + 1].rearrange("p (h x) -> p h x", h=8), tE[:], "tabsE", reads=[b_tE])
        ts("dve", tE[:], tC[:], nsel[:, q:q + 1], None, ALU.mult, None, [b_tC, b_stat[7]], [b_tE])
        dma("sp", tabs_d[2 + q * 2 + 0].rearrange("p (h x) -> p h x", h=8), tE[:], "tabsE", reads=[b_tE])

    P.barrier()
    for ji, jb in enumerate(J):
        for half in range(2):
            src = jb["mem"][half * 128:(half + 1) * 128, :]
            dma("sp", xres[:, half, :], src, f"x{half}", writes=[b_x[half]])
            sbf = b_stat[half]
            ss = stat[:, half:half + 1]
            xn, b_xn = xnb[half], b_xnb[half]
            act(xn[:], xres[:, half, :], AF.Square, [b_x[half]], [b_xn, sbf], accum=ss)
            rstd_from_ss(ss, ss, 1, D, sbf)
            ts("dve", xn[:], xres[:, half, :], ss, None, ALU.mult, None, [b_x[half], sbf], [b_xn])
            pv = [psb(0), psb(1)]
            for c in range(16):
                tr(pv[c // 8][:, (c % 8) * 128:(c % 8 + 1) * 128], xn[:, c * 128:(c + 1) * 128], [b_xn], [b_ps[c // 8]])
            for hf in range(2):
                tt("dve", hT[0][:, hf * 8:(hf + 1) * 8, half * 128:(half + 1) * 128],
                   pv[hf].rearrange("p (c t) -> p c t", t=128),
                   gcol["g_mem_c"][:, hf * 8:(hf + 1) * 8].unsqueeze(2).to_broadcast([128, 8, 128]),
                   ALU.mult, [b_ps[hf], b_const], [b_hT[0]])
        si = load_slab("w_ckv", 0, 16, 0)
        for h in range(4):
            pi = next_ps()
            for c in range(16):
                mm(psum[pi][:, 0:NMEM], slab[si][:, c, h * 128:(h + 1) * 128], hT[0][:, c, 0:NMEM], c == 0, c == 15,
                   [b_slab[si], b_hT[0]], [b_ps[pi]])
            cp("dve", KcT[ji][:, h, :], psum[pi][:, 0:NMEM], [b_ps[pi]], [b_kc[ji]])
        si = load_slab("w_ckv", 0, 16, 512)
        memset("dve", Vc1[ji][:, :, :, 128:129], 1.0, [b_kc[ji]])
        for half in range(2):
            pi = next_ps()
            for c in range(16):
                mm(psum[pi][:], hT[0][:, c, half * 128:(half + 1) * 128], slab[si][:, c, :], c == 0, c == 15,
                   [b_slab[si], b_hT[0]], [b_ps[pi]])
            cp("dve", Vc1[ji][:, half, :, 0:128], psum[pi][:].rearrange("p (h d) -> p h d", d=128), [b_ps[pi]], [b_kc[ji]])

    o2 = OFF_P13
    kvslab, o2 = carve(o2, [128, 16, 512], BF16)
    kT, o2 = carve(o2, [128, 2, 512], BF16)
    vt, o2 = carve(o2, [128, 4, 256], BF16)
    b_kvslab, b_kT, b_vt = P.bufs(3, "kv")
    dma("sp", kvslab[:], WB["w_kv"].rearrange("(c p) n -> p c n", p=128), "kvslab",
        reads=[WBUF["w_kv"]], writes=[b_kvslab])
    deferred_cast = [True]
    cnt["tA"] = 0

    def norm_transpose_m(x_rows, g, hbuf, m):
        dma("sp", xres[:, m, :], x_rows, f"x{m}", writes=[b_x[m]])
        sbf = b_stat[m % 4]
        ss = stat[:, m:m + 1]
        xi = cnt["xn"] % 2
        cnt["xn"] += 1
        xn, b_xn = xnb[xi], b_xnb[xi]
        act(xn[:], xres[:, m, :], AF.Square, [b_x[m]], [b_xn, sbf], accum=ss)
        rstd_from_ss(ss, ss, 1, D, sbf)
        act(xn[:], xres[:, m, :], AF.Copy, [b_x[m], sbf], [b_xn], scale=ss)
        bk = 2 * (cnt["tA"] % 2)
        cnt["tA"] += 1
        pv = [psb(bk), psb(bk + 1)]
        for c in range(16):
            tr(pv[c // 8][:, (c % 8) * 128:(c % 8 + 1) * 128], xn[:, c * 128:(c + 1) * 128], [b_xn], [b_ps[bk + c // 8]])

        def evac():
            for hf in range(2):
                tt("dve", hT[hbuf][:, hf * 8:(hf + 1) * 8, m * 128:(m + 1) * 128],
                   pv[hf].rearrange("p (c t) -> p c t", t=128),
                   gcol[g][:, hf * 8:(hf + 1) * 8].unsqueeze(2).to_broadcast([128, 8, 128]),
                   ALU.mult, [b_ps[bk + hf], b_const], [b_hT[hbuf]])
        return evac

    kTb, vtb = [kT], [vt]
    b_kTb, b_vtb = [b_kT], [b_vt]
    v_, o2 = carve(o2, [128, 2, 512], BF16)
    kTb.append(v_)
    v_, o2 = carve(o2, [128, 4, 256], BF16)
    vtb.append(v_)
    b_kTb.append(P.buf("kT1"))
    b_vtb.append(P.buf("vt1"))
    ropeCb, ropeSb, b_ropeb = [ropeC], [ropeS], [b_rope]
    v_, o2 = carve(o2, [128, 4, 128], F32)
    ropeCb.append(v_)
    v_, o2 = carve(o2, [128, 4, 128], F32)
    ropeSb.append(v_)
    b_ropeb.append(P.buf("rope1"))
    units = []
    for jb in J:
        for t in range(jb["S"] // 512):
            for m in range(4):
                units.append((jb, t, m))
    ust = {}

    def st_A(u):
        jb, t, m = units[u]
        if m == 0:
            ust[(id(jb), t)] = dict(hb=cnt["hT"] % 2, tb=(cnt["hT"] // 1) % 2)
            cnt["hT"] += 1
            tb = ust[(id(jb), t)]["tb"]
            dma("sp", ropeCb[tb][:], jb["rk_c"][t * 512:(t + 1) * 512, :].rearrange("(m p) d -> p m d", p=128), f"ropeC{tb}",
                writes=[b_ropeb[tb]])
            dma("sp", ropeSb[tb][:], jb["rk_s"][t * 512:(t + 1) * 512, :].rearrange("(m p) d -> p m d", p=128), f"ropeS{tb}",
                writes=[b_ropeb[tb]])
        us = ust[(id(jb), t)]
        return norm_transpose_m(jb["x_seq"][t * 512 + m * 128:t * 512 + (m + 1) * 128, :], "g_mix_c", us["hb"], m)

    def st_B1(u):
        jb, t, m = units[u]
        us = ust[(id(jb), t)]
        hb = us["hb"]
        pi = next_ps(4, 2)
        us[("pi", m)] = pi
        for c in range(16):
            mm(psum[pi][:], hT[hb][:, c, m * 128:(m + 1) * 128], kvslab[:, c, :], c == 0, c == 15,
               [b_kvslab, b_hT[hb]], [b_ps[pi]])

    def st_B2(u):
        nonlocal ropeC, ropeS, b_rope
        jb, t, m = units[u]
        us = ust[(id(jb), t)]
        tb = us["tb"]
        pi = us[("pi", m)]
        cp("act", vtb[tb][:, m, :], psum[pi][:, 256:512], [b_ps[pi]], [b_vtb[tb]])
        ropeC, ropeS, b_rope = ropeCb[tb], ropeSb[tb], b_ropeb[tb]
        us[("post", m)] = norm_rope_T(pi, 2, 0, m, gk_bc, kTb[tb], b_kTb[tb])
        ropeC, ropeS, b_rope = ropeCb[0], ropeSb[0], b_ropeb[0]

    def st_C(u):
        jb, t, m = units[u]
        us = ust[(id(jb), t)]
        tb = us["tb"]
        us[("post", m)]()
        if m == 3:
            dma("sp", jb["KbT"][:, t * 512:(t + 1) * 512].rearrange("(g p) s -> p g s", p=128), kTb[tb][:], f"kTst{tb}", reads=[b_kTb[tb]])
            dma("sp", jb["Vb"][t * 512:(t + 1) * 512, :].rearrange("(m p) c -> p m c", p=128), vtb[tb][:], f"vtst{tb}", reads=[b_vtb[tb]])

    NU = len(units)
    cast_list = []
    for k in ("w_in",):
        for r0 in range(0, wshape[k][0], 256):
            cast_list.append((k, r0))
    every = 2
    ci = 0
    for i in range(NU + 3):
        if 0 <= i - 2 < NU:
            st_B1(i - 2)
        ev_a = st_A(i) if i < NU else None
        if 0 <= i - 2 < NU:
            st_B2(i - 2)
        if 0 <= i - 3 < NU:
            st_C(i - 3)
        if ev_a is not None:
            ev_a()
        if i >= 4 and (i - 4) % every == 0 and ci < len(cast_list):
            k, r0 = cast_list[ci]
            ci += 1
            dma("pool", WB[k][r0:r0 + 256, :], W[k][r0:r0 + 256, :], "cast_" + k, writes=[WBUF[k]])
    while ci < len(cast_list):
        k, r0 = cast_list[ci]
        ci += 1
        dma("pool", WB[k][r0:r0 + 256, :], W[k][r0:r0 + 256, :], "cast_" + k, writes=[WBUF[k]])

    def fm_slab(si, hb, evac):
        for cc in range(4):
            pi = next_ps()
            for c in range(16):
                mm(psum[pi][:], slab[si][:, c, cc * 128:(cc + 1) * 128], hT[hb][:, c, :], c == 0, c == 15,
                   [b_slab[si], b_hT[hb]], [b_ps[pi]])
            evac(cc, pi)

    def tm_slab(si, hb, evac, nk=16, src=None, src_b=None):
        src = hT[hb] if src is None else src
        src_b = [b_hT[hb]] if src_b is None else (src_b if isinstance(src_b, list) else [src_b])
        pend = None
        for m in range(4):
            pi = next_ps()
            for c in range(nk):
                mm(psum[pi][:], src[:, c, m * 128:(m + 1) * 128], slab[si][:, c, :], c == 0, c == nk - 1,
                   [b_slab[si]] + src_b, [b_ps[pi]])
            if pend is not None:
                pend()
            pend = evac(m, pi)
        if pend is not None:
            pend()

    ev_rr = [0]

    def evac_copy(dst, src, rd, wr):
        e = ("act", "dve")[ev_rr[0] % 2]
        ev_rr[0] += 1
        cp(e, dst, src, rd, wr)

    def p1b_body(si, s, jb, kind, t, hb):
        T = jb["T"]
        gi = next_stg()

        def store_fm(dst_rows, col0, ncol=512):
            dma("sp", dst_rows[:, col0:col0 + ncol].rearrange("(c p) s -> p c s", p=128), stg[gi][:, :, 0:ncol],
                f"stg{gi}", reads=[b_stg[gi]])

        if s in (0, 1, 2, 3):
            def ev(cc, pi):
                evac_copy(stg[gi][:, cc, :], psum[pi][:], [b_ps[pi]], [b_stg[gi]])
            fm_slab(si, hb, ev)
            if s < 2:
                store_fm(jb["QaT"][s * 512:(s + 1) * 512, :], t * 512)
            elif kind == "own":
                store_fm(jb["KaT"][(s - 2) * 512:(s - 1) * 512, :], 256 + t * 512)
            else:
                dma("sp", jb["KaT"][(s - 2) * 512:(s - 1) * 512, 0:256].rearrange("(c p) s -> p c s", p=128),
                    stg[gi][:, :, 0:256], f"stg{gi}", reads=[b_stg[gi]])
                dma("sp", jb["KaT"][(s - 2) * 512:(s - 1) * 512, 256 + T:512 + T].rearrange("(c p) s -> p c s", p=128),
                    stg[gi][:, :, 256:512], f"stg{gi}", reads=[b_stg[gi]])
        elif s in (4, 5):
            def ev(m, pi):
                evac_copy(stg[gi][:, m, :], psum[pi][:], [b_ps[pi]], [b_stg[gi]])
            tm_slab(si, hb, ev)
            cs = (s - 4) * 512
            if kind == "own":
                dma("sp", jb["Va"][256 + t * 512:256 + (t + 1) * 512, cs:cs + 512].rearrange("(m p) c -> p m c", p=128),
                    stg[gi][:], f"stg{gi}", reads=[b_stg[gi]])
            else:
                dma("sp", jb["Va"][0:256, cs:cs + 512].rearrange("(m p) c -> p m c", p=128),
                    stg[gi][:, 0:2, :], f"stg{gi}", reads=[b_stg[gi]])
                dma("sp", jb["Va"][256 + T:512 + T, cs:cs + 512].rearrange("(m p) c -> p m c", p=128),
                    stg[gi][:, 2:4, :], f"stg{gi}", reads=[b_stg[gi]])
        elif s in (6, 7):
            def ev(m, pi):
                return norm_rope_T(pi, 4, 0, m, gq_bc, stg[gi], b_stg[gi])
            tm_slab(si, hb, ev)
            store_fm(jb["QbT"][(s - 6) * 512:(s - 5) * 512, :], t * 512)
        else:
            def ev(cc, pi):
                act(stg[gi][:, cc, :], psum[pi][:], AF.Sigmoid, [b_ps[pi]], [b_stg[gi]])
            fm_slab(si, hb, ev)
            store_fm(jb["sgT"][(s - 9) * 512:(s - 8) * 512, :], t * 512)

    per_tile = []
    for jb in J:
        T = jb["T"]
        for kind, t in [("own", t) for t in range(T // 512)] + [("halo", 0)]:
            ctx = {}

            def prep(_slot, jb=jb, kind=kind, t=t, ctx=ctx):
                ctx["hb"] = cnt["hT"] % 2
                cnt["hT"] += 1
                if kind == "own":
                    xs = jb["x_own"][t * 512:(t + 1) * 512, :]
                    load_rope(jb["rq_c"][t * 512:(t + 1) * 512, :], jb["rq_s"][t * 512:(t + 1) * 512, :])
                else:
                    xs = jb["x_halo"]
                norm_transpose(xs, "g_mix_c", ctx["hb"], False)

            slabs = [0, 1, 2, 3, 4, 5, 6, 7, 9, 10, 11, 12, 13, 14, 15, 16] if kind == "own" else [2, 3, 4, 5]
            sj = [((lambda s=s: load_slab("w_in", 0, 16, s * 512)),
                   (lambda si, s=s, jb=jb, kind=kind, t=t, ctx=ctx: p1b_body(si, s, jb, kind, t, ctx["hb"]))) for s in slabs]
            per_tile.append(((None, prep), sj))
    jobs = []
    carry = []
    for (pj, sj) in per_tile:
        jobs.append(pj)
        jobs.extend(carry)
        k = max(0, len(sj) - 4)
        jobs.extend(sj[:k])
        carry = sj[k:]
    jobs.extend(carry)
    late_casts = []
    for k in ("w_pa", "w_pb", "w_o", "w_cq", "w_co", "w_up", "w_down"):
        for r0 in range(0, wshape[k][0], 256):
            late_casts.append((k, r0))
    nslabjobs = sum(1 for j in jobs if j[0] is not None)
    stride = max(1, (nslabjobs - 8) // len(late_casts))
    jobs2 = []
    seen = 0
    for j in jobs:
        jobs2.append(j)
        if j[0] is not None:
            seen += 1
            if seen % stride == 0 and late_casts:
                k, r0 = late_casts.pop(0)
                jobs2.append((None, (lambda _s, k=k, r0=r0: dma("pool", WB[k][r0:r0 + 256, :], W[k][r0:r0 + 256, :],
                                                                "cast_" + k, writes=[WBUF[k]]))))
    for (k, r0) in late_casts:
        jobs2.append((None, (lambda _s, k=k, r0=r0: dma("pool", WB[k][r0:r0 + 256, :], W[k][r0:r0 + 256, :],
                                                        "cast_" + k, writes=[WBUF[k]]))))
    run_pipe(jobs2)
    P.barrier()

    o2 = 0
    pT = []
    for i in range(8):
        v, o2 = carve(o2, [128, 2, 512], BF16)
        pT.append(v)
    tsum = []
    for i in range(6):
        v, o2 = carve(o2, [128, 1024], BF16)
        tsum.append(v)
    b_tsum = P.bufs(6, "tsum")
    cnt["ts"] = 0
    dacc = []
    for i in range(4):
        v, o2 = carve(o2, [128, 1024], F32)
        dacc.append(v)
    dtot, o2 = carve(o2, [128, 512], F32)
    drec, o2 = carve(o2, [128, 512], F32)
    ones16, o2 = carve(o2, [128, 128], BF16)
    dhi, o2 = carve(o2, [128, 512], BF16)
    dlo, o2 = carve(o2, [128, 512], BF16)
    b_dhl = P.buf("dhl")
    oT, o2 = carve(o2, [128, 2, 512], BF16)
    b_pT = P.bufs(8, "pT")
    b_dacc = P.bufs(4, "dacc")
    b_dtot, b_drec, b_ones = P.bufs(3, "den")
    b_oT = P.bufs(2, "oT")
    OFF_ATT = (o2 + 31) // 32 * 32
    cnt["pT"] = 0
    cnt["oT"] = 0
    cnt["stb"] = 0
    cnt["hd"] = 0
    STB = [[0, 1], [2, 3], [4, 5]]
    OACC = [6, 7]
    DENB = [0]
    memset("dve", ones16[:], 1.0, [b_ones])

    def den_matmul(src_ap, src_b):
        cp("dve", dhi[:], src_ap, src_b, [b_dhl])
        tt("dve", dlo[:], src_ap, dhi[:], ALU.subtract, list(src_b) + [b_dhl], [b_dhl])
        db = DENB[0]
        mm(psum[db][:], ones16[:], dhi[:], True, False, [b_ones, b_dhl], [b_ps[db]])
        mm(psum[db][:], ones16[:], dlo[:], False, True, [b_ones, b_dhl], [b_ps[db]])

    def att_finish(par, dstT, row0, col0, used_pool, used_dve=True):
        src = par
        if used_pool and used_dve:
            tt("dve", dacc[par][:], dacc[par][:], dacc[2 + par][:], ALU.add, [b_dacc[par], b_dacc[2 + par]], [b_dacc[par]])
        elif used_pool:
            src = 2 + par
        tt("dve", dtot[:], dacc[src][:, 0:512], dacc[src][:, 512:1024], ALU.add, [b_dacc[src]], [b_dtot])
        den_matmul(dtot[:], [b_dtot])
        P.add("dve", lambda e, db=DENB[0]: e.reciprocal(drec[:], psum[db][:]), [b_ps[DENB[0]]], [b_drec])
        oi = cnt["oT"] % 2
        cnt["oT"] += 1
        tt("dve", oT[:, oi, :], psum[OACC[par]][:], drec[:], ALU.mult, [b_ps[OACC[par]], b_drec], [b_oT[oi]])
        dma("sp!", dstT[row0:row0 + 128, col0:col0 + 512], oT[:, oi, :], f"oT{oi}", reads=[b_oT[oi]])

    for jb in J:
        S, T = jb["S"], jb["T"]
        NCH = S // 128
        NP = NCH // 2
        o3 = OFF_ATT
        KTg, o3 = carve(o3, [128, S], BF16)
        Vg, o3 = carve(o3, [128, NCH, 128], BF16)
        qb, o3 = carve(o3, [128, 2, 512], BF16)
        b_KTg, b_Vg = P.bufs(2, "kvg")
        b_qb = P.bufs(2, "qb")
        for g in range(2):
            dma("sp", KTg[:], jb["KbT"][g * 128:(g + 1) * 128, :], "KTg", writes=[b_KTg])
            dma("sp", Vg[:], jb["Vb"][:, g * 128:(g + 1) * 128].rearrange("(c p) d -> p c d", p=128), "V1g",
                writes=[b_Vg])
            heads = [(qt, hh) for qt in range(T // 512) for hh in range(4)]
            steps = [(hi, p) for hi in range(len(heads)) for p in range(NP)]
            hstate = {}

            def stage_a(st_):
                hi, p = st_
                if p == 0:
                    par = cnt["hd"] % 2
                    cnt["hd"] += 1
                    qi = par
                    qt, hh = heads[hi]
                    h = g * 4 + hh
                    dma("sp", qb[:, qi, :], jb["QbT"][h * 128:(h + 1) * 128, qt * 512:(qt + 1) * 512], f"qb{qi}", writes=[b_qb[qi]])
                    hstate[hi] = dict(par=par, qi=qi, h=h, qt=qt, npool=0, ndve=0)
                hs = hstate[hi]
                sb_ = STB[cnt["stb"] % 3]
                cnt["stb"] += 1
                pi = cnt["pT"] % 8
                cnt["pT"] += 1
                for k in range(2):
                    ch = p * 2 + k
                    mm(psum[sb_[k]][:], KTg[:, ch * 128:(ch + 1) * 128], qb[:, hs["qi"], :], True, True,
                       [b_KTg, b_qb[hs["qi"]]], [b_ps[sb_[k]]])
                act(pT[pi][:].rearrange("p k s -> p (k s)"), psbig[sb_[0] // 2][:], AF.Exp,
                    [b_ps[sb_[0]], b_ps[sb_[1]]], [b_pT[pi]], scale=SCALE)
                return pi

            def stage_b(st_, pi):
                hi, p = st_
                hs = hstate[hi]
                par = hs["par"]
                for k in range(2):
                    ch = p * 2 + k
                    mm(psum[OACC[par]][:], Vg[:, ch, :], pT[pi][:, k, :], ch == 0, ch == NCH - 1,
                       [b_pT[pi], b_Vg], [b_ps[OACC[par]]])
                pflat = pT[pi][:].rearrange("p k s -> p (k s)")
                if p % 2 == 0:
                    hs["prev_pi"] = pi
                else:
                    ppi = hs["prev_pi"]
                    ti = cnt["ts"] % 6
                    cnt["ts"] += 1
                    tt("dve", tsum[ti][:], pT[ppi][:].rearrange("p k s -> p (k s)"), pflat, ALU.add,
                       [b_pT[ppi], b_pT[pi]], [b_tsum[ti]])
                    if p % 4 == 1:
                        hs["prev_ti"] = ti
                    else:
                        pti = hs["prev_ti"]
                        t2i = cnt["ts"] % 6
                        cnt["ts"] += 1
                        tt("dve", tsum[t2i][:], tsum[pti][:], tsum[ti][:], ALU.add, [b_tsum[pti], b_tsum[ti]], [b_tsum[t2i]])
                        if (p // 4) % 4 == 3:
                            e_, ai, key = "dve", par, "ndve"
                        else:
                            e_, ai, key = "pool", 2 + par, "npool"
                        if hs[key] == 0:
                            cp(e_, dacc[ai][:], tsum[t2i][:], [b_tsum[t2i]], [b_dacc[ai]])
                        else:
                            tt(e_, dacc[ai][:], dacc[ai][:], tsum[t2i][:], ALU.add, [b_tsum[t2i], b_dacc[ai]], [b_dacc[ai]])
                        hs[key] += 1
                if p == NP - 1:
                    att_finish(par, jb["ObT"], hs["h"] * 128, hs["qt"] * 512, hs["npool"] > 0, hs["ndve"] > 0)

            pend = []
            for st_ in steps:
                pi = stage_a(st_)
                pend.append((st_, pi))
                if len(pend) > 2:
                    stage_b(*pend.pop(0))
            while pend:
                stage_b(*pend.pop(0))
        P.barrier()

    DENB[0] = 4
    o3 = OFF_ATT
    tabI, o3 = carve(o3, [128, 8, 16, 64], BF16)
    tabX, o3 = carve(o3, [128, 4, 8 * 16 * 64], BF16)
    qa, o3 = carve(o3, [128, 8, 512], BF16)
    ka, o3 = carve(o3, [128, 8, 1024], BF16)
    Va_, o3 = carve(o3, [128, 8, 1024], BF16)
    b_tabI, b_tabX, b_qa, b_ka, b_Va = P.bufs(5, "na")
    dma("sp", tabI[:].rearrange("p h s q -> p (h s q)"), tabs_d[0], "tabI", writes=[b_tabI])
    tabXv = tabX.rearrange("p v (h s q) -> p v h s q", h=8, s=16)
    for ji, jb in enumerate(J):
        T = jb["T"]
        nblk = T // 512
        for v in range(4):
            cls, var = v // 2, v % 2
            dma("sp", tabX[:, v, :], tabs_d[2 + (ji * 2 + cls) * 2 + var], "tabX", writes=[b_tabX])
        for b in range(nblk):
            segs_all = na_segments(b == 0, b == nblk - 1)
            dma("sp", qa[:], jb["QaT"][:, b * 512:(b + 1) * 512].rearrange("(h p) s -> p h s", p=128), "qa", writes=[b_qa])
            dma("sp", ka[:], jb["KaT"][:, b * 512:b * 512 + 1024].rearrange("(h p) s -> p h s", p=128), "ka", writes=[b_ka])
            dma("sp", Va_[:], jb["Va"][b * 512:b * 512 + 1024, :].rearrange("(c p) x -> p c x", p=128), "V1a", writes=[b_Va])
            steps = [(h, j) for h in range(8) for j in range(8)]
            hstate = {}

            def na_a(st_):
                h, j = st_
                if j == 0:
                    par = cnt["hd"] % 2
                    cnt["hd"] += 1
                    hstate[h] = dict(par=par)
                m_lo, m_hi, segs = segs_all[j]
                c0, c1 = m_lo * 128, (m_hi + 1) * 128
                sbk = STB[cnt["stb"] % 2][cnt["stb"] // 2 % 2]
                cnt["stb"] += 1
                pi = cnt["pT"] % 8
                cnt["pT"] += 1
                mm(psum[sbk][:, c0:c1], ka[:, h, j * 128:(j + 1) * 128], qa[:, h, c0:c1], True, True,
                   [b_ka, b_qa], [b_ps[sbk]])
                act(pT[pi][:, 0, c0:c1], psum[sbk][:, c0:c1], AF.Exp, [b_ps[sbk]], [b_pT[pi]], scale=SCALE)
                eng_ = "pool" if (j % 2) else "dve"
                for (ra, rb, tb, s0) in segs:
                    n = rb - ra
                    if tb == "int":
                        tv = tabI[:, h, s0:s0 + n, :]
                        tbuf = b_tabI
                    else:
                        vi = {"first0": 0, "first1": 1, "last0": 2, "last1": 3}[tb]
                        tv = tabXv[:, vi, h, s0:s0 + n, :]
                        tbuf = b_tabX
                    pv_ = pT[pi][:, 0, ra * 64:rb * 64].rearrange("p (r q) -> p r q", q=64)
                    tt(eng_, pv_, pv_, tv, ALU.mult, [b_pT[pi], tbuf], [b_pT[pi]])
                return pi

            def na_b(st_, pi):
                h, j = st_
                par = hstate[h]["par"]
                m_lo, m_hi, segs = segs_all[j]
                c0, c1 = m_lo * 128, (m_hi + 1) * 128
                mm(psum[OACC[par]][:, c0:c1], Va_[:, j, h * 128:(h + 1) * 128], pT[pi][:, 0, c0:c1], j == 0, j == 7,
                   [b_pT[pi], b_Va], [b_ps[OACC[par]]], skip=True)
                dn = 4 + par
                mm(psum[dn][:, c0:c1], ones16[:], pT[pi][:, 0, c0:c1], j == 0, j == 7,
                   [b_pT[pi], b_ones], [b_ps[dn]], skip=True)
                if j == 7:
                    P.add("dve", lambda e, dn=dn: e.reciprocal(drec[:], psum[dn][:]), [b_ps[dn]], [b_drec])
                    oi = cnt["oT"] % 2
                    cnt["oT"] += 1
                    tt("dve", oT[:, oi, :], psum[OACC[par]][:], drec[:], ALU.mult, [b_ps[OACC[par]], b_drec], [b_oT[oi]])
                    dma("sp!", jb["OaT"][h * 128:(h + 1) * 128, b * 512:(b + 1) * 512], oT[:, oi, :], f"oT{oi}", reads=[b_oT[oi]])

            pend = []
            for st_ in steps:
                pi = na_a(st_)
                pend.append((st_, pi))
                if len(pend) > 6:
                    na_b(*pend.pop(0))
            while pend:
                na_b(*pend.pop(0))
    P.barrier()

    o2 = OFF_COMMON
    aT = []
    o_a = o2
    for i in range(2):
        v, o2 = carve(o2, [128, 16, 512], BF16)
        aT.append(v)
    OabT, o_a = carve(o_a, [128, 16, 512], BF16)
    sgs = []
    for i in range(2):
        v, o_a = carve(o_a, [128, 8, 512], BF16)
        sgs.append(v)
    tmpA, o2 = carve(o2, [128, 512], F32)
    tmpB, o2 = carve(o2, [128, 512], F32)
    qcT, o2 = carve(o2, [128, 4, 512], BF16)
    ocs, o2 = carve(o2, [128, 4, 512], BF16)
    ocT, o2 = carve(o2, [128, 4, 512], BF16)
    pTc, o2 = carve(o2, [128, 2, 512], BF16)
    gfin, o2 = carve(o2, [128, D], F32)
    b_Oab = P.buf("Oab")
    b_sgs = P.bufs(2, "sgs")
    b_aT = [[b_Oab], [b_sgs[0], b_sgs[1]]]
    b_tmpA, b_tmpB, b_qcT, b_ocs, b_ocT, b_gfin = P.bufs(6, "p3")
    b_pTc = P.bufs(2, "pTc")
    dma("sp", gfin[:], gfin_d, "gfin", writes=[b_gfin])

    def resid_add(m, pi, n):
        tt("dve", xres[:, m, n * 512:(n + 1) * 512], xres[:, m, n * 512:(n + 1) * 512], psum[pi][:], ALU.add,
           [b_ps[pi], b_x[m]], [b_x[m]])

    def p3_tile_jobs(ji, jb, t):
        tsl = slice(t * 512, (t + 1) * 512)
        ctx = {}
        jobs = []

        def xload(_):
            for m in range(4):
                dma("sp", xres[:, m, :], jb["x_own"][t * 512 + m * 128:t * 512 + (m + 1) * 128, :], f"x{m}", writes=[b_x[m]])

        def start(_):
            dma("sp", OabT[:, 0:8, :], jb["OaT"][:, tsl].rearrange("(c p) s -> p c s", p=128), "Oab", writes=[b_Oab])
            dma("sp", OabT[:, 8:16, :], jb["ObT"][:, tsl].rearrange("(c p) s -> p c s", p=128), "Oab", writes=[b_Oab])
            ctx["mixb"] = cnt["hT"] % 2
            cnt["hT"] += 1
        jobs.append((None, start))

        def ld_merge(n):
            si = next_slab()
            dma("sp", slab[si][:, 0:8, :], WB["w_pa"][:, n * 512:(n + 1) * 512].rearrange("(c p) n -> p c n", p=128),
                f"slab{si}", reads=[WBUF["w_pa"]], writes=[b_slab[si]])
            dma("sp", slab[si][:, 8:16, :], WB["w_pb"][:, n * 512:(n + 1) * 512].rearrange("(c p) n -> p c n", p=128),
                f"slab{si}", reads=[WBUF["w_pb"]], writes=[b_slab[si]])
            return si

        def merge(si, n):
            mixb = ctx["mixb"]
            gi = n % 2
            dma("sp", sgs[gi][:, 0:4, :], jb["sgT"][n * 512:(n + 1) * 512, tsl].rearrange("(c p) s -> p c s", p=128),
                f"sgs{gi}", writes=[b_sgs[gi]])
            dma("sp", sgs[gi][:, 4:8, :], jb["sgT"][2048 + n * 512:2048 + (n + 1) * 512, tsl].rearrange("(c p) s -> p c s", p=128),
                f"sgs{gi}", writes=[b_sgs[gi]])
            for cc in range(4):
                pa = next_ps()
                for c in range(8):
                    mm(psum[pa][:], slab[si][:, c, cc * 128:(cc + 1) * 128], OabT[:, c, :], c == 0, c == 7,
                       [b_slab[si], b_Oab], [b_ps[pa]])
                pbk = next_ps()
                for c in range(8):
                    mm(psum[pbk][:], slab[si][:, 8 + c, cc * 128:(cc + 1) * 128], OabT[:, 8 + c, :], c == 0, c == 7,
                       [b_slab[si], b_Oab], [b_ps[pbk]])
                tt("dve", tmpA[:], psum[pa][:], sgs[gi][:, cc, :], ALU.mult, [b_ps[pa], b_sgs[gi]], [b_tmpA])
                tt("dve", tmpB[:], psum[pbk][:], sgs[gi][:, 4 + cc, :], ALU.mult, [b_ps[pbk], b_sgs[gi]], [b_tmpB])
                tt("pool", hT[mixb][:, n * 4 + cc, :], tmpA[:], tmpB[:], ALU.add, [b_tmpA, b_tmpB], [b_hT[mixb]])
        for n in range(4):
            jobs.append(((lambda n=n: ld_merge(n)), (lambda si, n=n: merge(si, n))))
        jobs.insert(3, (None, xload))
        for n in range(4):
            jobs.append(((lambda n=n: load_slab("w_o", 0, 16, n * 512)),
                         (lambda si, n=n: tm_slab(si, ctx["mixb"], lambda m, pi: resid_add(m, pi, n)))))

        def cq(si):
            hb = cnt["hT"] % 2
            cnt["hT"] += 1
            norm_transpose(None, "g_cross_c", hb, True)

            def ev_q(cc, pi):
                cp("act", qcT[:, cc, :], psum[pi][:], [b_ps[pi]], [b_qcT])
            fm_slab(si, hb, ev_q)
            for h in range(4):
                for k in range(2):
                    pi = next_ps()
                    mm(psum[pi][:], KcT[ji][:, h, k * 128:(k + 1) * 128], qcT[:, h, :], True, True, [b_kc[ji], b_qcT], [b_ps[pi]])
                    act(pTc[:, k, :], psum[pi][:], AF.Exp, [b_ps[pi]], [b_pTc[0]], scale=SCALE)
                for m in range(4):
                    pi = 6 + m % 2
                    for k in range(2):
                        mm(psum[pi][:, 0:129], pTc[:, k, m * 128:(m + 1) * 128], Vc1[ji][:, k, h, 0:129], k == 0, k == 1,
                           [b_pTc[0], b_kc[ji]], [b_ps[pi]])
                    P.add("dve", lambda e, m=m, pi=pi: e.reciprocal(rden[:, m:m + 1], psum[pi][:, 128:129]), [b_ps[pi]], [b_rden])
                    ts("dve", ocs[:, m, h * 128:(h + 1) * 128], psum[pi][:, 0:128], rden[:, m:m + 1], None, ALU.mult, None,
                       [b_ps[pi], b_rden], [b_ocs])
            for m in range(4):
                pb = psb(6 + m % 2)
                for h in range(4):
                    tr(pb[:, h * 128:(h + 1) * 128], ocs[:, m, h * 128:(h + 1) * 128], [b_ocs], [b_ps[6 + m % 2]])
                cp("dve", ocT[:, :, m * 128:(m + 1) * 128], pb[:, 0:512].rearrange("p (h t) -> p h t", t=128),
                   [b_ps[6 + m % 2]], [b_ocT])
        jobs.append(((lambda: load_slab("w_cq", 0, 16, 0)), cq))
        for n in range(4):
            jobs.append(((lambda n=n: load_slab("w_co", 0, 4, n * 512)),
                         (lambda si, n=n: tm_slab(si, None, lambda m, pi: resid_add(m, pi, n), nk=4, src=ocT, src_b=b_ocT))))

        def mlp_norm(_):
            ctx["hb"] = cnt["hT"] % 2
            cnt["hT"] += 1
            norm_transpose(None, "g_mlp_c", ctx["hb"], True)
        jobs.append((None, mlp_norm))

        def up(si, n, ab):
            def ev_up(cc, pi):
                act(tmpA[:], psum[pi][:], AF.Relu, [b_ps[pi]], [b_tmpA])
                tt("dve", aT[ab][:, n * 4 + cc, :], tmpA[:], tmpA[:], ALU.mult, [b_tmpA], b_aT[ab])
            fm_slab(si, ctx["hb"], ev_up)
        for kg in range(4):
            ab = kg % 2
            for n in range(4):
                jobs.append(((lambda kg=kg, n=n: load_slab("w_up", 0, 16, (kg * 4 + n) * 512)),
                             (lambda si, n=n, ab=ab: up(si, n, ab))))
            for n in range(4):
                jobs.append(((lambda kg=kg, n=n: load_slab("w_down", kg * 16, 16, n * 512)),
                             (lambda si, n=n, ab=ab: tm_slab(si, None, lambda m, pi: resid_add(m, pi, n), src=aT[ab], src_b=b_aT[ab]))))

        def fin(_):
            for m in range(4):
                sbf = b_stat[m % 4]
                ss = stat[:, m:m + 1]
                xn, b_xn = xnb[m % 2], b_xnb[m % 2]
                act(xn[:], xres[:, m, :], AF.Square, [b_x[m]], [b_xn, sbf], accum=ss)
                rstd_from_ss(ss, ss, 1, D, sbf)
                stt("dve", xres[:, m, :], xres[:, m, :], ss, gfin[:], ALU.mult, ALU.mult, [b_x[m], sbf, b_gfin], [b_x[m]])
                dma("sp", jb["y"][t * 512 + m * 128:t * 512 + (m + 1) * 128, :], xres[:, m, :], f"xo{m}", reads=[b_x[m]])
        jobs.append((None, fin))
        return jobs

    jobs = []
    for ji, jb in enumerate(J):
        for t in range(jb["T"] // 512):
            jobs.extend(p3_tile_jobs(ji, jb, t))
    run_pipe(jobs)

    P.emit()
    st.close()
    return nc, P


def _rope_tables(S):
    t = np.arange(S)
    row = (t // 64).astype(np.float32)
    col = (t % 64).astype(np.float32)
    inv = (10000.0 ** (-np.arange(32, dtype=np.float32) / 32)).astype(np.float32)
    ar = row[:, None] * inv[None, :]
    ac = col[:, None] * inv[None, :]
    C = np.concatenate([np.cos(ar), np.cos(ar), np.cos(ac), np.cos(ac)], axis=1).astype(np.float32)
    Sn = np.concatenate([-np.sin(ar), np.sin(ar), -np.sin(ac), np.sin(ac)], axis=1).astype(np.float32)
    return C, Sn


def _na_consts(rpb):
    rp = np.asarray(rpb, np.float32)[0]
    e = np.arange(128) // 64
    kc = np.arange(128) % 64
    s = np.arange(16)
    qc = np.arange(64)
    ap = s[None, :] - e[:, None]
    a = 14 - ap
    slot_ok = (ap >= 0) & (ap <= 14)
    dc = kc[:, None] - qc[None, :]
    cs = np.clip(qc - 8, 0, 48)
    col_ok = (kc[:, None] >= cs[None, :]) & (kc[:, None] < cs[None, :] + 16)
    bi = np.clip(dc + 15, 0, 30)
    A = np.clip(a, 0, 14)
    g = rp[:, A[:, :, None], bi[:, None, :]]
    ok = slot_ok[:, :, None] & col_ok[:, None, :]
    g = np.where(ok[None], g, 0.0).astype(np.float32)
    rpb_exp = np.ascontiguousarray(g.transpose(1, 0, 2, 3)).reshape(128, 8 * 16 * 64)
    mfull = ok.astype(np.float32).reshape(128, 1024)
    mint = (ok & ((ap >= 4) & (ap <= 11))[:, :, None]).astype(np.float32).reshape(128, 1024)
    return rpb_exp, mfull, mint


def _gcol(g):
    return np.ascontiguousarray(np.asarray(g, np.float32).reshape(16, 128).T)


_CACHE = {}


def _run(inputs, S_p, S_s, debug=False):
    cfg = Cfg(S_p, S_s)
    key = (S_p, S_s, debug)
    if key not in _CACHE:
        _CACHE[key] = build(cfg, debug)
    nc, P = _CACHE[key]
    f = lambda k: np.asarray(inputs[k], np.float32)
    xp, xs, mp, ms = f("x_prompt"), f("x_sample"), f("mem_prompt"), f("mem_sample")
    shared = {k: np.ascontiguousarray(f(k)[0]) for k in ("w_in", "w_pa", "w_pb", "w_o", "w_cq", "w_ckv", "w_co", "w_up", "w_down")}
    shared["g_mix_c"] = _gcol(f("g_mix")[0])
    shared["g_cross_c"] = _gcol(f("g_cross")[0])
    shared["g_mem_c"] = _gcol(f("g_mem")[0])
    shared["g_mlp_c"] = _gcol(f("g_mlp")[0])
    shared["gq_bc"] = np.ascontiguousarray(np.broadcast_to(f("g_q")[0][None, :], (128, 128)))
    shared["gk_bc"] = np.ascontiguousarray(np.broadcast_to(f("g_k")[0][None, :], (128, 128)))
    shared["gfin_bc"] = np.ascontiguousarray(np.broadcast_to(f("g_final")[None, :], (128, D)))
    shared["ident"] = np.eye(128, dtype=np.float32)
    shared["rpb_exp"], shared["mask_full"], shared["mask_int"] = _na_consts(f("rpb"))
    Cp, Sp = _rope_tables(S_p)
    Cs, Ss = _rope_tables(S_s)
    Tp, Ts = S_p // 2, S_s // 8
    zeros256 = np.zeros((256, D), np.float32)
    in_maps = []
    for c in range(8):
        b, hf = c // 2, c % 2
        m = dict(shared)
        o0 = hf * Tp
        m["xp_seq"] = np.ascontiguousarray(xp[b])
        m["xp_own"] = np.ascontiguousarray(xp[b, o0:o0 + Tp])
        pre = xp[b, o0 - 256:o0] if o0 > 0 else zeros256
        post = xp[b, o0 + Tp:o0 + Tp + 256] if o0 + Tp < S_p else zeros256
        m["xp_halo"] = np.ascontiguousarray(np.concatenate([pre, post], 0))
        m["memp"] = np.ascontiguousarray(mp[b])
        m["rkp_c"], m["rkp_s"] = Cp, Sp
        m["rqp_c"], m["rqp_s"] = np.ascontiguousarray(Cp[o0:o0 + Tp]), np.ascontiguousarray(Sp[o0:o0 + Tp])
        s0 = c * Ts
        m["xs_seq"] = np.ascontiguousarray(xs[0])
        m["xs_own"] = np.ascontiguousarray(xs[0, s0:s0 + Ts])
        pre = xs[0, s0 - 256:s0] if s0 > 0 else zeros256
        post = xs[0, s0 + Ts:s0 + Ts + 256] if s0 + Ts < S_s else zeros256
        m["xs_halo"] = np.ascontiguousarray(np.concatenate([pre, post], 0))
        m["mems"] = np.ascontiguousarray(ms[0])
        m["rks_c"], m["rks_s"] = Cs, Ss
        m["rqs_c"], m["rqs_s"] = np.ascontiguousarray(Cs[s0:s0 + Ts]), np.ascontiguousarray(Ss[s0:s0 + Ts])
        selv = np.array([hf == 0, hf == 1, c == 0, c == 7], np.float32)
        m["sel"] = np.ascontiguousarray(np.broadcast_to(selv[None, :], (128, 4)))
        in_maps.append(m)
    res = run_bass_kernel_spmd(nc, in_maps, core_ids=list(range(8)))
    yp = np.zeros((4, S_p, D), np.float32)
    ys = np.zeros((1, S_s, D), np.float32)
    for c in range(8):
        b, hf = c // 2, c % 2
        yp[b, hf * Tp:(hf + 1) * Tp] = res.results[c]["yp"]
        ys[0, c * Ts:(c + 1) * Ts] = res.results[c]["ys"]
    return (yp, ys), res


def kernel(**inputs):
    (yp, ys), _ = _run(inputs, 8192, 16384)
    return (yp, ys)
```

```python
import numpy as np
from contextlib import ExitStack
import concourse.bass as bass
import concourse.mybir as mybir
from concourse.bass_utils import run_bass_kernel_spmd

F32 = mybir.dt.float32
BF16 = mybir.dt.bfloat16
U8 = mybir.dt.uint8
ALU = mybir.AluOpType
AF = mybir.ActivationFunctionType

D = 2048
DIN = 8704
DFF = 8192
NMEM = 256
EPS = 1e-6
SCALE = 128 ** -0.5


class Buf:
    __slots__ = ("name", "writers", "readers")

    def __init__(self, name):
        self.name = name
        self.writers = {}
        self.readers = {}


class Op:
    __slots__ = ("eng", "fn", "waits", "signal", "sigval", "idx", "dma")

    def __init__(self, eng, fn):
        self.eng = eng
        self.fn = fn
        self.waits = []
        self.signal = False
        self.sigval = None
        self.idx = None
        self.dma = None


ENGS = ("pe", "act", "dve", "pool", "sp")
STRICT = ("act", "dve", "pool")


class Prog:
    def __init__(self, nc):
        self.nc = nc
        self.ops = {e: [] for e in ENGS}
        self.waited = {e: {} for e in ENGS}
        self.dma_count = {}

    def buf(self, name="b"):
        return Buf(name)

    def bufs(self, n, name="b"):
        return [Buf(f"{name}{i}") for i in range(n)]

    def add(self, eng, fn, reads=(), writes=(), dma=None):
        mykey = dma if dma is not None else eng
        deps = []
        for b in reads:
            for k, t in b.writers.items():
                if k != mykey or (dma is None and eng in STRICT):
                    deps.append(t)
        for b in writes:
            for k, t in b.writers.items():
                if k != mykey:
                    deps.append(t)
            for k, t in b.readers.items():
                if k != mykey:
                    deps.append(t)
        op = Op(eng, fn)
        w = self.waited[eng]
        for t in deps:
            if t[0] == "c":
                p = t[1]
                if w.get(p.eng, -1) >= p.idx:
                    continue
                w[p.eng] = p.idx
                p.signal = True
                op.waits.append(t)
            else:
                sk = t[1]
                v = self.dma_count[sk]
                if w.get(sk, 0) >= v:
                    continue
                w[sk] = v
                op.waits.append(("d", sk, v))
        op.idx = len(self.ops[eng])
        self.ops[eng].append(op)
        if dma is not None:
            self.dma_count[dma] = self.dma_count.get(dma, 0) + 16
            op.dma = (dma, self.dma_count[dma])
            tok = ("d", dma, self.dma_count[dma])
        else:
            tok = ("c", op)
        for b in reads:
            b.readers[mykey] = tok
        for b in writes:
            b.writers = {mykey: tok}
            b.readers = {}
        return op

    def barrier(self):
        lasts = {}
        for e in ("pe", "act", "dve", "pool"):
            cl = [o for o in self.ops[e] if o.dma is None and o.fn is not None]
            if cl:
                lasts[e] = cl[-1]
                cl[-1].signal = True
        for f in ENGS:
            op = Op(f, None)
            w = self.waited[f]
            for e, l in lasts.items():
                if e != f and w.get(e, -1) < l.idx:
                    w[e] = l.idx
                    op.waits.append(("c", l))
            for sk, v in self.dma_count.items():
                if w.get(sk, 0) < v:
                    w[sk] = v
                    op.waits.append(("d", sk, v))
            op.idx = len(self.ops[f])
            self.ops[f].append(op)

    def emit(self):
        nc = self.nc
        self.barrier()
        for e in ENGS:
            c = 0
            for op in self.ops[e]:
                if op.signal and op.dma is None:
                    c += 1
                    op.sigval = c
        with ExitStack() as st:
            esem = {e: st.enter_context(nc.semaphore(f"s_{e}")) for e in ENGS}
            dsem = {}
            for i, sk in enumerate(self.dma_count):
                dsem[sk] = st.enter_context(nc.semaphore(f"d_{i}"))
            block = st.enter_context(nc.Block())

            def run(e, eng):
                for op in self.ops[e]:
                    for t in op.waits:
                        if t[0] == "c":
                            eng.wait_ge(esem[t[1].eng], t[1].sigval)
                        else:
                            eng.wait_ge(dsem[t[1]], t[2])
                    if op.fn is None:
                        continue
                    ins = op.fn(eng)
                    if op.dma is not None:
                        ins.then_inc(dsem[op.dma[0]], 16)
                    elif op.signal:
                        ins.then_inc(esem[e], 1)

            block.tensor(lambda eng: run("pe", eng))
            block.scalar(lambda eng: run("act", eng))
            block.vector(lambda eng: run("dve", eng))
            block.gpsimd(lambda eng: run("pool", eng))
            block.sync(lambda eng: run("sp", eng))
        self.stats = {e: len(self.ops[e]) for e in ENGS}


def na_segments(first, last):
    out = []
    for j in range(8):
        m_lo, m_hi = max(0, j - 4), min(3, j)
        if 2 <= j <= 5:
            if first:
                m_lo = 0
            if last:
                m_hi = 3
        r_lo, r_hi = 2 * m_lo, 2 * m_hi + 2
        segs = []
        r = r_lo
        while r < r_hi:
            if first and r < 4:
                e_ = min(4, r_hi)
                tb = "first1" if 2 <= j <= 5 else "first0"
            elif last and r >= 5:
                e_ = r_hi
                tb = "last1" if 2 <= j <= 5 else "last0"
            else:
                e_ = r_hi
                if last:
                    e_ = min(e_, 5)
                tb = "int"
            segs.append((r, e_, tb, 11 - 2 * j + r))
            r = e_
        for (a, b, tb, s0) in segs:
            assert 0 <= s0 and s0 + (b - a) <= 16, (j, segs)
        out.append((m_lo, m_hi, segs))
    return out


class Cfg:
    def __init__(self, S_p, S_s):
        self.jobs = [dict(n="p", S=S_p, T=S_p // 2), dict(n="s", S=S_s, T=S_s // 8)]


def build(cfg, debug=False):
    nc = bass.Bass("TRN2", target_bir_lowering=False)
    P = Prog(nc)
    st = ExitStack()
    dkind = "ExternalOutput" if debug else "Internal"

    def din(name, shape, dt=F32):
        return nc.dram_tensor(name, list(shape), dt, kind="ExternalInput").ap()

    def dscr(name, shape, dt=BF16):
        return nc.dram_tensor(name, list(shape), dt, kind=dkind).ap()

    W = {}
    wshape = dict(w_in=(D, DIN), w_pa=(1024, D), w_pb=(1024, D), w_o=(D, D), w_cq=(D, 512),
                  w_ckv=(D, 1024), w_co=(512, D), w_up=(D, DFF), w_down=(DFF, D))
    WB = {}
    WBUF = {}
    for k, s in wshape.items():
        W[k] = din(k, s)
        WB[k] = nc.dram_tensor("b_" + k, list(s), BF16, kind="Internal").ap()
        WBUF[k] = P.buf("wb_" + k)
    gcols = {k: din(k, (128, 16)) for k in ("g_mix_c", "g_cross_c", "g_mem_c", "g_mlp_c")}
    gq_bc_d = din("gq_bc", (128, 128))
    gk_bc_d = din("gk_bc", (128, 128))
    gfin_d = din("gfin_bc", (128, D))
    ident_d = din("ident", (128, 128))
    sel_d = din("sel", (128, 4))
    rpbx_d = din("rpb_exp", (128, 8 * 16 * 64))
    mfull_d = din("mask_full", (128, 16 * 64))
    mint_d = din("mask_int", (128, 16 * 64))
    tabs_d = dscr("tabs", (10, 128, 8192))
    J = cfg.jobs
    for jb in J:
        n, S, T = jb["n"], jb["S"], jb["T"]
        jb["x_seq"] = din(f"x{n}_seq", (S, D))
        jb["x_own"] = din(f"x{n}_own", (T, D))
        jb["x_halo"] = din(f"x{n}_halo", (512, D))
        jb["mem"] = din(f"mem{n}", (NMEM, D))
        jb["rk_c"] = din(f"rk{n}_c", (S, 128))
        jb["rk_s"] = din(f"rk{n}_s", (S, 128))
        jb["rq_c"] = din(f"rq{n}_c", (T, 128))
        jb["rq_s"] = din(f"rq{n}_s", (T, 128))
        jb["y"] = nc.dram_tensor(f"y{n}", [T, D], F32, kind="ExternalOutput").ap()
        jb["KbT"] = dscr(f"KbT{n}", (256, S))
        jb["Vb"] = dscr(f"Vb{n}", (S, 256))
        jb["QaT"] = dscr(f"QaT{n}", (1024, T))
        jb["KaT"] = dscr(f"KaT{n}", (1024, T + 512))
        jb["Va"] = dscr(f"Va{n}", (T + 512, 1024))
        jb["QbT"] = dscr(f"QbT{n}", (1024, T))
        jb["sgT"] = dscr(f"sgT{n}", (4096, T))
        jb["OaT"] = dscr(f"OaT{n}", (1024, T))
        jb["ObT"] = dscr(f"ObT{n}", (1024, T))

    def sb(name, shape, dt):
        return st.enter_context(nc.sbuf_tensor("s_" + name, list(shape), dt))

    ARENA = 186 * 1024
    arena = sb("arena", [128, ARENA], U8)

    def carve(off, shape, dt):
        esz = 4 if dt == F32 else 2
        n = int(np.prod(shape[1:]))
        assert off % 32 == 0 and off + n * esz <= ARENA, (off, shape)
        v = arena[:, off:off + n * esz].bitcast(dt)
        if len(shape) == 3:
            v = v.rearrange("p (a b) -> p a b", b=shape[2])
        elif len(shape) == 4:
            v = v.rearrange("p (a b c) -> p a b c", b=shape[2], c=shape[3])
        return v, off + n * esz

    ident = sb("ident", [128, 128], BF16)
    gcol = {k: sb(k, [128, 16], F32) for k in gcols}
    gq_bc = sb("gq_bc", [128, 128], F32)
    gk_bc = sb("gk_bc", [128, 128], F32)
    sel = sb("sel", [128, 4], F32)
    nsel = sb("nsel", [128, 4], F32)
    stat = sb("stat", [128, 64], F32)
    epsb = sb("epsb", [128, 1], F32)
    rden = sb("rden", [128, 8], F32)
    b_rden = P.buf("rden")
    KcT = [sb(f"KcT{i}", [128, 4, NMEM], BF16) for i in range(2)]
    Vc1 = [sb(f"Vc1{i}", [128, 2, 4, 136], BF16) for i in range(2)]
    b_const = P.buf("const")
    b_stat = P.bufs(8, "stat")
    b_kc = P.bufs(2, "kc")

    psbig = [st.enter_context(nc.psum_tensor(f"ps{i}", [128, 1024], F32)) for i in range(4)]
    psum = [psbig[i // 2][:, (i % 2) * 512:(i % 2 + 1) * 512] for i in range(8)]
    b_ps = P.bufs(8, "ps")

    def psb(i):
        return psum[i].bitcast(BF16)

    def mm(out, lhsT, rhs, start, stop, reads, writes, skip=False):
        if skip:
            P.add("pe", lambda e: e.matmul(out, lhsT, rhs, start=start, stop=stop, skip_group_check=True), reads, writes)
        else:
            P.add("pe", lambda e: e.matmul(out, lhsT, rhs, start=start, stop=stop), reads, writes)

    def tr(out, in_, reads, writes):
        P.add("pe", lambda e: e.transpose(out, in_, ident[:]), list(reads) + [b_const], writes)

    def act(out, in_, func, reads, writes, bias=0.0, scale=1.0, accum=None):
        if accum is None:
            P.add("act", lambda e: e.activation(out, in_, func, bias=bias, scale=scale), reads, writes)
        else:
            P.add("act", lambda e: e.activation(out, in_, func, bias=bias, scale=scale, accum_out=accum), reads, writes)

    def ts(eng, out, in0, s1, s2, op0, op1, reads, writes):
        if op1 is None:
            P.add(eng, lambda e: e.tensor_scalar(out, in0, s1, s2, op0), reads, writes)
        else:
            P.add(eng, lambda e: e.tensor_scalar(out, in0, s1, s2, op0, op1), reads, writes)

    def tt(eng, out, in0, in1, op, reads, writes):
        P.add(eng, lambda e: e.tensor_tensor(out, in0, in1, op), reads, writes)

    def stt(eng, out, in0, scalar, in1, op0, op1, reads, writes):
        P.add(eng, lambda e: e.scalar_tensor_tensor(out, in0, scalar, in1, op0, op1), reads, writes)

    def cp(eng, out, in_, reads, writes):
        if eng == "act":
            P.add("act", lambda e: e.copy(out, in_), reads, writes)
        else:
            P.add(eng, lambda e: e.tensor_copy(out, in_), reads, writes)

    def memset(eng, ap, val, writes):
        P.add(eng, lambda e: e.memset(ap, val), (), writes)

    def dma(q, out, in_, key, reads=(), writes=()):
        if q == "sp!":
            q = "sp"
        elif q == "sp" and len(writes) == 0:
            q = "pool"
        P.add(q, lambda e: e.dma_start(out=out, in_=in_), reads, writes, dma=key)

    def run_pipe(jobs, ahead=2):
        L = [i for i, j in enumerate(jobs) if j[0] is not None]
        slots = {}
        nl = 0
        seen = 0
        for i, (lf, cf) in enumerate(jobs):
            if lf is not None:
                seen += 1
            while nl < len(L) and nl < seen + ahead:
                slots[L[nl]] = jobs[L[nl]][0]()
                nl += 1
            cf(slots.get(i))

    dma("pool", ident[:], ident_d, "c_id", writes=[b_const])
    for k in gcols:
        dma("sp", gcol[k][:], gcols[k], "c_" + k, writes=[b_const])
    dma("sp", gq_bc[:], gq_bc_d, "c_gq", writes=[b_const])
    dma("sp", gk_bc[:], gk_bc_d, "c_gk", writes=[b_const])
    dma("sp", sel[:], sel_d, "c_sel", writes=[b_const])
    memset("dve", epsb[:], EPS, [b_const])
    ts("dve", nsel[:], sel[:], -1.0, 1.0, ALU.mult, ALU.add, [b_const], [b_stat[7]])
    def cast_weights(names):
        for k in names:
            rows = wshape[k][0]
            step = 256
            for r0 in range(0, rows, step):
                dma("pool", WB[k][r0:r0 + step, :], W[k][r0:r0 + step, :], "cast_" + k, writes=[WBUF[k]])
    WB["w_kv"] = nc.dram_tensor("b_w_kv", [D, 512], BF16, kind="Internal").ap()
    WBUF["w_kv"] = P.buf("wb_w_kv")
    dma("pool", WB["w_kv"], W["w_in"][:, 4096:4608], "cast_w_kv", writes=[WBUF["w_kv"]])
    cast_weights(["w_ckv"])

    off = 0
    slab = []
    for i in range(3):
        v, off = carve(off, [128, 16, 512], BF16)
        slab.append(v)
    hT = []
    for i in range(2):
        v, off = carve(off, [128, 16, 512], BF16)
        hT.append(v)
    xres, off = carve(off, [128, 4, D], F32)
    xnb = []
    for i in range(2):
        v, off = carve(off, [128, D], BF16)
        xnb.append(v)
    OFF_COMMON = off
    stg = []
    for i in range(3):
        v, off = carve(off, [128, 4, 512], BF16)
        stg.append(v)
    ropeC, off = carve(off, [128, 4, 128], F32)
    ropeS, off = carve(off, [128, 4, 128], F32)
    kn, off = carve(off, [128, 4, 128], F32)
    t1, off = carve(off, [128, 4, 128], F32)
    t2, off = carve(off, [128, 4, 128], F32)
    krb = []
    for i in range(2):
        v, off = carve(off, [128, 4, 128], BF16)
        krb.append(v)
    OFF_P13 = off
    b_slab = P.bufs(3, "slab")
    b_hT = P.bufs(2, "hT")
    b_x = P.bufs(4, "x")
    b_xnb = P.bufs(2, "xn")
    b_stg = P.bufs(3, "stg")
    b_rope = P.buf("rope")
    b_kn, b_t1, b_t2 = P.bufs(3, "ropetmp")
    b_krb = P.bufs(2, "krb")
    cnt = dict(slab=0, stg=0, hT=0, ps=0, st=0, xn=0)

    def next_slab():
        i = cnt["slab"] % 3
        cnt["slab"] += 1
        return i

    def load_slab(wname, k0, nk, c0, ncols=512):
        i = next_slab()
        src = WB[wname][k0 * 128:(k0 + nk) * 128, c0:c0 + ncols].rearrange("(c p) n -> p c n", p=128)
        dma("sp", slab[i][:, 0:nk, 0:ncols], src, f"slab{i}", reads=[WBUF[wname]], writes=[b_slab[i]])
        return i

    def next_ps(lo=2, n=4):
        i = lo + cnt["ps"] % n
        cnt["ps"] += 1
        return i

    def next_stg():
        i = cnt["stg"] % 3
        cnt["stg"] += 1
        return i

    def rstd_from_ss(ss_ap, out_ap, n, width, b):
        act(out_ap, ss_ap, AF.Sqrt, [b, b_const], [b], bias=epsb[:], scale=1.0 / width)
        P.add("dve", lambda e: e.reciprocal(out_ap, out_ap), [b], [b])

    def norm_transpose(x_src, g, hbuf, keep_x):
        for m in range(4):
            if x_src is not None:
                dma("sp", xres[:, m, :], x_src[m * 128:(m + 1) * 128, :], f"x{m}", writes=[b_x[m]])
            sbf = b_stat[m % 4]
            ss = stat[:, m:m + 1]
            xi = cnt["xn"] % 2
            cnt["xn"] += 1
            xn, b_xn = xnb[xi], b_xnb[xi]
            act(xn[:], xres[:, m, :], AF.Square, [b_x[m]], [b_xn, sbf], accum=ss)
            rstd_from_ss(ss, ss, 1, D, sbf)
            act(xn[:], xres[:, m, :], AF.Copy, [b_x[m], sbf], [b_xn], scale=ss)
            pv = [psb(0), psb(1)]
            for c in range(16):
                tr(pv[c // 8][:, (c % 8) * 128:(c % 8 + 1) * 128], xn[:, c * 128:(c + 1) * 128], [b_xn], [b_ps[c // 8]])
            for hf in range(2):
                tt("dve", hT[hbuf][:, hf * 8:(hf + 1) * 8, m * 128:(m + 1) * 128],
                   pv[hf].rearrange("p (c t) -> p c t", t=128),
                   gcol[g][:, hf * 8:(hf + 1) * 8].unsqueeze(2).to_broadcast([128, 8, 128]),
                   ALU.mult, [b_ps[hf], b_const], [b_hT[hbuf]])

    def load_rope(c_src, s_src):
        dma("sp", ropeC[:], c_src.rearrange("(m p) d -> p m d", p=128), "ropeC", writes=[b_rope])
        dma("sp", ropeS[:], s_src.rearrange("(m p) d -> p m d", p=128), "ropeS", writes=[b_rope])

    def nr_compute(ps_i, nh, col0, m, g_bc, kb):
        sbf = b_stat[4 + kb]
        ssq = stat[:, 8 + 8 * kb: 8 + 8 * kb + nh]
        for h in range(nh):
            act(t1[:, h, :], psum[ps_i][:, col0 + h * 128: col0 + (h + 1) * 128], AF.Square,
                [b_ps[ps_i]], [b_t1, sbf], accum=ssq[:, h:h + 1])
        rstd_from_ss(ssq, ssq, nh, 128, sbf)
        for h in range(nh):
            stt("dve", kn[:, h, :], psum[ps_i][:, col0 + h * 128: col0 + (h + 1) * 128], ssq[:, h:h + 1], g_bc[:],
                ALU.mult, ALU.mult, [b_ps[ps_i], sbf, b_const], [b_kn])
        Cb = ropeC[:, m, :].unsqueeze(1).to_broadcast([128, nh, 128])
        tt("dve", t1[:, 0:nh, :], kn[:, 0:nh, :], Cb, ALU.mult, [b_kn, b_rope], [b_t1])
        knv = kn[:, 0:nh, :].rearrange("p h (a b c) -> p h a b c", a=2, b=2)
        t2v = t2[:, 0:nh, :].rearrange("p h (a b c) -> p h a b c", a=2, b=2)
        Sv = ropeS[:, m, :].rearrange("p (a b c) -> p a b c", a=2, b=2)
        for hf in range(2):
            tt("pool", t2v[:, :, :, hf, :], knv[:, :, :, 1 - hf, :], Sv[:, :, hf, :].unsqueeze(1).to_broadcast([128, nh, 2, 32]),
               ALU.mult, [b_kn, b_rope], [b_t2])
        tt("dve", krb[kb][:, 0:nh, :], t1[:, 0:nh, :], t2[:, 0:nh, :], ALU.add, [b_t1, b_t2], [b_krb[kb]])

    def nr_transpose(nh, m, kb, dst, dst_b):
        pb = psb(6 + kb)
        for h in range(nh):
            tr(pb[:, h * 128:(h + 1) * 128], krb[kb][:, h, :], [b_krb[kb]], [b_ps[6 + kb]])
        cp("dve", dst[:, 0:nh, m * 128:(m + 1) * 128], pb[:, 0:nh * 128].rearrange("p (h t) -> p h t", t=128),
           [b_ps[6 + kb]], [dst_b])

    cnt["kr"] = 0

    def norm_rope_T(ps_i, nh, col0, m, g_bc, dst, dst_b):
        kb = cnt["kr"] % 2
        cnt["kr"] += 1
        nr_compute(ps_i, nh, col0, m, g_bc, kb)
        return lambda: nr_transpose(nh, m, kb, dst, dst_b)

    o2 = 0
    tA, o2 = carve(o2, [128, 8192], F32)
    tB, o2 = carve(o2, [128, 8, 1024], BF16)
    tC, o2 = carve(o2, [128, 8, 1024], BF16)
    tD, o2 = carve(o2, [128, 8, 1024], BF16)
    tE, o2 = carve(o2, [128, 8, 1024], BF16)
    tM, o2 = carve(o2, [128, 2, 1024], F32)
    b_tA, b_tB, b_tC, b_tD, b_tE, b_tM = P.bufs(6, "tab")
    dma("sp", tA[:], rpbx_d, "tA", writes=[b_tA])
    dma("sp", tM[:, 0, :], mfull_d, "tM", writes=[b_tM])
    dma("sp", tM[:, 1, :], mint_d, "tM", writes=[b_tM])
    act(tA[:], tA[:], AF.Exp, [b_tA], [b_tA])
    tA3 = tA.rearrange("p (h x) -> p h x", h=8)
    tt("dve", tB[:], tA3, tM[:, 0, :].unsqueeze(1).to_broadcast([128, 8, 1024]), ALU.mult, [b_tA, b_tM], [b_tB])
    tt("dve", tC[:], tA3, tM[:, 1, :].unsqueeze(1).to_broadcast([128, 8, 1024]), ALU.mult, [b_tA, b_tM], [b_tC])
    tt("dve", tD[:], tB[:], tC[:], ALU.subtract, [b_tB, b_tC], [b_tD])
    dma("sp", tabs_d[0].rearrange("p (h x) -> p h x", h=8), tC[:], "tabst", reads=[b_tC])
    dma("sp", tabs_d[1].rearrange("p (h x) -> p h x", h=8), tB[:], "tabst", reads=[b_tB])
    for q in range(4):
        stt("dve", tE[:], tD[:], sel[:, q:q + 1], tC[:], ALU.mult, ALU.add, [b_tD, b_tC, b_const], [b_tE])
        dma("sp", tabs_d[2 + q * 2 + 1].rearrange("p (h x) -> p h x", h=8), tE[:], "tabsE", reads=[b_tE])
        ts("dve", tE[:], tC[:], nsel[:, q:q + 1], None, ALU.mult, None, [b_tC, b_stat[7]], [b_tE])
        dma("sp", tabs_d[2 + q * 2 + 0].rearrange("p (h x) -> p h x", h=8), tE[:], "tabsE", reads=[b_tE])

    P.barrier()
    for ji, jb in enumerate(J):
        for half in range(2):
            src = jb["mem"][half * 128:(half + 1) * 128, :]
            dma("sp", xres[:, half, :], src, f"x{half}", writes=[b_x[half]])
            sbf = b_stat[half]
            ss = stat[:, half:half + 1]
            xn, b_xn = xnb[half], b_xnb[half]
            act(xn[:], xres[:, half, :], AF.Square, [b_x[half]], [b_xn, sbf], accum=ss)
            rstd_from_ss(ss, ss, 1, D, sbf)
            ts("dve", xn[:], xres[:, half, :], ss, None, ALU.mult, None, [b_x[half], sbf], [b_xn])
            pv = [psb(0), psb(1)]
            for c in range(16):
                tr(pv[c // 8][:, (c % 8) * 128:(c % 8 + 1) * 128], xn[:, c * 128:(c + 1) * 128], [b_xn], [b_ps[c // 8]])
            for hf in range(2):
                tt("dve", hT[0][:, hf * 8:(hf + 1) * 8, half * 128:(half + 1) * 128],
                   pv[hf].rearrange("p (c t) -> p c t", t=128),
                   gcol["g_mem_c"][:, hf * 8:(hf + 1) * 8].unsqueeze(2).to_broadcast([128, 8, 128]),
                   ALU.mult, [b_ps[hf], b_const], [b_hT[0]])
        si = load_slab("w_ckv", 0, 16, 0)
        for h in range(4):
            pi = next_ps()
            for c in range(16):
                mm(psum[pi][:, 0:NMEM], slab[si][:, c, h * 128:(h + 1) * 128], hT[0][:, c, 0:NMEM], c == 0, c == 15,
                   [b_slab[si], b_hT[0]], [b_ps[pi]])
            cp("dve", KcT[ji][:, h, :], psum[pi][:, 0:NMEM], [b_ps[pi]], [b_kc[ji]])
        si = load_slab("w_ckv", 0, 16, 512)
        memset("dve", Vc1[ji][:, :, :, 128:129], 1.0, [b_kc[ji]])
        for half in range(2):
            pi = next_ps()
            for c in range(16):
                mm(psum[pi][:], hT[0][:, c, half * 128:(half + 1) * 128], slab[si][:, c, :], c == 0, c == 15,
                   [b_slab[si], b_hT[0]], [b_ps[pi]])
            cp("dve", Vc1[ji][:, half, :, 0:128], psum[pi][:].rearrange("p (h d) -> p h d", d=128), [b_ps[pi]], [b_kc[ji]])

    o2 = OFF_P13
    kvslab, o2 = carve(o2, [128, 16, 512], BF16)
    kT, o2 = carve(o2, [128, 2, 512], BF16)
    vt, o2 = carve(o2, [128, 4, 256], BF16)
    b_kvslab, b_kT, b_vt = P.bufs(3, "kv")
    dma("sp", kvslab[:], WB["w_kv"].rearrange("(c p) n -> p c n", p=128), "kvslab",
        reads=[WBUF["w_kv"]], writes=[b_kvslab])
    deferred_cast = [True]
    cnt["tA"] = 0

    def norm_transpose_m(x_rows, g, hbuf, m):
        dma("sp", xres[:, m, :], x_rows, f"x{m}", writes=[b_x[m]])
        sbf = b_stat[m % 4]
        ss = stat[:, m:m + 1]
        xi = cnt["xn"] % 2
        cnt["xn"] += 1
        xn, b_xn = xnb[xi], b_xnb[xi]
        act(xn[:], xres[:, m, :], AF.Square, [b_x[m]], [b_xn, sbf], accum=ss)
        rstd_from_ss(ss, ss, 1, D, sbf)
        act(xn[:], xres[:, m, :], AF.Copy, [b_x[m], sbf], [b_xn], scale=ss)
        bk = 2 * (cnt["tA"] % 2)
        cnt["tA"] += 1
        pv = [psb(bk), psb(bk + 1)]
        for c in range(16):
            tr(pv[c // 8][:, (c % 8) * 128:(c % 8 + 1) * 128], xn[:, c * 128:(c + 1) * 128], [b_xn], [b_ps[bk + c // 8]])

        def evac():
            for hf in range(2):
                tt("dve", hT[hbuf][:, hf * 8:(hf + 1) * 8, m * 128:(m + 1) * 128],
                   pv[hf].rearrange("p (c t) -> p c t", t=128),
                   gcol[g][:, hf * 8:(hf + 1) * 8].unsqueeze(2).to_broadcast([128, 8, 128]),
                   ALU.mult, [b_ps[bk + hf], b_const], [b_hT[hbuf]])
        return evac

    kTb, vtb = [kT], [vt]
    b_kTb, b_vtb = [b_kT], [b_vt]
    v_, o2 = carve(o2, [128, 2, 512], BF16)
    kTb.append(v_)
    v_, o2 = carve(o2, [128, 4, 256], BF16)
    vtb.append(v_)
    b_kTb.append(P.buf("kT1"))
    b_vtb.append(P.buf("vt1"))
    ropeCb, ropeSb, b_ropeb = [ropeC], [ropeS], [b_rope]
    v_, o2 = carve(o2, [128, 4, 128], F32)
    ropeCb.append(v_)
    v_, o2 = carve(o2, [128, 4, 128], F32)
    ropeSb.append(v_)
    b_ropeb.append(P.buf("rope1"))
    units = []
    for jb in J:
        for t in range(jb["S"] // 512):
            for m in range(4):
                units.append((jb, t, m))
    ust = {}

    def st_A(u):
        jb, t, m = units[u]
        if m == 0:
            ust[(id(jb), t)] = dict(hb=cnt["hT"] % 2, tb=(cnt["hT"] // 1) % 2)
            cnt["hT"] += 1
            tb = ust[(id(jb), t)]["tb"]
            dma("sp", ropeCb[tb][:], jb["rk_c"][t * 512:(t + 1) * 512, :].rearrange("(m p) d -> p m d", p=128), f"ropeC{tb}",
                writes=[b_ropeb[tb]])
            dma("sp", ropeSb[tb][:], jb["rk_s"][t * 512:(t + 1) * 512, :].rearrange("(m p) d -> p m d", p=128), f"ropeS{tb}",
                writes=[b_ropeb[tb]])
        us = ust[(id(jb), t)]
        return norm_transpose_m(jb["x_seq"][t * 512 + m * 128:t * 512 + (m + 1) * 128, :], "g_mix_c", us["hb"], m)

    def st_B1(u):
        jb, t, m = units[u]
        us = ust[(id(jb), t)]
        hb = us["hb"]
        pi = next_ps(4, 2)
        us[("pi", m)] = pi
        for c in range(16):
            mm(psum[pi][:], hT[hb][:, c, m * 128:(m + 1) * 128], kvslab[:, c, :], c == 0, c == 15,
               [b_kvslab, b_hT[hb]], [b_ps[pi]])

    def st_B2(u):
        nonlocal ropeC, ropeS, b_rope
        jb, t, m = units[u]
        us = ust[(id(jb), t)]
        tb = us["tb"]
        pi = us[("pi", m)]
        cp("act", vtb[tb][:, m, :], psum[pi][:, 256:512], [b_ps[pi]], [b_vtb[tb]])
        ropeC, ropeS, b_rope = ropeCb[tb], ropeSb[tb], b_ropeb[tb]
        us[("post", m)] = norm_rope_T(pi, 2, 0, m, gk_bc, kTb[tb], b_kTb[tb])
        ropeC, ropeS, b_rope = ropeCb[0], ropeSb[0], b_ropeb[0]

    def st_C(u):
        jb, t, m = units[u]
        us = ust[(id(jb), t)]
        tb = us["tb"]
        us[("post", m)]()
        if m == 3:
            dma("sp", jb["KbT"][:, t * 512:(t + 1) * 512].rearrange("(g p) s -> p g s", p=128), kTb[tb][:], f"kTst{tb}", reads=[b_kTb[tb]])
            dma("sp", jb["Vb"][t * 512:(t + 1) * 512, :].rearrange("(m p) c -> p m c", p=128), vtb[tb][:], f"vtst{tb}", reads=[b_vtb[tb]])

    NU = len(units)
    cast_list = []
    for k in ("w_in",):
        for r0 in range(0, wshape[k][0], 256):
            cast_list.append((k, r0))
    every = 2
    ci = 0
    for i in range(NU + 3):
        if 0 <= i - 2 < NU:
            st_B1(i - 2)
        ev_a = st_A(i) if i < NU else None
        if 0 <= i - 2 < NU:
            st_B2(i - 2)
        if 0 <= i - 3 < NU:
            st_C(i - 3)
        if ev_a is not None:
            ev_a()
        if i >= 4 and (i - 4) % every == 0 and ci < len(cast_list):
            k, r0 = cast_list[ci]
            ci += 1
            dma("pool", WB[k][r0:r0 + 256, :], W[k][r0:r0 + 256, :], "cast_" + k, writes=[WBUF[k]])
    while ci < len(cast_list):
        k, r0 = cast_list[ci]
        ci += 1
        dma("pool", WB[k][r0:r0 + 256, :], W[k][r0:r0 + 256, :], "cast_" + k, writes=[WBUF[k]])

    def fm_slab(si, hb, evac):
        for cc in range(4):
            pi = next_ps()
            for c in range(16):
                mm(psum[pi][:], slab[si][:, c, cc * 128:(cc + 1) * 128], hT[hb][:, c, :], c == 0, c == 15,
                   [b_slab[si], b_hT[hb]], [b_ps[pi]])
            evac(cc, pi)

    def tm_slab(si, hb, evac, nk=16, src=None, src_b=None):
        src = hT[hb] if src is None else src
        src_b = [b_hT[hb]] if src_b is None else (src_b if isinstance(src_b, list) else [src_b])
        pend = None
        for m in range(4):
            pi = next_ps()
            for c in range(nk):
                mm(psum[pi][:], src[:, c, m * 128:(m + 1) * 128], slab[si][:, c, :], c == 0, c == nk - 1,
                   [b_slab[si]] + src_b, [b_ps[pi]])
            if pend is not None:
                pend()
            pend = evac(m, pi)
        if pend is not None:
            pend()

    ev_rr = [0]

    def evac_copy(dst, src, rd, wr):
        e = ("act", "dve")[ev_rr[0] % 2]
        ev_rr[0] += 1
        cp(e, dst, src, rd, wr)

    def p1b_body(si, s, jb, kind, t, hb):
        T = jb["T"]
        gi = next_stg()

        def store_fm(dst_rows, col0, ncol=512):
            dma("sp", dst_rows[:, col0:col0 + ncol].rearrange("(c p) s -> p c s", p=128), stg[gi][:, :, 0:ncol],
                f"stg{gi}", reads=[b_stg[gi]])

        if s in (0, 1, 2, 3):
            def ev(cc, pi):
                evac_copy(stg[gi][:, cc, :], psum[pi][:], [b_ps[pi]], [b_stg[gi]])
            fm_slab(si, hb, ev)
            if s < 2:
                store_fm(jb["QaT"][s * 512:(s + 1) * 512, :], t * 512)
            elif kind == "own":
                store_fm(jb["KaT"][(s - 2) * 512:(s - 1) * 512, :], 256 + t * 512)
            else:
                dma("sp", jb["KaT"][(s - 2) * 512:(s - 1) * 512, 0:256].rearrange("(c p) s -> p c s", p=128),
                    stg[gi][:, :, 0:256], f"stg{gi}", reads=[b_stg[gi]])
                dma("sp", jb["KaT"][(s - 2) * 512:(s - 1) * 512, 256 + T:512 + T].rearrange("(c p) s -> p c s", p=128),
                    stg[gi][:, :, 256:512], f"stg{gi}", reads=[b_stg[gi]])
        elif s in (4, 5):
            def ev(m, pi):
                evac_copy(stg[gi][:, m, :], psum[pi][:], [b_ps[pi]], [b_stg[gi]])
            tm_slab(si, hb, ev)
            cs = (s - 4) * 512
            if kind == "own":
                dma("sp", jb["Va"][256 + t * 512:256 + (t + 1) * 512, cs:cs + 512].rearrange("(m p) c -> p m c", p=128),
                    stg[gi][:], f"stg{gi}", reads=[b_stg[gi]])
            else:
                dma("sp", jb["Va"][0:256, cs:cs + 512].rearrange("(m p) c -> p m c", p=128),
                    stg[gi][:, 0:2, :], f"stg{gi}", reads=[b_stg[gi]])
                dma("sp", jb["Va"][256 + T:512 + T, cs:cs + 512].rearrange("(m p) c -> p m c", p=128),
                    stg[gi][:, 2:4, :], f"stg{gi}", reads=[b_stg[gi]])
        elif s in (6, 7):
            def ev(m, pi):
                return norm_rope_T(pi, 4, 0, m, gq_bc, stg[gi], b_stg[gi])
            tm_slab(si, hb, ev)
            store_fm(jb["QbT"][(s - 6) * 512:(s - 5) * 512, :], t * 512)
        else:
            def ev(cc, pi):
                act(stg[gi][:, cc, :], psum[pi][:], AF.Sigmoid, [b_ps[pi]], [b_stg[gi]])
            fm_slab(si, hb, ev)
            store_fm(jb["sgT"][(s - 9) * 512:(s - 8) * 512, :], t * 512)

    per_tile = []
    for jb in J:
        T = jb["T"]
        for kind, t in [("own", t) for t in range(T // 512)] + [("halo", 0)]:
            ctx = {}

            def prep(_slot, jb=jb, kind=kind, t=t, ctx=ctx):
                ctx["hb"] = cnt["hT"] % 2
                cnt["hT"] += 1
                if kind == "own":
                    xs = jb["x_own"][t * 512:(t + 1) * 512, :]
                    load_rope(jb["rq_c"][t * 512:(t + 1) * 512, :], jb["rq_s"][t * 512:(t + 1) * 512, :])
                else:
                    xs = jb["x_halo"]
                norm_transpose(xs, "g_mix_c", ctx["hb"], False)

            slabs = [0, 1, 2, 3, 4, 5, 6, 7, 9, 10, 11, 12, 13, 14, 15, 16] if kind == "own" else [2, 3, 4, 5]
            sj = [((lambda s=s: load_slab("w_in", 0, 16, s * 512)),
                   (lambda si, s=s, jb=jb, kind=kind, t=t, ctx=ctx: p1b_body(si, s, jb, kind, t, ctx["hb"]))) for s in slabs]
            per_tile.append(((None, prep), sj))
    jobs = []
    carry = []
    for (pj, sj) in per_tile:
        jobs.append(pj)
        jobs.extend(carry)
        k = max(0, len(sj) - 4)
        jobs.extend(sj[:k])
        carry = sj[k:]
    jobs.extend(carry)
    late_casts = []
    for k in ("w_pa", "w_pb", "w_o", "w_cq", "w_co", "w_up", "w_down"):
        for r0 in range(0, wshape[k][0], 256):
            late_casts.append((k, r0))
    nslabjobs = sum(1 for j in jobs if j[0] is not None)
    stride = max(1, (nslabjobs - 8) // len(late_casts))
    jobs2 = []
    seen = 0
    for j in jobs:
        jobs2.append(j)
        if j[0] is not None:
            seen += 1
            if seen % stride == 0 and late_casts:
                k, r0 = late_casts.pop(0)
                jobs2.append((None, (lambda _s, k=k, r0=r0: dma("pool", WB[k][r0:r0 + 256, :], W[k][r0:r0 + 256, :],
                                                                "cast_" + k, writes=[WBUF[k]]))))
    for (k, r0) in late_casts:
        jobs2.append((None, (lambda _s, k=k, r0=r0: dma("pool", WB[k][r0:r0 + 256, :], W[k][r0:r0 + 256, :],
                                                        "cast_" + k, writes=[WBUF[k]]))))
    run_pipe(jobs2)
    P.barrier()

    o2 = 0
    pT = []
    for i in range(8):
        v, o2 = carve(o2, [128, 2, 512], BF16)
        pT.append(v)
    tsum = []
    for i in range(6):
        v, o2 = carve(o2, [128, 1024], BF16)
        tsum.append(v)
    b_tsum = P.bufs(6, "tsum")
    cnt["ts"] = 0
    dacc = []
    for i in range(4):
        v, o2 = carve(o2, [128, 1024], F32)
        dacc.append(v)
    dtot, o2 = carve(o2, [128, 512], F32)
    drec, o2 = carve(o2, [128, 512], F32)
    ones16, o2 = carve(o2, [128, 128], BF16)
    dhi, o2 = carve(o2, [128, 512], BF16)
    dlo, o2 = carve(o2, [128, 512], BF16)
    b_dhl = P.buf("dhl")
    oT, o2 = carve(o2, [128, 2, 512], BF16)
    b_pT = P.bufs(8, "pT")
    b_dacc = P.bufs(4, "dacc")
    b_dtot, b_drec, b_ones = P.bufs(3, "den")
    b_oT = P.bufs(2, "oT")
    OFF_ATT = (o2 + 31) // 32 * 32
    cnt["pT"] = 0
    cnt["oT"] = 0
    cnt["stb"] = 0
    cnt["hd"] = 0
    STB = [[0, 1], [2, 3], [4, 5]]
    OACC = [6, 7]
    DENB = [0]
    memset("dve", ones16[:], 1.0, [b_ones])

    def den_matmul(src_ap, src_b):
        cp("dve", dhi[:], src_ap, src_b, [b_dhl])
        tt("dve", dlo[:], src_ap, dhi[:], ALU.subtract, list(src_b) + [b_dhl], [b_dhl])
        db = DENB[0]
        mm(psum[db][:], ones16[:], dhi[:], True, False, [b_ones, b_dhl], [b_ps[db]])
        mm(psum[db][:], ones16[:], dlo[:], False, True, [b_ones, b_dhl], [b_ps[db]])

    def att_finish(par, dstT, row0, col0, used_pool, used_dve=True):
        src = par
        if used_pool and used_dve:
            tt("dve", dacc[par][:], dacc[par][:], dacc[2 + par][:], ALU.add, [b_dacc[par], b_dacc[2 + par]], [b_dacc[par]])
        elif used_pool:
            src = 2 + par
        tt("dve", dtot[:], dacc[src][:, 0:512], dacc[src][:, 512:1024], ALU.add, [b_dacc[src]], [b_dtot])
        den_matmul(dtot[:], [b_dtot])
        P.add("dve", lambda e, db=DENB[0]: e.reciprocal(drec[:], psum[db][:]), [b_ps[DENB[0]]], [b_drec])
        oi = cnt["oT"] % 2
        cnt["oT"] += 1
        tt("dve", oT[:, oi, :], psum[OACC[par]][:], drec[:], ALU.mult, [b_ps[OACC[par]], b_drec], [b_oT[oi]])
        dma("sp!", dstT[row0:row0 + 128, col0:col0 + 512], oT[:, oi, :], f"oT{oi}", reads=[b_oT[oi]])

    for jb in J:
        S, T = jb["S"], jb["T"]
        NCH = S // 128
        NP = NCH // 2
        o3 = OFF_ATT
        KTg, o3 = carve(o3, [128, S], BF16)
        Vg, o3 = carve(o3, [128, NCH, 128], BF16)
        qb, o3 = carve(o3, [128, 2, 512], BF16)
        b_KTg, b_Vg = P.bufs(2, "kvg")
        b_qb = P.bufs(2, "qb")
        for g in range(2):
            dma("sp", KTg[:], jb["KbT"][g * 128:(g + 1) * 128, :], "KTg", writes=[b_KTg])
            dma("sp", Vg[:], jb["Vb"][:, g * 128:(g + 1) * 128].rearrange("(c p) d -> p c d", p=128), "V1g",
                writes=[b_Vg])
            heads = [(qt, hh) for qt in range(T // 512) for hh in range(4)]
            steps = [(hi, p) for hi in range(len(heads)) for p in range(NP)]
            hstate = {}

            def stage_a(st_):
                hi, p = st_
                if p == 0:
                    par = cnt["hd"] % 2
                    cnt["hd"] += 1
                    qi = par
                    qt, hh = heads[hi]
                    h = g * 4 + hh
                    dma("sp", qb[:, qi, :], jb["QbT"][h * 128:(h + 1) * 128, qt * 512:(qt + 1) * 512], f"qb{qi}", writes=[b_qb[qi]])
                    hstate[hi] = dict(par=par, qi=qi, h=h, qt=qt, npool=0, ndve=0)
                hs = hstate[hi]
                sb_ = STB[cnt["stb"] % 3]
                cnt["stb"] += 1
                pi = cnt["pT"] % 8
                cnt["pT"] += 1
                for k in range(2):
                    ch = p * 2 + k
                    mm(psum[sb_[k]][:], KTg[:, ch * 128:(ch + 1) * 128], qb[:, hs["qi"], :], True, True,
                       [b_KTg, b_qb[hs["qi"]]], [b_ps[sb_[k]]])
                act(pT[pi][:].rearrange("p k s -> p (k s)"), psbig[sb_[0] // 2][:], AF.Exp,
                    [b_ps[sb_[0]], b_ps[sb_[1]]], [b_pT[pi]], scale=SCALE)
                return pi

            def stage_b(st_, pi):
                hi, p = st_
                hs = hstate[hi]
                par = hs["par"]
                for k in range(2):
                    ch = p * 2 + k
                    mm(psum[OACC[par]][:], Vg[:, ch, :], pT[pi][:, k, :], ch == 0, ch == NCH - 1,
                       [b_pT[pi], b_Vg], [b_ps[OACC[par]]])
                pflat = pT[pi][:].rearrange("p k s -> p (k s)")
                if p % 2 == 0:
                    hs["prev_pi"] = pi
                else:
                    ppi = hs["prev_pi"]
                    ti = cnt["ts"] % 6
                    cnt["ts"] += 1
                    tt("dve", tsum[ti][:], pT[ppi][:].rearrange("p k s -> p (k s)"), pflat, ALU.add,
                       [b_pT[ppi], b_pT[pi]], [b_tsum[ti]])
                    if p % 4 == 1:
                        hs["prev_ti"] = ti
                    else:
                        pti = hs["prev_ti"]
                        t2i = cnt["ts"] % 6
                        cnt["ts"] += 1
                        tt("dve", tsum[t2i][:], tsum[pti][:], tsum[ti][:], ALU.add, [b_tsum[pti], b_tsum[ti]], [b_tsum[t2i]])
                        if (p // 4) % 4 == 3:
                            e_, ai, key = "dve", par, "ndve"
                        else:
                            e_, ai, key = "pool", 2 + par, "npool"
                        if hs[key] == 0:
                            cp(e_, dacc[ai][:], tsum[t2i][:], [b_tsum[t2i]], [b_dacc[ai]])
                        else:
                            tt(e_, dacc[ai][:], dacc[ai][:], tsum[t2i][:], ALU.add, [b_tsum[t2i], b_dacc[ai]], [b_dacc[ai]])
                        hs[key] += 1
                if p == NP - 1:
                    att_finish(par, jb["ObT"], hs["h"] * 128, hs["qt"] * 512, hs["npool"] > 0, hs["ndve"] > 0)

            pend = []
            for st_ in steps:
                pi = stage_a(st_)
                pend.append((st_, pi))
                if len(pend) > 2:
                    stage_b(*pend.pop(0))
            while pend:
                stage_b(*pend.pop(0))
        P.barrier()

    DENB[0] = 4
    o3 = OFF_ATT
    tabI, o3 = carve(o3, [128, 8, 16, 64], BF16)
    tabX, o3 = carve(o3, [128, 4, 8 * 16 * 64], BF16)
    qa, o3 = carve(o3, [128, 8, 512], BF16)
    ka, o3 = carve(o3, [128, 8, 1024], BF16)
    Va_, o3 = carve(o3, [128, 8, 1024], BF16)
    b_tabI, b_tabX, b_qa, b_ka, b_Va = P.bufs(5, "na")
    dma("sp", tabI[:].rearrange("p h s q -> p (h s q)"), tabs_d[0], "tabI", writes=[b_tabI])
    tabXv = tabX.rearrange("p v (h s q) -> p v h s q", h=8, s=16)
    for ji, jb in enumerate(J):
        T = jb["T"]
        nblk = T // 512
        for v in range(4):
            cls, var = v // 2, v % 2
            dma("sp", tabX[:, v, :], tabs_d[2 + (ji * 2 + cls) * 2 + var], "tabX", writes=[b_tabX])
        for b in range(nblk):
            segs_all = na_segments(b == 0, b == nblk - 1)
            dma("sp", qa[:], jb["QaT"][:, b * 512:(b + 1) * 512].rearrange("(h p) s -> p h s", p=128), "qa", writes=[b_qa])
            dma("sp", ka[:], jb["KaT"][:, b * 512:b * 512 + 1024].rearrange("(h p) s -> p h s", p=128), "ka", writes=[b_ka])
            dma("sp", Va_[:], jb["Va"][b * 512:b * 512 + 1024, :].rearrange("(c p) x -> p c x", p=128), "V1a", writes=[b_Va])
            steps = [(h, j) for h in range(8) for j in range(8)]
            hstate = {}

            def na_a(st_):
                h, j = st_
                if j == 0:
                    par = cnt["hd"] % 2
                    cnt["hd"] += 1
                    hstate[h] = dict(par=par)
                m_lo, m_hi, segs = segs_all[j]
                c0, c1 = m_lo * 128, (m_hi + 1) * 128
                sbk = STB[cnt["stb"] % 2][cnt["stb"] // 2 % 2]
                cnt["stb"] += 1
                pi = cnt["pT"] % 8
                cnt["pT"] += 1
                mm(psum[sbk][:, c0:c1], ka[:, h, j * 128:(j + 1) * 128], qa[:, h, c0:c1], True, True,
                   [b_ka, b_qa], [b_ps[sbk]])
                act(pT[pi][:, 0, c0:c1], psum[sbk][:, c0:c1], AF.Exp, [b_ps[sbk]], [b_pT[pi]], scale=SCALE)
                eng_ = "pool" if (j % 2) else "dve"
                for (ra, rb, tb, s0) in segs:
                    n = rb - ra
                    if tb == "int":
                        tv = tabI[:, h, s0:s0 + n, :]
                        tbuf = b_tabI
                    else:
                        vi = {"first0": 0, "first1": 1, "last0": 2, "last1": 3}[tb]
                        tv = tabXv[:, vi, h, s0:s0 + n, :]
                        tbuf = b_tabX
                    pv_ = pT[pi][:, 0, ra * 64:rb * 64].rearrange("p (r q) -> p r q", q=64)
                    tt(eng_, pv_, pv_, tv, ALU.mult, [b_pT[pi], tbuf], [b_pT[pi]])
                return pi

            def na_b(st_, pi):
                h, j = st_
                par = hstate[h]["par"]
                m_lo, m_hi, segs = segs_all[j]
                c0, c1 = m_lo * 128, (m_hi + 1) * 128
                mm(psum[OACC[par]][:, c0:c1], Va_[:, j, h * 128:(h + 1) * 128], pT[pi][:, 0, c0:c1], j == 0, j == 7,
                   [b_pT[pi], b_Va], [b_ps[OACC[par]]], skip=True)
                dn = 4 + par
                mm(psum[dn][:, c0:c1], ones16[:], pT[pi][:, 0, c0:c1], j == 0, j == 7,
                   [b_pT[pi], b_ones], [b_ps[dn]], skip=True)
                if j == 7:
                    P.add("dve", lambda e, dn=dn: e.reciprocal(drec[:], psum[dn][:]), [b_ps[dn]], [b_drec])
                    oi = cnt["oT"] % 2
                    cnt["oT"] += 1
                    tt("dve", oT[:, oi, :], psum[OACC[par]][:], drec[:], ALU.mult, [b_ps[OACC[par]], b_drec], [b_oT[oi]])
                    dma("sp!", jb["OaT"][h * 128:(h + 1) * 128, b * 512:(b + 1) * 512], oT[:, oi, :], f"oT{oi}", reads=[b_oT[oi]])

            pend = []
            for st_ in steps:
                pi = na_a(st_)
                pend.append((st_, pi))
                if len(pend) > 6:
                    na_b(*pend.pop(0))
            while pend:
                na_b(*pend.pop(0))
    P.barrier()

    o2 = OFF_COMMON
    aT = []
    o_a = o2
    for i in range(2):
        v, o2 = carve(o2, [128, 16, 512], BF16)
        aT.append(v)
    OabT, o_a = carve(o_a, [128, 16, 512], BF16)
    sgs = []
    for i in range(2):
        v, o_a = carve(o_a, [128, 8, 512], BF16)
        sgs.append(v)
    tmpA, o2 = carve(o2, [128, 512], F32)
    tmpB, o2 = carve(o2, [128, 512], F32)
    qcT, o2 = carve(o2, [128, 4, 512], BF16)
    ocs, o2 = carve(o2, [128, 4, 512], BF16)
    ocT, o2 = carve(o2, [128, 4, 512], BF16)
    pTc, o2 = carve(o2, [128, 2, 512], BF16)
    gfin, o2 = carve(o2, [128, D], F32)
    b_Oab = P.buf("Oab")
    b_sgs = P.bufs(2, "sgs")
    b_aT = [[b_Oab], [b_sgs[0], b_sgs[1]]]
    b_tmpA, b_tmpB, b_qcT, b_ocs, b_ocT, b_gfin = P.bufs(6, "p3")
    b_pTc = P.bufs(2, "pTc")
    dma("sp", gfin[:], gfin_d, "gfin", writes=[b_gfin])

    def resid_add(m, pi, n):
        tt("dve", xres[:, m, n * 512:(n + 1) * 512], xres[:, m, n * 512:(n + 1) * 512], psum[pi][:], ALU.add,
           [b_ps[pi], b_x[m]], [b_x[m]])

    def p3_tile_jobs(ji, jb, t):
        tsl = slice(t * 512, (t + 1) * 512)
        ctx = {}
        jobs = []

        def xload(_):
            for m in range(4):
                dma("sp", xres[:, m, :], jb["x_own"][t * 512 + m * 128:t * 512 + (m + 1) * 128, :], f"x{m}", writes=[b_x[m]])

        def start(_):
            dma("sp", OabT[:, 0:8, :], jb["OaT"][:, tsl].rearrange("(c p) s -> p c s", p=128), "Oab", writes=[b_Oab])
            dma("sp", OabT[:, 8:16, :], jb["ObT"][:, tsl].rearrange("(c p) s -> p c s", p=128), "Oab", writes=[b_Oab])
            ctx["mixb"] = cnt["hT"] % 2
            cnt["hT"] += 1
        jobs.append((None, start))

        def ld_merge(n):
            si = next_slab()
            dma("sp", slab[si][:, 0:8, :], WB["w_pa"][:, n * 512:(n + 1) * 512].rearrange("(c p) n -> p c n", p=128),
                f"slab{si}", reads=[WBUF["w_pa"]], writes=[b_slab[si]])
            dma("sp", slab[si][:, 8:16, :], WB["w_pb"][:, n * 512:(n + 1) * 512].rearrange("(c p) n -> p c n", p=128),
                f"slab{si}", reads=[WBUF["w_pb"]], writes=[b_slab[si]])
            return si

        def merge(si, n):
            mixb = ctx["mixb"]
            gi = n % 2
            dma("sp", sgs[gi][:, 0:4, :], jb["sgT"][n * 512:(n + 1) * 512, tsl].rearrange("(c p) s -> p c s", p=128),
                f"sgs{gi}", writes=[b_sgs[gi]])
            dma("sp", sgs[gi][:, 4:8, :], jb["sgT"][2048 + n * 512:2048 + (n + 1) * 512, tsl].rearrange("(c p) s -> p c s", p=128),
                f"sgs{gi}", writes=[b_sgs[gi]])
            for cc in range(4):
                pa = next_ps()
                for c in range(8):
                    mm(psum[pa][:], slab[si][:, c, cc * 128:(cc + 1) * 128], OabT[:, c, :], c == 0, c == 7,
                       [b_slab[si], b_Oab], [b_ps[pa]])
                pbk = next_ps()
                for c in range(8):
                    mm(psum[pbk][:], slab[si][:, 8 + c, cc * 128:(cc + 1) * 128], OabT[:, 8 + c, :], c == 0, c == 7,
                       [b_slab[si], b_Oab], [b_ps[pbk]])
                tt("dve", tmpA[:], psum[pa][:], sgs[gi][:, cc, :], ALU.mult, [b_ps[pa], b_sgs[gi]], [b_tmpA])
                tt("dve", tmpB[:], psum[pbk][:], sgs[gi][:, 4 + cc, :], ALU.mult, [b_ps[pbk], b_sgs[gi]], [b_tmpB])
                tt("pool", hT[mixb][:, n * 4 + cc, :], tmpA[:], tmpB[:], ALU.add, [b_tmpA, b_tmpB], [b_hT[mixb]])
        for n in range(4):
            jobs.append(((lambda n=n: ld_merge(n)), (lambda si, n=n: merge(si, n))))
        jobs.insert(3, (None, xload))
        for n in range(4):
            jobs.append(((lambda n=n: load_slab("w_o", 0, 16, n * 512)),
                         (lambda si, n=n: tm_slab(si, ctx["mixb"], lambda m, pi: resid_add(m, pi, n)))))

        def cq(si):
            hb = cnt["hT"] % 2
            cnt["hT"] += 1
            norm_transpose(None, "g_cross_c", hb, True)

            def ev_q(cc, pi):
                cp("act", qcT[:, cc, :], psum[pi][:], [b_ps[pi]], [b_qcT])
            fm_slab(si, hb, ev_q)
            for h in range(4):
                for k in range(2):
                    pi = next_ps()
                    mm(psum[pi][:], KcT[ji][:, h, k * 128:(k + 1) * 128], qcT[:, h, :], True, True, [b_kc[ji], b_qcT], [b_ps[pi]])
                    act(pTc[:, k, :], psum[pi][:], AF.Exp, [b_ps[pi]], [b_pTc[0]], scale=SCALE)
                for m in range(4):
                    pi = 6 + m % 2
                    for k in range(2):
                        mm(psum[pi][:, 0:129], pTc[:, k, m * 128:(m + 1) * 128], Vc1[ji][:, k, h, 0:129], k == 0, k == 1,
                           [b_pTc[0], b_kc[ji]], [b_ps[pi]])
                    P.add("dve", lambda e, m=m, pi=pi: e.reciprocal(rden[:, m:m + 1], psum[pi][:, 128:129]), [b_ps[pi]], [b_rden])
                    ts("dve", ocs[:, m, h * 128:(h + 1) * 128], psum[pi][:, 0:128], rden[:, m:m + 1], None, ALU.mult, None,
                       [b_ps[pi], b_rden], [b_ocs])
            for m in range(4):
                pb = psb(6 + m % 2)
                for h in range(4):
                    tr(pb[:, h * 128:(h + 1) * 128], ocs[:, m, h * 128:(h + 1) * 128], [b_ocs], [b_ps[6 + m % 2]])
                cp("dve", ocT[:, :, m * 128:(m + 1) * 128], pb[:, 0:512].rearrange("p (h t) -> p h t", t=128),
                   [b_ps[6 + m % 2]], [b_ocT])
        jobs.append(((lambda: load_slab("w_cq", 0, 16, 0)), cq))
        for n in range(4):
            jobs.append(((lambda n=n: load_slab("w_co", 0, 4, n * 512)),
                         (lambda si, n=n: tm_slab(si, None, lambda m, pi: resid_add(m, pi, n), nk=4, src=ocT, src_b=b_ocT))))

        def mlp_norm(_):
            ctx["hb"] = cnt["hT"] % 2
            cnt["hT"] += 1
            norm_transpose(None, "g_mlp_c", ctx["hb"], True)
        jobs.append((None, mlp_norm))

        def up(si, n, ab):
            def ev_up(cc, pi):
                act(tmpA[:], psum[pi][:], AF.Relu, [b_ps[pi]], [b_tmpA])
                tt("dve", aT[ab][:, n * 4 + cc, :], tmpA[:], tmpA[:], ALU.mult, [b_tmpA], b_aT[ab])
            fm_slab(si, ctx["hb"], ev_up)
        for kg in range(4):
            ab = kg % 2
            for n in range(4):
                jobs.append(((lambda kg=kg, n=n: load_slab("w_up", 0, 16, (kg * 4 + n) * 512)),
                             (lambda si, n=n, ab=ab: up(si, n, ab))))
            for n in range(4):
                jobs.append(((lambda kg=kg, n=n: load_slab("w_down", kg * 16, 16, n * 512)),
                             (lambda si, n=n, ab=ab: tm_slab(si, None, lambda m, pi: resid_add(m, pi, n), src=aT[ab], src_b=b_aT[ab]))))

        def fin(_):
            for m in range(4):
                sbf = b_stat[m % 4]
                ss = stat[:, m:m + 1]
                xn, b_xn = xnb[m % 2], b_xnb[m % 2]
                act(xn[:], xres[:, m, :], AF.Square, [b_x[m]], [b_xn, sbf], accum=ss)
                rstd_from_ss(ss, ss, 1, D, sbf)
                stt("dve", xres[:, m, :], xres[:, m, :], ss, gfin[:], ALU.mult, ALU.mult, [b_x[m], sbf, b_gfin], [b_x[m]])
                dma("sp", jb["y"][t * 512 + m * 128:t * 512 + (m + 1) * 128, :], xres[:, m, :], f"xo{m}", reads=[b_x[m]])
        jobs.append((None, fin))
        return jobs

    jobs = []
    for ji, jb in enumerate(J):
        for t in range(jb["T"] // 512):
            jobs.extend(p3_tile_jobs(ji, jb, t))
    run_pipe(jobs)

    P.emit()
    st.close()
    return nc, P


def _rope_tables(S):
    t = np.arange(S)
    row = (t // 64).astype(np.float32)
    col = (t % 64).astype(np.float32)
    inv = (10000.0 ** (-np.arange(32, dtype=np.float32) / 32)).astype(np.float32)
    ar = row[:, None] * inv[None, :]
    ac = col[:, None] * inv[None, :]
    C = np.concatenate([np.cos(ar), np.cos(ar), np.cos(ac), np.cos(ac)], axis=1).astype(np.float32)
    Sn = np.concatenate([-np.sin(ar), np.sin(ar), -np.sin(ac), np.sin(ac)], axis=1).astype(np.float32)
    return C, Sn


def _na_consts(rpb):
    rp = np.asarray(rpb, np.float32)[0]
    e = np.arange(128) // 64
    kc = np.arange(128) % 64
    s = np.arange(16)
    qc = np.arange(64)
    ap = s[None, :] - e[:, None]
    a = 14 - ap
    slot_ok = (ap >= 0) & (ap <= 14)
    dc = kc[:, None] - qc[None, :]
    cs = np.clip(qc - 8, 0, 48)
    col_ok = (kc[:, None] >= cs[None, :]) & (kc[:, None] < cs[None, :] + 16)
    bi = np.clip(dc + 15, 0, 30)
    A = np.clip(a, 0, 14)
    g = rp[:, A[:, :, None], bi[:, None, :]]
    ok = slot_ok[:, :, None] & col_ok[:, None, :]
    g = np.where(ok[None], g, 0.0).astype(np.float32)
    rpb_exp = np.ascontiguousarray(g.transpose(1, 0, 2, 3)).reshape(128, 8 * 16 * 64)
    mfull = ok.astype(np.float32).reshape(128, 1024)
    mint = (ok & ((ap >= 4) & (ap <= 11))[:, :, None]).astype(np.float32).reshape(128, 1024)
    return rpb_exp, mfull, mint


def _gcol(g):
    return np.ascontiguousarray(np.asarray(g, np.float32).reshape(16, 128).T)


_CACHE = {}


def _run(inputs, S_p, S_s, debug=False):
    cfg = Cfg(S_p, S_s)
    key = (S_p, S_s, debug)
    if key not in _CACHE:
        _CACHE[key] = build(cfg, debug)
    nc, P = _CACHE[key]
    f = lambda k: np.asarray(inputs[k], np.float32)
    xp, xs, mp, ms = f("x_prompt"), f("x_sample"), f("mem_prompt"), f("mem_sample")
    shared = {k: np.ascontiguousarray(f(k)[0]) for k in ("w_in", "w_pa", "w_pb", "w_o", "w_cq", "w_ckv", "w_co", "w_up", "w_down")}
    shared["g_mix_c"] = _gcol(f("g_mix")[0])
    shared["g_cross_c"] = _gcol(f("g_cross")[0])
    shared["g_mem_c"] = _gcol(f("g_mem")[0])
    shared["g_mlp_c"] = _gcol(f("g_mlp")[0])
    shared["gq_bc"] = np.ascontiguousarray(np.broadcast_to(f("g_q")[0][None, :], (128, 128)))
    shared["gk_bc"] = np.ascontiguousarray(np.broadcast_to(f("g_k")[0][None, :], (128, 128)))
    shared["gfin_bc"] = np.ascontiguousarray(np.broadcast_to(f("g_final")[None, :], (128, D)))
    shared["ident"] = np.eye(128, dtype=np.float32)
    shared["rpb_exp"], shared["mask_full"], shared["mask_int"] = _na_consts(f("rpb"))
    Cp, Sp = _rope_tables(S_p)
    Cs, Ss = _rope_tables(S_s)
    Tp, Ts = S_p // 2, S_s // 8
    zeros256 = np.zeros((256, D), np.float32)
    in_maps = []
    for c in range(8):
        b, hf = c // 2, c % 2
        m = dict(shared)
        o0 = hf * Tp
        m["xp_seq"] = np.ascontiguousarray(xp[b])
        m["xp_own"] = np.ascontiguousarray(xp[b, o0:o0 + Tp])
        pre = xp[b, o0 - 256:o0] if o0 > 0 else zeros256
        post = xp[b, o0 + Tp:o0 + Tp + 256] if o0 + Tp < S_p else zeros256
        m["xp_halo"] = np.ascontiguousarray(np.concatenate([pre, post], 0))
        m["memp"] = np.ascontiguousarray(mp[b])
        m["rkp_c"], m["rkp_s"] = Cp, Sp
        m["rqp_c"], m["rqp_s"] = np.ascontiguousarray(Cp[o0:o0 + Tp]), np.ascontiguousarray(Sp[o0:o0 + Tp])
        s0 = c * Ts
        m["xs_seq"] = np.ascontiguousarray(xs[0])
        m["xs_own"] = np.ascontiguousarray(xs[0, s0:s0 + Ts])
        pre = xs[0, s0 - 256:s0] if s0 > 0 else zeros256
        post = xs[0, s0 + Ts:s0 + Ts + 256] if s0 + Ts < S_s else zeros256
        m["xs_halo"] = np.ascontiguousarray(np.concatenate([pre, post], 0))
        m["mems"] = np.ascontiguousarray(ms[0])
        m["rks_c"], m["rks_s"] = Cs, Ss
        m["rqs_c"], m["rqs_s"] = np.ascontiguousarray(Cs[s0:s0 + Ts]), np.ascontiguousarray(Ss[s0:s0 + Ts])
        selv = np.array([hf == 0, hf == 1, c == 0, c == 7], np.float32)
        m["sel"] = np.ascontiguousarray(np.broadcast_to(selv[None, :], (128, 4)))
        in_maps.append(m)
    res = run_bass_kernel_spmd(nc, in_maps, core_ids=list(range(8)))
    yp = np.zeros((4, S_p, D), np.float32)
    ys = np.zeros((1, S_s, D), np.float32)
    for c in range(8):
        b, hf = c // 2, c % 2
        yp[b, hf * Tp:(hf + 1) * Tp] = res.results[c]["yp"]
        ys[0, c * Ts:(c + 1) * Ts] = res.results[c]["ys"]
    return (yp, ys), res


def kernel(**inputs):
    (yp, ys), _ = _run(inputs, 8192, 16384)
    return (yp, ys)
```
